# Optimizing a Trainium2 kernel written in Bass

```python
import math
import jax, jax.numpy as jnp
from jax import lax
import numpy as np

D_MODEL = 1024
BATCH = 32
SEQ = 2048
DEPTH = 4

CTX_LEN = 256
GRID_W = 64
HEAD_DIM = 64
N_Q_HEADS = 8
N_KV_HEADS = 2
Q_PER_KV = N_Q_HEADS // N_KV_HEADS
ATTN_WIDTH = N_Q_HEADS * HEAD_DIM
KV_WIDTH = N_KV_HEADS * HEAD_DIM
WINDOW = 128
Q_BLOCK = 128
ROPE_BASE = 10000.0
ROPE_PAIRS_PER_AXIS = HEAD_DIM // 4
CONV_WIDTH = 256
CONV_K = 3
SSM_WIDTH = 256
SSM_GROUP = 16
SSM_GROUPS = SSM_WIDTH // SSM_GROUP
SSM_STATE = 64
MIX_WIDTH = ATTN_WIDTH + CONV_WIDTH + SSM_WIDTH
IN_WIDTH = ATTN_WIDTH + 2 * KV_WIDTH + 3 * CONV_WIDTH + SSM_WIDTH
D_FF = 4 * D_MODEL
N_MOD = 6
EPS = 1e-6
NEG_INF = -1e30

kernel_name = 'hybrid_dit_parallel_groups'


def rms_norm(t, g):
    tf = t.astype(jnp.float32)
    y = tf * lax.rsqrt(jnp.mean(tf * tf, axis=-1, keepdims=True) + EPS)
    return y.astype(t.dtype) * g


def modulate(t, shift, scale):
    return t * (1 + scale) + shift


def split_in_proj(p):
    sizes = [ATTN_WIDTH, KV_WIDTH, KV_WIDTH, CONV_WIDTH, CONV_WIDTH, CONV_WIDTH]
    return jnp.split(p, list(np.cumsum(sizes)), axis=-1)


def axial_rope_tables(n_tokens):
    rows = n_tokens // GRID_W
    row = jnp.broadcast_to(jnp.arange(rows)[:, None], (rows, GRID_W)).reshape(-1)
    col = jnp.broadcast_to(jnp.arange(GRID_W)[None, :], (rows, GRID_W)).reshape(-1)
    freqs = ROPE_BASE ** (-jnp.arange(ROPE_PAIRS_PER_AXIS, dtype=jnp.float32) / ROPE_PAIRS_PER_AXIS)
    ang = jnp.concatenate([row[:, None].astype(jnp.float32) * freqs,
                           col[:, None].astype(jnp.float32) * freqs], axis=-1)
    return jnp.cos(ang), jnp.sin(ang)


def _rotate(t, c, s):
    t1, t2 = jnp.split(t, 2, axis=-1)
    return jnp.concatenate([t1 * c - t2 * s, t1 * s + t2 * c], axis=-1)


def apply_axial_rope(t, cos, sin):
    c = cos[:, None, :].astype(t.dtype)
    s = sin[:, None, :].astype(t.dtype)
    n = ROPE_PAIRS_PER_AXIS
    half = HEAD_DIM // 2
    return jnp.concatenate([_rotate(t[..., :half], c[..., :n], s[..., :n]),
                            _rotate(t[..., half:], c[..., n:], s[..., n:])], axis=-1)


def windowed_latent_attention(q, k, v, kc, vc, sink):
    bsz, n_lat = q.shape[0], q.shape[1]
    n_ctx = kc.shape[1]
    n_blocks = n_lat // Q_BLOCK
    span = Q_BLOCK + 2 * WINDOW
    scale = HEAD_DIM ** -0.5
    pad = ((0, 0), (WINDOW, WINDOW), (0, 0), (0, 0))
    kp = jnp.pad(k, pad)
    vp = jnp.pad(v, pad)
    s_ctx_all = None

    def block(i):
        start = i * Q_BLOCK
        qb = lax.dynamic_slice_in_dim(q, start, Q_BLOCK, axis=1)
        kb = lax.dynamic_slice_in_dim(kp, start, span, axis=1)
        vb = lax.dynamic_slice_in_dim(vp, start, span, axis=1)
        qpos = start + jnp.arange(Q_BLOCK)
        kpos = start - WINDOW + jnp.arange(span)
        mask = (jnp.abs(qpos[:, None] - kpos[None, :]) <= WINDOW) & (kpos >= 0) & (kpos < n_lat)
        s_lat = jnp.einsum('bqhgd,bkhd->bhgqk', qb, kb).astype(jnp.float32) * scale
        s_lat = jnp.where(mask, s_lat, NEG_INF)
        s_ctx = jnp.einsum('bqhgd,bkhd->bhgqk', qb, kc).astype(jnp.float32) * scale
        s_sink = jnp.broadcast_to(sink.astype(jnp.float32)[None, :, :, None, None],
                                  s_ctx.shape[:-1] + (1,))
        p = jax.nn.softmax(jnp.concatenate([s_lat, s_ctx, s_sink], axis=-1), axis=-1)
        p_lat = p[..., :span].astype(v.dtype)
        p_ctx = p[..., span:span + n_ctx].astype(v.dtype)
        return (jnp.einsum('bhgqk,bkhd->bqhgd', p_lat, vb)
                + jnp.einsum('bhgqk,bkhd->bqhgd', p_ctx, vc))

    o = lax.map(block, jnp.arange(n_blocks))
    return jnp.moveaxis(o, 0, 1).reshape(bsz, n_lat, ATTN_WIDTH)


def context_attention(qc, kc, vc, sink):
    bsz, n_ctx = qc.shape[0], qc.shape[1]
    s = jnp.einsum('bqhgd,bkhd->bhgqk', qc, kc).astype(jnp.float32) * (HEAD_DIM ** -0.5)
    s_sink = jnp.broadcast_to(sink.astype(jnp.float32)[None, :, :, None, None], s.shape[:-1] + (1,))
    p = jax.nn.softmax(jnp.concatenate([s, s_sink], axis=-1), axis=-1)[..., :n_ctx]
    o = jnp.einsum('bhgqk,bkhd->bqhgd', p.astype(vc.dtype), vc)
    return o.reshape(bsz, n_ctx, ATTN_WIDTH)


def centred_conv3(z, w):
    zp = jnp.pad(z, ((0, 0), (1, 1), (0, 0)))
    return zp[:, :-2] * w[0] + zp[:, 1:-1] * w[1] + zp[:, 2:] * w[2]


def diag_scan(lam_bar, drive, h0, reverse):
    if h0 is not None:
        edge = -1 if reverse else 0
        drive = drive.at[:, edge].add(lam_bar * h0)
    decay = jnp.broadcast_to(lam_bar, (1, drive.shape[1]) + lam_bar.shape)

    def combine(left, right):
        a_l, b_l = left
        a_r, b_r = right
        return a_l * a_r, a_r * b_l + b_r

    _, h = lax.associative_scan(combine, (decay, drive), reverse=reverse, axis=1)
    return h


def s5_bidirectional(u, uc, lam_re, lam_im, log_dt, b_re, b_im, c_re, c_im, d_skip, w_glu, b_glu,
                     with_ctx_out):
    f32 = jnp.float32
    lam = lax.complex(lam_re.astype(f32), lam_im.astype(f32))
    dt = jnp.exp(log_dt.astype(f32))[..., None]
    lam_bar = jnp.exp(lam * dt)
    b_bar = ((lam_bar - 1) / lam)[..., None] * lax.complex(b_re.astype(f32), b_im.astype(f32))
    c_mat = lax.complex(c_re.astype(f32), c_im.astype(f32))

    def drive(t, direction):
        tg = t.astype(f32).reshape(t.shape[0], t.shape[1], SSM_GROUPS, SSM_GROUP).astype(jnp.complex64)
        return jnp.einsum('blgi,gpi->blgp', tg, b_bar[direction])

    h_ctx_f = diag_scan(lam_bar[0], drive(uc, 0), None, False)
    h_ctx_b = diag_scan(lam_bar[1], drive(uc, 1), None, True)
    h_lat_f = diag_scan(lam_bar[0], drive(u, 0), h_ctx_f[:, -1], False)
    h_lat_b = diag_scan(lam_bar[1], drive(u, 1), h_ctx_b[:, 0], True)

    def readout(t, hf, hb):
        y = jnp.real(jnp.einsum('blgp,gip->blgi', hf, c_mat[0])
                     + jnp.einsum('blgp,gip->blgi', hb, c_mat[1])).reshape(t.shape)
        y = (y + d_skip.astype(f32) * t.astype(f32)).astype(t.dtype)
        g = jax.nn.gelu(y)
        return g * jax.nn.sigmoid(g @ w_glu + b_glu)

    out_ctx = readout(uc, h_ctx_f, h_ctx_b) if with_ctx_out else None
    return readout(u, h_lat_f, h_lat_b), out_ctx


def hybrid_mixer(h_lat, h_ctx, w_in, conv_w, sink, lam_re, lam_im, log_dt, b_re, b_im, c_re, c_im,
                 d_skip, w_glu, b_glu, w_out, cos, sin, with_ctx_out):
    bsz, n_lat, _ = h_lat.shape
    n_ctx = h_ctx.shape[1]
    q, k, v, cb, cc, cx, u = split_in_proj(h_lat @ w_in)
    qc, kc, vc, cbc, ccc, cxc, uc = split_in_proj(h_ctx @ w_in)
    sink = sink.reshape(N_KV_HEADS, Q_PER_KV)
    q = apply_axial_rope(q.reshape(bsz, n_lat, N_Q_HEADS, HEAD_DIM), cos, sin)
    q = q.reshape(bsz, n_lat, N_KV_HEADS, Q_PER_KV, HEAD_DIM)
    k = apply_axial_rope(k.reshape(bsz, n_lat, N_KV_HEADS, HEAD_DIM), cos, sin)
    v = v.reshape(bsz, n_lat, N_KV_HEADS, HEAD_DIM)
    kc = kc.reshape(bsz, n_ctx, N_KV_HEADS, HEAD_DIM)
    vc = vc.reshape(bsz, n_ctx, N_KV_HEADS, HEAD_DIM)
    attn = windowed_latent_attention(q, k, v, kc, vc, sink)
    conv = cb * centred_conv3(cc * cx, conv_w)
    ssm, ssm_c = s5_bidirectional(u, uc, lam_re, lam_im, log_dt, b_re, b_im, c_re, c_im, d_skip,
                                  w_glu, b_glu, with_ctx_out)
    out_lat = jnp.concatenate([attn, conv, ssm], axis=-1) @ w_out
    if not with_ctx_out:
        return out_lat, None
    qc = qc.reshape(bsz, n_ctx, N_KV_HEADS, Q_PER_KV, HEAD_DIM)
    attn_c = context_attention(qc, kc, vc, sink)
    conv_c = cbc * centred_conv3(ccc * cxc, conv_w)
    out_ctx = jnp.concatenate([attn_c, conv_c, ssm_c], axis=-1) @ w_out
    return out_lat, out_ctx


def squared_relu_mlp(t, w1, w2):
    return jnp.square(jax.nn.relu(t @ w1)) @ w2


def setup_inputs(seed: int = 0) -> dict:
    key = jax.random.key(seed)
    ks = jax.random.split(key, 24)
    f32 = jnp.float32
    nrm = lambda k, shape, s: jax.random.normal(k, shape, f32) * s
    lam_im_base = jnp.pi * jnp.arange(SSM_STATE, dtype=f32)
    return {
        'x': nrm(ks[0], (BATCH, SEQ, D_MODEL), 1.0),
        'c': nrm(ks[1], (BATCH, D_MODEL), 1.0),
        'ctx': nrm(ks[2], (BATCH, CTX_LEN, D_MODEL), 1.0),
        'c_ctx': nrm(ks[3], (D_MODEL,), 1.0),
        'w_ada': nrm(ks[4], (DEPTH, D_MODEL, N_MOD * D_MODEL), 0.5 * D_MODEL ** -0.5),
        'b_ada': nrm(ks[5], (DEPTH, N_MOD * D_MODEL), 0.02),
        'norm_g': 1.0 + nrm(ks[6], (DEPTH, 4, D_MODEL), 0.02),
        'w_in': nrm(ks[7], (DEPTH, D_MODEL, IN_WIDTH), D_MODEL ** -0.5),
        'conv_w': nrm(ks[8], (DEPTH, CONV_K, CONV_WIDTH), CONV_K ** -0.5),
        'attn_sink': nrm(ks[9], (DEPTH, N_Q_HEADS), 0.5),
        'ssm_lam_re': -0.5 + nrm(ks[10], (DEPTH, 2, SSM_GROUPS, SSM_STATE), 0.01),
        'ssm_lam_im': lam_im_base + nrm(ks[11], (DEPTH, 2, SSM_GROUPS, SSM_STATE), 0.01),
        'ssm_log_dt': jax.random.uniform(ks[12], (DEPTH, 2, SSM_GROUPS), f32,
                                         minval=math.log(1e-3), maxval=math.log(1e-1)),
        'ssm_b_re': nrm(ks[13], (DEPTH, 2, SSM_GROUPS, SSM_STATE, SSM_GROUP), (2 * SSM_GROUP) ** -0.5),
        'ssm_b_im': nrm(ks[14], (DEPTH, 2, SSM_GROUPS, SSM_STATE, SSM_GROUP), (2 * SSM_GROUP) ** -0.5),
        'ssm_c_re': nrm(ks[15], (DEPTH, 2, SSM_GROUPS, SSM_GROUP, SSM_STATE), (2 * SSM_STATE) ** -0.5),
        'ssm_c_im': nrm(ks[16], (DEPTH, 2, SSM_GROUPS, SSM_GROUP, SSM_STATE), (2 * SSM_STATE) ** -0.5),
        'ssm_d': nrm(ks[17], (DEPTH, SSM_WIDTH), 1.0),
        'w_glu': nrm(ks[18], (DEPTH, SSM_WIDTH, SSM_WIDTH), SSM_WIDTH ** -0.5),
        'b_glu': nrm(ks[19], (DEPTH, SSM_WIDTH), 0.02),
        'w_out': nrm(ks[20], (DEPTH, MIX_WIDTH, D_MODEL), MIX_WIDTH ** -0.5),
        'w_mlp_in': nrm(ks[21], (DEPTH, D_MODEL, D_FF), D_MODEL ** -0.5),
        'w_mlp_out': nrm(ks[22], (DEPTH, D_FF, D_MODEL), D_FF ** -0.5),
    }


def reference(x, c, ctx, c_ctx, w_ada, b_ada, norm_g, w_in, conv_w, attn_sink, ssm_lam_re, ssm_lam_im,
              ssm_log_dt, ssm_b_re, ssm_b_im, ssm_c_re, ssm_c_im, ssm_d, w_glu, b_glu, w_out,
              w_mlp_in, w_mlp_out):
    cos, sin = axial_rope_tables(x.shape[1])
    c_act = jax.nn.silu(c)
    c_ctx_act = jax.nn.silu(c_ctx)
    h, hc = x, ctx
    for l in range(DEPTH):
        with_ctx_out = l < DEPTH - 1
        mod = (c_act @ w_ada[l] + b_ada[l])[:, None, :]
        mod_c = c_ctx_act @ w_ada[l] + b_ada[l]
        sh1, sc1, g1, sh2, sc2, g2 = jnp.split(mod, N_MOD, axis=-1)
        sh1c, sc1c, g1c, sh2c, sc2c, g2c = jnp.split(mod_c, N_MOD, axis=-1)
        g_pre_mix, g_post_mix, g_pre_mlp, g_post_mlp = norm_g[l]
        a_lat = modulate(rms_norm(h, g_pre_mix), sh1, sc1)
        a_ctx = modulate(rms_norm(hc, g_pre_mix), sh1c, sc1c)
        m_lat, m_ctx = hybrid_mixer(a_lat, a_ctx, w_in[l], conv_w[l], attn_sink[l], ssm_lam_re[l],
                                    ssm_lam_im[l], ssm_log_dt[l], ssm_b_re[l], ssm_b_im[l], ssm_c_re[l],
                                    ssm_c_im[l], ssm_d[l], w_glu[l], b_glu[l], w_out[l], cos, sin,
                                    with_ctx_out)
        h = h + g1 * rms_norm(m_lat, g_post_mix)
        f_lat = squared_relu_mlp(modulate(rms_norm(h, g_pre_mlp), sh2, sc2), w_mlp_in[l], w_mlp_out[l])
        h = h + g2 * rms_norm(f_lat, g_post_mlp)
        if with_ctx_out:
            hc = hc + g1c * rms_norm(m_ctx, g_post_mix)
            f_ctx = squared_relu_mlp(modulate(rms_norm(hc, g_pre_mlp), sh2c, sc2c), w_mlp_in[l], w_mlp_out[l])
            hc = hc + g2c * rms_norm(f_ctx, g_post_mlp)
    return h
```

```python
import os
import numpy as np
from contextlib import ExitStack
import concourse.bass as bass
import concourse.mybir as mybir
from concourse.bass_utils import run_bass_kernel_spmd

F32 = mybir.dt.float32
BF16 = mybir.dt.bfloat16
I32 = mybir.dt.int32
AF = mybir.ActivationFunctionType
ALU = mybir.AluOpType
ENGS = ('pe', 'act', 'dve', 'pool', 'sp')

D = 1024
KT = 8
L = 2048
LC = 256
T = L + LC
DEPTH = 4
NSEQ = 4
NCORES = 8
EPS = 1e-6
ARENA_F32 = 53000
PI = float(np.pi)

O_Q, O_QS, O_K, O_KS, O_CB, O_CC, O_CX, O_U, O_V = 0, 4, 8, 10, 12, 14, 16, 18, 20
N_WIN = 21
BLOCKS = [(0, 256, 1)] + [(256 + 512 * i, 256 + 512 * (i + 1), 0) for i in range(4)]


class Sched:
    LIMIT = 16000

    def __init__(self, nc, es):
        self.nc, self.es = nc, es
        self.q = {e: [] for e in ENGS}
        self.cur = {}
        self.waited = {e: {} for e in ENGS}
        self.lastw = {}
        self.rds = {}
        self.dsem = {}
        self.dcnt = {}
        self.nsem = 0
        self.semobj = {}
        self.pending = {}

    def newsem(self):
        self.nsem += 1
        s = self.es.enter_context(self.nc.semaphore("s%d" % self.nsem))
        self.semobj[self.nsem] = s
        return self.nsem

    def _deps(self, eng, reads, writes):
        deps = []
        for b in reads:
            ev = self.lastw.get(b)
            if ev is not None:
                deps.append(ev)
        for b in writes:
            ev = self.lastw.get(b)
            if ev is not None:
                deps.append(ev)
            deps.extend(self.rds.get(b, ()))
        waits = []
        w = self.waited[eng]
        for (sem, val, src) in deps:
            if src == eng and eng == 'pe':
                continue
            if w.get(sem, 0) >= val:
                continue
            w[sem] = val
            waits.append((sem, val))
        return waits

    def _commit(self, ev, reads, writes):
        for b in reads:
            self.rds.setdefault(b, []).append(ev)
        for b in writes:
            self.lastw[b] = ev
            self.rds[b] = []

    def op(self, eng, fn, reads=(), writes=(), inc=True):
        waits = self._deps(eng, reads, writes)
        sem, c = self.cur.get(eng, (None, 0))
        if sem is None or (c >= self.LIMIT and not self.pending.get(eng, False)):
            sem, c = self.newsem(), 0
        self.pending[eng] = not inc
        if inc:
            c += 1
            self.cur[eng] = (sem, c)
            ev = (sem, c, eng)
        else:
            self.cur[eng] = (sem, c)
            ev = (sem, c + 1, eng)
        self.q[eng].append((waits, fn, sem, 1 if inc else 0))
        self._commit(ev, reads, writes)

    def dma(self, eng, out, in_, reads=(), writes=(), key=None, slow=False):
        key = key if key is not None else (writes[0] if writes else reads[0])
        waits = self._deps(eng, reads, writes)
        if key not in self.dsem:
            self.dsem[key] = self.newsem()
            self.dcnt[key] = 0
        self.dcnt[key] += 16
        sem = self.dsem[key]
        ev = (sem, self.dcnt[key], 'dma')
        if slow:
            fn = lambda e: e.dma_start(out=out, in_=in_, allow_slow_non_contiguous=True)
        else:
            fn = lambda e: e.dma_start(out=out, in_=in_)
        self.q[eng].append((waits, fn, sem, 16))
        self._commit(ev, reads, writes)

    def barrier(self):
        evs = []
        for eng, (sem, c) in self.cur.items():
            if c > 0:
                evs.append((sem, c))
        for key, sem in self.dsem.items():
            evs.append((sem, self.dcnt[key]))
        for eng in ENGS:
            w = self.waited[eng]
            waits = []
            for (sem, val) in evs:
                if w.get(sem, 0) >= val:
                    continue
                w[sem] = val
                waits.append((sem, val))
            if waits:
                self.q[eng].append((waits, None, None, 0))

    def emit(self):
        self.barrier()
        so = self.semobj
        with self.nc.Block() as block:
            def run(name):
                def f(e):
                    for (waits, fn, sem, inc) in self.q[name]:
                        for (s, v) in waits:
                            e.wait_ge(so[s], v)
                        if fn is not None:
                            ins = fn(e)
                            if inc:
                                ins.then_inc(so[sem], inc)
                return f
            block.tensor(run('pe'))
            block.scalar(run('act'))
            block.vector(run('dve'))
            block.gpsimd(run('pool'))
            block.sync(run('sp'))


def MM(out, lhsT, rhs, start=True, stop=True, tp=None):
    if tp is None:
        return lambda e: e.matmul(out, lhsT, rhs, start=start, stop=stop)
    return lambda e: e.matmul(out, lhsT, rhs, start=start, stop=stop, tile_position=tp)


def TR(out, in_, ident):
    return lambda e: e.transpose(out, in_, ident)


def ACT(out, in_, func, bias=None, scale=None):
    kw = {}
    if bias is not None:
        kw['bias'] = bias
    if scale is not None:
        kw['scale'] = scale
    return lambda e: e.activation(out, in_, func, **kw)


def TT(out, a, b, op):
    return lambda e: e.tensor_tensor(out, a, b, op)


def TS2(out, a, s1, s2, op0, op1):
    return lambda e: e.tensor_scalar(out, a, s1, s2, op0, op1)


def TS1(out, a, s, op):
    return lambda e: e.tensor_single_scalar(out, a, s, op)


def STT(out, a, s, b, op0, op1):
    return lambda e: e.scalar_tensor_tensor(out, a, s, b, op0, op1)


def CP(out, a):
    return lambda e: e.tensor_copy(out, a)


def MS(out, v):
    return lambda e: e.memset(out, v)


def RCP(out, a):
    return lambda e: e.reciprocal(out, a)


class _Stop(Exception):
    pass


STAGE = os.environ.get('MK_STAGE', '')
ATT_LEVEL = int(os.environ.get('MK_ATT', '3'))
ATT_NQB = int(os.environ.get('MK_NQB', '16'))


def ckpt(name):
    if STAGE == name:
        raise _Stop()


class Arena:
    def __init__(self, ap, n):
        self.ap, self.n, self.top = ap, n, 0

    def f32(self, n):
        off = self.top
        self.top += n
        assert self.top <= self.n, ("arena overflow", self.top)
        return self.ap[:, off:off + n]

    def bf16(self, n):
        nf = (n + 1) // 2
        v = self.f32(nf).bitcast(BF16)
        return v[:, 0:n]

    def i32(self, n):
        return self.f32(n).bitcast(I32)


def v3(ap, a):
    return ap.rearrange("p (a b) -> p a b", a=a)


def v4(ap, a, b):
    return ap.rearrange("p (a b c) -> p a b c", a=a, b=b)


def v5(ap, a, b, c):
    return ap.rearrange("p (a b c d) -> p a b c d", a=a, b=b, c=c)


def tk(name, c0, c1):
    return [("%s_%d" % (name, i)) for i in range(c0 // 128, (c1 - 1) // 128 + 1)]


def build_program(layers, nseq=NSEQ, out_hc=False, dbg=None):
    nc = bass.Bass("TRN2", target_bir_lowering=False)
    es = ExitStack()
    dbg = dbg or []
    NL = len(layers)

    def din(name, shape, dt=F32):
        return nc.dram_tensor(name, list(shape), dt, kind="ExternalInput")

    x_t = din("x", [nseq, L, D])
    ctx_t = din("ctx", [nseq, LC, D])
    cc_t = din("cc", [nseq + 1, D])
    wada_t = din("w_ada", [DEPTH, D, 6 * D])
    bada_t = din("b_ada", [DEPTH, 6 * D])
    ng_t = din("norm_g", [DEPTH, 4, D])
    win_t = din("w_in", [DEPTH, D, 1792])
    convw_t = din("conv_w", [DEPTH, 3, 256])
    sink_t = din("attn_sink", [DEPTH, 8])
    lre_t = din("ssm_lam_re", [DEPTH, 2, 16, 64])
    lim_t = din("ssm_lam_im", [DEPTH, 2, 16, 64])
    ldt_t = din("ssm_log_dt", [DEPTH, 2, 16])
    bre_t = din("ssm_b_re", [DEPTH, 2, 16, 64, 16])
    bim_t = din("ssm_b_im", [DEPTH, 2, 16, 64, 16])
    cre_t = din("ssm_c_re", [DEPTH, 2, 16, 16, 64])
    cim_t = din("ssm_c_im", [DEPTH, 2, 16, 16, 64])
    sd_t = din("ssm_d", [DEPTH, 256])
    wglu_t = din("w_glu", [DEPTH, 256, 256])
    bglu_t = din("b_glu", [DEPTH, 256])
    wout_t = din("w_out", [DEPTH, D, D])
    w1_t = din("w_mlp_in", [DEPTH, D, 4 * D])
    w2_t = din("w_mlp_out", [DEPTH, 4 * D, D])
    ropec_t = din("ropec", [128, L])
    ropes_t = din("ropes", [128, L])
    maskp_t = din("maskp", [128, 128])
    maskn_t = din("maskn", [128, 128])
    ident_t = din("ident", [128, 128])
    out_t = nc.dram_tensor("out", [nseq, L, D], F32, kind="ExternalOutput")
    hc_t = nc.dram_tensor("hc_out", [nseq, LC, D], F32, kind="ExternalOutput") if out_hc else None
    win_s = nc.dram_tensor("win_s", [DEPTH, N_WIN, 128, KT, 128], BF16)
    wout_s = nc.dram_tensor("wout_s", [DEPTH, KT, 128, KT, 128], BF16)
    w1_s = nc.dram_tensor("w1_s", [DEPTH, 32, 128, KT, 128], BF16)
    w2_s = nc.dram_tensor("w2_s", [DEPTH, KT, 128, 32, 128], BF16)
    wglu_s = nc.dram_tensor("wglu_s", [DEPTH, 128, 2, 256], BF16)
    s5m_s = nc.dram_tensor("s5m_s", [DEPTH, 2, 128, 10240], BF16)
    s5k_s = nc.dram_tensor("s5k_s", [DEPTH, 2, 128, 216], F32)
    dbg_t = {}

    def dap(t, offset, ap):
        return bass.AP(tensor=t, offset=offset, ap=[list(a) for a in ap])

    with es:
        S = Sched(nc, es)
        arena_t = es.enter_context(nc.sbuf_tensor("arena", [128, ARENA_F32], F32))
        AR = Arena(arena_t[:, :], ARENA_F32)
        PS = [es.enter_context(nc.psum_tensor("ps%d" % i, [128, 512], F32)) for i in range(8)]
        psn = [0]

        def psum():
            i = psn[0] % 8
            psn[0] += 1
            return PS[i], "ps%d" % i

        def dump(name, ap, shape, key):
            if name not in dbg:
                return
            t = nc.dram_tensor("dbg_" + name, list(shape), ap.dtype, kind="ExternalOutput")
            dbg_t[name] = t
            S.dma('sp', t.ap(), ap, reads=key, writes=["dbg_" + name])

        identF = AR.f32(128)
        ones_bf = AR.bf16(128)
        maskp = AR.bf16(128)
        maskn = AR.bf16(128)
        vecs = AR.f32(256)
        mods = v5(AR.f32(DEPTH * 6 * KT * 5), DEPTH, 6, KT)
        sinkexp = AR.f32(32)
        dv = v4(AR.f32(2 * 6 * KT), 2, 6)
        smark = AR.top
        h = v3(AR.f32(KT * T), KT)
        A = v3(AR.bf16(KT * T), KT)
        gmark = AR.top
        AR.top = smark

        def hk(kt, c0, c1):
            return tk("h%d" % kt, c0, c1)

        def ak(kt, c0, c1):
            return tk("A%d" % kt, c0, c1)

        try:
            S.dma('sp', identF, ident_t.ap(), writes=['identF'])
            S.dma('pool', maskp, maskp_t.ap(), writes=['maskp'])
            S.dma('pool', maskn, maskn_t.ap(), writes=['maskn'])
            S.op('pool', MS(ones_bf, 1.0), writes=['ones_bf'])
            for sl_ in range(4):
                hh_ = (0, 2, 1, 3)[sl_]
                S.dma('sp', sinkexp[:, sl_:32:4], dap(sink_t, hh_, [[0, 128], [4, 8]]), writes=['sinkexp'], slow=True)
            S.op('act', ACT(sinkexp, sinkexp, AF.Exp), reads=['sinkexp'], writes=['sinkexp'])
            ckpt('c1')

            pm = AR.top
            st1 = AR.f32(128)
            st2 = AR.f32(128)
            S.op('dve', MS(st2, 0.0), writes=['st2'])
            S.dma('sp', st1, ng_t.ap().rearrange("l k (kt p) -> (l k kt) p", p=128), writes=['st1'])
            S.dma('sp', st2[0:24, :], convw_t.ap().rearrange("l k (hf p) -> (l k hf) p", p=128), writes=['st2'])
            S.dma('sp', st2[32:40, :], sd_t.ap().rearrange("l (hf p) -> (l hf) p", p=128), writes=['st2'])
            S.dma('sp', st2[64:72, :], bglu_t.ap().rearrange("l (hf p) -> (l hf) p", p=128), writes=['st2'])
            p0, k0 = psum()
            S.op('pe', TR(p0[:, 0:128], st1, identF), reads=['st1', 'identF'], writes=[k0])
            S.op('pe', TR(p0[:, 128:256], st2, identF), reads=['st2', 'identF'], writes=[k0])
            S.op('dve', CP(vecs, p0[:, 0:256]), reads=[k0], writes=['vecs'])
            ckpt('c2')

            def NG(l, k):
                return vecs[:, (l * 4 + k) * 8:(l * 4 + k) * 8 + 8]

            def CW(l, k, hf):
                c = 128 + (l * 3 + k) * 2 + hf
                return vecs[:, c:c + 1]

            def SD(l, hf):
                c = 160 + l * 2 + hf
                return vecs[:, c:c + 1]

            def BG(l, hf):
                c = 192 + l * 2 + hf
                return vecs[:, c:c + 1]

            R = nseq + 1
            ccs = AR.f32(D)
            cactT = v3(AR.f32(KT * R), KT)
            S.op('dve', MS(ccs, 0.0), writes=['ccs'])
            S.dma('sp', ccs[0:R, :], cc_t.ap(), writes=['ccs'])
            S.op('act', ACT(ccs, ccs, AF.Silu), reads=['ccs'], writes=['ccs'])
            for g4 in range(2):
                p0, k0 = psum()
                for j in range(4):
                    kt = g4 * 4 + j
                    S.op('pe', TR(p0[:, j * 128:(j + 1) * 128], ccs[:, kt * 128:(kt + 1) * 128], identF),
                         reads=['ccs', 'identF'], writes=[k0])
                S.op('dve', CP(cactT[:, g4 * 4:g4 * 4 + 4, :], v3(p0[:, :], 4)[:, :, 0:R]), reads=[k0], writes=['cactT'])
            ckpt('c3')
            badaT = AR.f32(DEPTH * 48)
            stb = AR.f32(128)
            S.op('dve', MS(stb, 0.0), writes=['stb'])
            for hb in range(2):
                S.dma('sp', stb[0:96, :], dap(bada_t, hb * 96 * 128, [[128, 96], [1, 128]]), writes=['stb'])
                p0, k0 = psum()
                S.op('pe', TR(p0[:, 0:128], stb, identF), reads=['stb', 'identF'], writes=[k0])
                S.op('dve', CP(badaT[:, hb * 96:(hb + 1) * 96], p0[:, 0:96]), reads=[k0], writes=['badaT'])
            wr = [v3(AR.f32(KT * 512), KT) for _ in range(2)]
            ckpt('c4')
            nld = 0
            for l in layers:
                pmod, kmod = psum()
                for cb in range(12):
                    slot = nld % 2
                    nld += 1
                    S.dma('sp', wr[slot], wada_t.ap()[l, :, cb * 512:(cb + 1) * 512].rearrange("(kt p) c -> p kt c", p=128),
                          writes=['wr%d' % slot])
                    for ct in range(4):
                        ctg = cb * 4 + ct
                        for kt in range(KT):
                            S.op('pe', MM(pmod[:, ctg * R:(ctg + 1) * R], wr[slot][:, kt, ct * 128:(ct + 1) * 128],
                                          cactT[:, kt, :], start=(kt == 0), stop=(kt == KT - 1)),
                                 reads=['wr%d' % slot, 'cactT'], writes=[kmod], inc=(kt == KT - 1))
                for m in range(6):
                    S.op('dve', TT(mods[:, l, m, :, 0:R], v3(pmod[:, m * 8 * R:(m + 1) * 8 * R], 8),
                                   badaT[:, l * 48 + m * 8:l * 48 + m * 8 + 8].unsqueeze(2).broadcast_to([128, 8, R]), ALU.add),
                         reads=[kmod, 'badaT'], writes=['mods'])
            S.barrier()
            AR.top = pm
            ckpt('mods')

            def wcast(dst_ap, src_ap, key):
                S.dma('pool', dst_ap, src_ap, writes=[key])

            def wcast_layer(l):
                wi = win_t.ap()
                ws = win_s.ap()

                def colsrc(c0):
                    return wi[l, :, c0:c0 + 128].rearrange("(kt p) c -> p kt c", p=128)

                def sw_cast(dst_tile, dcol0, c0, nblk):
                    for b_ in range(nblk):
                        for half in range(2):
                            sc = c0 + 32 * b_ + 16 * (1 - half)
                            dc = dcol0 + 32 * b_ + 16 * half
                            src = wi[l, :, sc:sc + 16].rearrange("(kt p) c -> p kt c", p=128)
                            wcast(dst_tile[:, :, dc:dc + 16], src, key)
                key = 'win_s%d' % l
                for i in range(4):
                    wcast(ws[l, O_Q + i], colsrc(128 * i), key)
                    sw_cast(ws[l, O_QS + i], 0, 128 * i, 4)
                for g in range(2):
                    c0 = 512 + 64 * g
                    for dup in range(2):
                        src = wi[l, :, c0:c0 + 64].rearrange("(kt p) c -> p kt c", p=128)
                        wcast(ws[l, O_K + g][:, :, dup * 64:(dup + 1) * 64], src, key)
                        sw_cast(ws[l, O_KS + g], dup * 64, c0, 2)
                for j, (o, c0) in enumerate([(O_CB, 768), (O_CB + 1, 896), (O_CC, 1024), (O_CC + 1, 1152),
                                             (O_CX, 1280), (O_CX + 1, 1408), (O_U, 1536), (O_U + 1, 1664), (O_V, 640)]):
                    wcast(ws[l, o], colsrc(c0), key)
                for m in range(KT):
                    wcast(wout_s.ap()[l, m], wout_t.ap()[l, :, m * 128:(m + 1) * 128].rearrange("(kt p) c -> p kt c", p=128),
                          'wout_s%d' % l)
                for j in range(32):
                    wcast(w1_s.ap()[l, j], w1_t.ap()[l, :, j * 128:(j + 1) * 128].rearrange("(kt p) c -> p kt c", p=128),
                          'w1_s%d' % l)
                for m in range(KT):
                    for jh in range(2):
                        wcast(w2_s.ap()[l, m][:, jh * 16:(jh + 1) * 16, :],
                              w2_t.ap()[l, jh * 2048:(jh + 1) * 2048, m * 128:(m + 1) * 128].rearrange("(j p) c -> p j c", p=128),
                              'w2_s%d' % l)
                wcast(wglu_s.ap()[l], wglu_t.ap()[l].rearrange("(kt p) c -> p kt c", p=128), 'wglu_s%d' % l)


            wcast_layer(layers[0])
            ckpt('wcast')
            def s5_precompute(l):
                m0 = AR.top
                f = AR.f32
                LRe, LIm, DT = f(16), f(16), f(16)
                BR, BI = v3(f(256), 16), v3(f(256), 16)
                CR, CI = v3(f(256), 16), v3(f(256), 16)
                for two in range(2):
                    ps_ = slice(64 * two, 64 * two + 64)
                    S.dma('sp', v3(LRe[ps_, :], 2), dap(lre_t, l * 2048 + two * 64, [[1, 64], [1024, 2], [128, 8]]),
                          writes=['LRe'], slow=True)
                    S.dma('sp', v3(LIm[ps_, :], 2), dap(lim_t, l * 2048 + two * 64, [[1, 64], [1024, 2], [128, 8]]),
                          writes=['LIm'], slow=True)
                    S.dma('sp', v3(DT[ps_, :], 2), dap(ldt_t, l * 32 + two, [[0, 64], [16, 2], [2, 8]]),
                          writes=['DT'], slow=True)
                    S.dma('sp', v4(BR[ps_, :, :].rearrange("p a b -> p (a b)"), 2, 8),
                          dap(bre_t, l * 32768 + two * 1024, [[16, 64], [16384, 2], [2048, 8], [1, 16]]), writes=['BR'])
                    S.dma('sp', v4(BI[ps_, :, :].rearrange("p a b -> p (a b)"), 2, 8),
                          dap(bim_t, l * 32768 + two * 1024, [[16, 64], [16384, 2], [2048, 8], [1, 16]]), writes=['BI'])
                CT = f(512)
                cst = f(128)
                S.op('dve', MS(cst, 0.0), writes=['cst'])
                for (src_t, dstC, nm) in ((cre_t, CR, 'CR'), (cim_t, CI, 'CI')):
                    for tq in range(4):
                        S.dma('sp', cst[:, 0:64], dap(src_t, l * 32768 + tq * 8192, [[64, 128], [1, 64]]), writes=['cst'])
                        p0, k0 = psum()
                        S.op('pe', TR(p0[:, 0:128], cst, identF), reads=['cst', 'identF'], writes=[k0])
                        S.op('dve', CP(CT[0:64, tq * 128:(tq + 1) * 128], p0[0:64, 0:128]), reads=[k0], writes=['CT'])
                    ctv = CT[0:64, :].rearrange("p (d gp two i) -> p d gp two i", d=2, gp=8, two=2)
                    for d_ in range(2):
                        S.op('dve', CP(dstC[0:64, d_ * 8:(d_ + 1) * 8, :], ctv[:, d_, :, 0, :]), reads=['CT'], writes=[nm])
                        S.op('dve', CP(dstC[64:128, d_ * 8:(d_ + 1) * 8, :], ctv[:, d_, :, 1, :]), reads=['CT'], writes=[nm])
                E = 'dve'

                def o1(fn, rd, wr_):
                    S.op(E, fn, reads=rd, writes=wr_)
                ar, ai, mag, sn, cs = f(16), f(16), f(16), f(16), f(16)
                t1, t2 = f(16), f(16)
                ki = AR.i32(16)
                S.op('act', ACT(DT, DT, AF.Exp), reads=['DT'], writes=['DT'])
                o1(TT(ar, LRe, DT, ALU.mult), ['LRe', 'DT'], ['ar'])
                o1(TT(ai, LIm, DT, ALU.mult), ['LIm', 'DT'], ['ai'])
                S.op('act', ACT(mag, ar, AF.Exp), reads=['ar'], writes=['mag'])
                for (dst, shift, nm) in ((sn, 0.0, 'sn'), (cs, PI / 2, 'cs')):
                    o1(TS1(t1, ai, shift, ALU.add), ['ai'], ['t1'])
                    o1(TS1(t2, t1, 1.0 / (2 * PI), ALU.mult), ['t1'], ['t2'])
                    o1(CP(ki, t2), ['t2'], ['ki'])
                    o1(CP(t2, ki), ['ki'], ['t2'])
                    o1(STT(t1, t2, -2 * PI, t1, ALU.mult, ALU.add), ['t1', 't2'], ['t1'])
                    o1(TS1(t2, t1, PI, ALU.is_gt), ['t1'], ['t2'])
                    o1(STT(t1, t2, -2 * PI, t1, ALU.mult, ALU.add), ['t1', 't2'], ['t1'])
                    o1(TS1(t2, t1, -PI, ALU.is_lt), ['t1'], ['t2'])
                    o1(STT(t1, t2, 2 * PI, t1, ALU.mult, ALU.add), ['t1', 't2'], ['t1'])
                    S.op('act', ACT(dst, t1, AF.Sin), reads=['t1'], writes=[nm])
                PWr, PWi = v3(f(9 * 16), 9), v3(f(9 * 16), 9)
                o1(MS(PWr[:, 0, :], 1.0), [], ['PW'])
                o1(MS(PWi[:, 0, :], 0.0), [], ['PW'])
                o1(TT(PWr[:, 1, :], mag, cs, ALU.mult), ['mag', 'cs'], ['PW'])
                o1(TT(PWi[:, 1, :], mag, sn, ALU.mult), ['mag', 'sn'], ['PW'])
                xr, den, cr, ci = f(16), f(16), f(16), f(16)
                o1(TS1(xr, PWr[:, 1, :], -1.0, ALU.add), ['PW'], ['xr'])
                o1(TT(den, LRe, LRe, ALU.mult), ['LRe'], ['den'])
                o1(TT(t1, LIm, LIm, ALU.mult), ['LIm'], ['t1'])
                o1(TT(den, den, t1, ALU.add), ['den', 't1'], ['den'])
                o1(RCP(den, den), ['den'], ['den'])
                o1(TT(cr, xr, LRe, ALU.mult), ['xr', 'LRe'], ['cr'])
                o1(TT(t1, PWi[:, 1, :], LIm, ALU.mult), ['PW', 'LIm'], ['t1'])
                o1(TT(cr, cr, t1, ALU.add), ['cr', 't1'], ['cr'])
                o1(TT(cr, cr, den, ALU.mult), ['cr', 'den'], ['cr'])
                o1(TT(ci, PWi[:, 1, :], LRe, ALU.mult), ['PW', 'LRe'], ['ci'])
                o1(TT(t1, xr, LIm, ALU.mult), ['xr', 'LIm'], ['t1'])
                o1(TT(ci, ci, t1, ALU.subtract), ['ci', 't1'], ['ci'])
                o1(TT(ci, ci, den, ALU.mult), ['ci', 'den'], ['ci'])
                for n in range(1, 8):
                    o1(TT(PWr[:, n + 1, :], PWr[:, n, :], PWr[:, 1, :], ALU.mult), ['PW'], ['PW'])
                    o1(TT(t1, PWi[:, n, :], PWi[:, 1, :], ALU.mult), ['PW'], ['t1'])
                    o1(TT(PWr[:, n + 1, :], PWr[:, n + 1, :], t1, ALU.subtract), ['PW', 't1'], ['PW'])
                    o1(TT(PWi[:, n + 1, :], PWr[:, n, :], PWi[:, 1, :], ALU.mult), ['PW'], ['PW'])
                    o1(TT(t1, PWi[:, n, :], PWr[:, 1, :], ALU.mult), ['PW'], ['t1'])
                    o1(TT(PWi[:, n + 1, :], PWi[:, n + 1, :], t1, ALU.add), ['PW', 't1'], ['PW'])
                Qr, Qi, Qn = v3(f(9 * 16), 9), v3(f(9 * 16), 9), v3(f(9 * 16), 9)
                o1(CP(Qr[:, 0, :], PWr[:, 8, :]), ['PW'], ['Q'])
                o1(CP(Qi[:, 0, :], PWi[:, 8, :]), ['PW'], ['Q'])
                for k in range(8):
                    o1(TT(Qr[:, k + 1, :], Qr[:, k, :], Qr[:, k, :], ALU.mult), ['Q'], ['Q'])
                    o1(TT(t1, Qi[:, k, :], Qi[:, k, :], ALU.mult), ['Q'], ['t1'])
                    o1(TT(Qr[:, k + 1, :], Qr[:, k + 1, :], t1, ALU.subtract), ['Q', 't1'], ['Q'])
                    o1(TT(t1, Qr[:, k, :], Qi[:, k, :], ALU.mult), ['Q'], ['t1'])
                    o1(TS1(Qi[:, k + 1, :], t1, 2.0, ALU.mult), ['t1'], ['Q'])
                o1(TS1(Qn, Qi, -1.0, ALU.mult), ['Q'], ['Q'])
                KSc = v5(f(2 * 2 * 4 * 9 * 3), 2, 2, 4)
                for hf in range(2):
                    for j, Qx in enumerate((Qr, Qi, Qn)):
                        src = Qx.rearrange("p k (d hf gpl) -> p d hf gpl k", d=2, hf=2)[:, :, hf, :, :]
                        dst = KSc[:, hf, :, :, :].rearrange("p d gpl (k j) -> p d gpl k j", j=3)[:, :, :, :, j]
                        o1(CP(dst, src), ['Q'], ['KSc'])
                    S.dma('sp', s5k_s.ap()[l, hf], KSc[:, hf, :, :, :].rearrange("p d gpl x -> p (d gpl x)"),
                          reads=['KSc'], writes=['s5k_s%d' % l])
                Bbr, Bbi, tB = v3(f(256), 16), v3(f(256), 16), v3(f(256), 16)

                def bc16(v):
                    return v.unsqueeze(2).broadcast_to([128, 16, 16])
                o1(TT(Bbr, BR, bc16(cr), ALU.mult), ['BR', 'cr'], ['Bbr'])
                o1(TT(tB, BI, bc16(ci), ALU.mult), ['BI', 'ci'], ['tB'])
                o1(TT(Bbr, Bbr, tB, ALU.subtract), ['Bbr', 'tB'], ['Bbr'])
                o1(TT(Bbi, BI, bc16(cr), ALU.mult), ['BI', 'cr'], ['Bbi'])
                o1(TT(tB, BR, bc16(ci), ALU.mult), ['BR', 'ci'], ['tB'])
                o1(TT(Bbi, Bbi, tB, ALU.add), ['Bbi', 'tB'], ['Bbi'])
                EBr, EBi = v3(f(2048), 16), v3(f(2048), 16)
                EWr, EWi = v3(f(2048), 16), v3(f(2048), 16)
                for (Ex, nm) in ((EBr, 'EBr'), (EBi, 'EBi'), (EWr, 'EWr'), (EWi, 'EWi')):
                    S.op('dve', MS(Ex, 0.0), writes=[nm])

                def expand(Ex, val, nm, vnm):
                    ev = Ex.rearrange("p (d hf gpl) c -> p d hf gpl c", d=2, hf=2)
                    vv = val.rearrange("p (d hf gpl) j -> p d hf gpl j", d=2, hf=2)
                    for two in range(2):
                        ps_ = slice(64 * two, 64 * two + 64)
                        for gpl in range(4):
                            for d_ in range(2):
                                o1(CP(ev[ps_, d_, :, gpl, 32 * gpl + 16 * two:32 * gpl + 16 * two + 16], vv[ps_, d_, :, gpl, :]),
                                   [vnm], [nm])
                expand(EBr, Bbr, 'EBr', 'Bbr')
                expand(EBi, Bbi, 'EBi', 'Bbi')
                WUP = [v4(AR.bf16(4096), 2, 2) .rearrange("p d part (s c) -> p d part s c", s=8) for _ in range(2)]
                WDN = [AR.bf16(4096) for _ in range(2)]
                KM = [v3(AR.bf16(2048), 2).rearrange("p d (t c) -> p d t c", t=8) for _ in range(2)]
                for hf in range(2):
                    S.op('dve', MS(WDN[hf], 0.0), writes=['WDN%d' % hf])
                WDNv = [w.rearrange("p (d gpl part n c) -> p d gpl part n c", d=2, gpl=4, part=2, n=8) for w in WDN]
                Wre, Wim, BPr, BPi, tW = (v3(f(256), 16) for _ in range(5))
                WUT = v3(f(8 * 128), 8)
                for (nm,) in (('WUT',),):
                    S.op('dve', MS(WUT, 0.0), writes=[nm])
                for n in range(9):
                    prn, pin = bc16(PWr[:, n, :]), bc16(PWi[:, n, :])
                    o1(TT(Wre, CR, prn, ALU.mult), ['CR', 'PW'], ['Wre'])
                    o1(TT(tW, CI, pin, ALU.mult), ['CI', 'PW'], ['tW'])
                    o1(TT(Wre, Wre, tW, ALU.subtract), ['Wre', 'tW'], ['Wre'])
                    o1(TT(Wim, CR, pin, ALU.mult), ['CR', 'PW'], ['Wim'])
                    o1(TT(tW, CI, prn, ALU.mult), ['CI', 'PW'], ['tW'])
                    o1(TT(Wim, Wim, tW, ALU.add), ['Wim', 'tW'], ['Wim'])
                    o1(TS1(Wim, Wim, -1.0, ALU.mult), ['Wim'], ['Wim'])
                    if n >= 1:
                        for part, Wx, wnm in ((0, Wre, 'Wre'), (1, Wim, 'Wim')):
                            wv = Wx.rearrange("p (d hf gpl) i -> p d hf gpl i", d=2, hf=2)
                            for hf in range(2):
                                for two in range(2):
                                    ps_ = slice(64 * two, 64 * two + 64)
                                    o1(CP(WDNv[hf][ps_, :, :, part, n - 1, 16 * two:16 * two + 16], wv[ps_, :, hf, :, :]),
                                       [wnm], ['WDN%d' % hf])
                    if n <= 7:
                        expand(EWr, Wre, 'EWr', 'Wre')
                        expand(EWi, Wim, 'EWi', 'Wim')
                        o1(TT(BPr, Bbr, prn, ALU.mult), ['Bbr', 'PW'], ['BPr'])
                        o1(TT(tW, Bbi, pin, ALU.mult), ['Bbi', 'PW'], ['tW'])
                        o1(TT(BPr, BPr, tW, ALU.subtract), ['BPr', 'tW'], ['BPr'])
                        o1(TT(BPi, Bbr, pin, ALU.mult), ['Bbr', 'PW'], ['BPi'])
                        o1(TT(tW, Bbi, prn, ALU.mult), ['Bbi', 'PW'], ['tW'])
                        o1(TT(BPi, BPi, tW, ALU.add), ['BPi', 'tW'], ['BPi'])
                        wut = WUT.rearrange("p (d hf part) (gpl c) -> p d hf part gpl c", d=2, hf=2, gpl=4)
                        for part, Bx, bnm in ((0, BPr, 'BPr'), (1, BPi, 'BPi')):
                            bv = Bx.rearrange("p (d hf gpl) j -> p d hf gpl j", d=2, hf=2)
                            for two in range(2):
                                ps_ = slice(64 * two, 64 * two + 64)
                                for d_ in range(2):
                                    o1(CP(wut[ps_, d_, :, part, :, 16 * two:16 * two + 16], bv[ps_, d_, :, :, :]),
                                       [bnm], ['WUT'])
                        for d_ in range(2):
                            s_idx = (7 - n) if d_ == 0 else n
                            for hf in range(2):
                                p0, k0 = psum()
                                for part in range(2):
                                    idx = (d_ * 2 + hf) * 2 + part
                                    S.op('pe', TR(p0[:, part * 128:(part + 1) * 128], WUT[:, idx, :], identF),
                                         reads=['WUT', 'identF'], writes=[k0])
                                S.op('act', ACT(WUP[hf][:, d_, :, s_idx, :], v3(p0[:, 0:256], 2), AF.Copy),
                                     reads=[k0], writes=['WUP%d' % hf])
                        for d_ in range(2):
                            for hf in range(2):
                                p0, k0 = psum()
                                cnt = 0
                                for gpl in range(4):
                                    cidx = d_ * 8 + hf * 4 + gpl
                                    for (Eb, Ew, bn, wn) in ((EBr, EWr, 'EBr', 'EWr'), (EBi, EWi, 'EBi', 'EWi')):
                                        S.op('pe', MM(p0[:, 0:128], Eb[:, cidx, :], Ew[:, cidx, :], start=(cnt == 0), stop=(cnt == 7)),
                                             reads=[bn, wn], writes=[k0], inc=(cnt == 7))
                                        cnt += 1
                                S.op('act', ACT(KM[hf][:, d_, n, :], p0[:, 0:128], AF.Copy), reads=[k0], writes=['KM%d' % hf])
                for hf in range(2):
                    dst = s5m_s.ap()[l, hf]
                    S.dma('sp', dst[:, 0:4096], WUP[hf].rearrange("p d part s c -> p (d part s c)"),
                          reads=['WUP%d' % hf], writes=['s5m_s%d' % l])
                    S.dma('sp', dst[:, 4096:8192], WDN[hf], reads=['WDN%d' % hf], writes=['s5m_s%d' % l])
                    S.dma('sp', dst[:, 8192:10240], KM[hf].rearrange("p d t c -> p (d t c)"),
                          reads=['KM%d' % hf], writes=['s5m_s%d' % l])
                S.barrier()
                AR.top = m0

            for l in layers:
                s5_precompute(l)
            assert AR.top == smark
            AR.top = gmark
            ckpt('s5pre')

            ring = {}

            def load_w(name, slots, shape_ap_fn, src_ap, skey):
                st = ring[name]
                i = st['n'] % len(st['slots'])
                st['n'] += 1
                ap = st['slots'][i]
                key = '%s_%d' % (name, i)
                S.dma('sp', ap, src_ap, reads=[skey], writes=[key])
                return ap, key

            def mk_ring(name, nslots, nelem_bf16, shaper):
                ring[name] = {'n': 0, 'slots': [shaper(AR.bf16(nelem_bf16)) for _ in range(nslots)]}

            def load_seq(s):
                m0 = AR.top
                stg = [AR.f32(D) for _ in range(2)]
                for tt in range(T // 128):
                    sl = tt % 2
                    if tt < 2:
                        src = ctx_t.ap()[s, tt * 128:(tt + 1) * 128, :]
                    else:
                        src = x_t.ap()[s, (tt - 2) * 128:(tt - 1) * 128, :]
                    S.dma('sp', stg[sl], src, writes=['stg%d' % sl])
                    for g4 in range(2):
                        p0, k0 = psum()
                        for j in range(4):
                            kt = g4 * 4 + j
                            S.op('pe', TR(p0[:, j * 128:(j + 1) * 128], stg[sl][:, kt * 128:(kt + 1) * 128], identF),
                                 reads=['stg%d' % sl, 'identF'], writes=[k0])
                        wk = []
                        for j in range(4):
                            wk += hk(g4 * 4 + j, tt * 128, tt * 128 + 128)
                        eng = 'act' if (g4 == 0) else 'dve'
                        if eng == 'act':
                            S.op('act', ACT(h[:, g4 * 4:g4 * 4 + 4, tt * 128:(tt + 1) * 128], v3(p0[:, :], 4), AF.Copy),
                                 reads=[k0], writes=wk)
                        else:
                            S.op('dve', CP(h[:, g4 * 4:g4 * 4 + 4, tt * 128:(tt + 1) * 128], v3(p0[:, :], 4)), reads=[k0], writes=wk)
                S.barrier()
                AR.top = m0

            def store_seq(s):
                m0 = AR.top
                stg = [AR.f32(D) for _ in range(2)]
                tts = list(range(2, T // 128)) + (list(range(2)) if out_hc else [])
                for n_, tt in enumerate(tts):
                    sl = n_ % 2
                    for g4 in range(2):
                        p0, k0 = psum()
                        rk = []
                        for j in range(4):
                            kt = g4 * 4 + j
                            S.op('pe', TR(p0[:, j * 128:(j + 1) * 128], h[:, kt, tt * 128:(tt + 1) * 128], identF),
                                 reads=hk(kt, tt * 128, tt * 128 + 128) + ['identF'], writes=[k0])
                        if g4 == 0:
                            S.op('act', ACT(stg[sl][:, 0:512], p0[:, :], AF.Copy), reads=[k0], writes=['ostg%d' % sl])
                        else:
                            S.op('dve', CP(stg[sl][:, 512:1024], p0[:, :]), reads=[k0], writes=['ostg%d' % sl])
                    if tt >= 2:
                        dst = out_t.ap()[s, (tt - 2) * 128:(tt - 1) * 128, :]
                    else:
                        dst = hc_t.ap()[s, tt * 128:(tt + 1) * 128, :]
                    S.dma('sp', dst, stg[sl], reads=['ostg%d' % sl], writes=['outd'], key='ostg%d' % sl)
                S.barrier()
                AR.top = m0

            def rms_stats(src_fn, src_keys_fn, n, tmp_sq, eng_sq='pool'):
                pss, kss = psum()
                for kt in range(KT):
                    sq, sqk = tmp_sq[kt % 2]
                    S.op(eng_sq, TT(sq[:, 0:n], src_fn(kt), src_fn(kt), ALU.mult), reads=src_keys_fn(kt), writes=[sqk])
                    S.op('pe', MM(pss[:, 0:n], ones_bf, sq[:, 0:n], start=(kt == 0), stop=(kt == KT - 1)),
                         reads=[sqk, 'ones_bf'], writes=[kss])
                return pss, kss

            def rstd_from(pss, kss, n, tmp, tmpk, rstd, rstdk):
                S.op('act', ACT(tmp[:, 0:n], pss[:, 0:n], AF.Sqrt, bias=EPS, scale=1.0 / D), reads=[kss], writes=[tmpk])
                S.op('dve', RCP(rstd[:, 0:n], tmp[:, 0:n]), reads=[tmpk], writes=[rstdk])

            def mk_scratch():
                sc = {'sq': [(AR.bf16(512), 'scq%d' % i) for i in range(2)],
                      'f': [(AR.f32(512), 'scf%d' % i) for i in range(5)]}
                return sc

            def norm_to_A(i_scale, i_shift, sc):
                sqs = sc['sq']
                tmp, tmpk = sc['f'][2]
                rstds = [sc['f'][3], sc['f'][4]]
                tts = [sc['f'][0], sc['f'][1]]
                for bi, (c0, c1, w) in enumerate(BLOCKS):
                    n = c1 - c0
                    pss, kss = rms_stats(lambda kt: h[:, kt, c0:c1], lambda kt: hk(kt, c0, c1), n, sqs)
                    rstd, rk = rstds[bi % 2]
                    rstd_from(pss, kss, n, tmp, tmpk, rstd, rk)
                    for kt in range(KT):
                        t_, tkk = tts[kt % 2]
                        S.op('dve', TT(t_[:, 0:n], h[:, kt, c0:c1], rstd[:, 0:n], ALU.mult), reads=hk(kt, c0, c1) + [rk], writes=[tkk])
                        S.op('act', ACT(A[:, kt, c0:c1], t_[:, 0:n], AF.Identity, bias=dv[:, w, i_shift, kt:kt + 1],
                                        scale=dv[:, w, i_scale, kt:kt + 1]), reads=[tkk, 'dv'], writes=ak(kt, c0, c1))

            def proj_mm(wt, wkey, c0, c1):
                pp, pk = psum()
                n = c1 - c0
                for kt in range(KT):
                    S.op('pe', MM(pp[:, 0:n], wt[:, kt, :], A[:, kt, c0:c1], start=(kt == 0), stop=(kt == KT - 1)),
                         reads=[wkey] + ak(kt, c0, c1), writes=[pk], inc=(kt == KT - 1))
                return pp, pk

            def win_tile(l, o):
                return load_w('win', None, None, win_s.ap()[l, o], 'win_s%d' % l)

            def layer(l, s):
                lm0 = AR.top
                for w, r in ((0, s), (1, nseq)):
                    M = lambda m: mods[:, l, m, :, r]
                    S.op('dve', STT(dv[:, w, 0, :], M(1), 1.0, NG(l, 0), ALU.add, ALU.mult), reads=['mods', 'vecs', 'dv'], writes=['dv'])
                    S.op('dve', CP(dv[:, w, 1, :], M(0)), reads=['mods', 'dv'], writes=['dv'])
                    S.op('dve', TT(dv[:, w, 2, :], M(2), NG(l, 1), ALU.mult), reads=['mods', 'vecs', 'dv'], writes=['dv'])
                    S.op('dve', STT(dv[:, w, 3, :], M(4), 1.0, NG(l, 2), ALU.add, ALU.mult), reads=['mods', 'vecs', 'dv'], writes=['dv'])
                    S.op('dve', CP(dv[:, w, 4, :], M(3)), reads=['mods', 'dv'], writes=['dv'])
                    S.op('dve', TT(dv[:, w, 5, :], M(5), NG(l, 3), ALU.mult), reads=['mods', 'vecs', 'dv'], writes=['dv'])
                sso = v3(AR.bf16(2 * T), 2)
                pm0 = AR.top
                sc = mk_scratch()
                norm_to_A(0, 1, sc)
                dump('a', A[:, :, :], [128, KT, T], sum([ak(kt, 0, T) for kt in range(KT)], []))
                ckpt('norm')
                mk_ring('win', 2, KT * 128, lambda a: v3(a, KT))
                u = v3(AR.bf16(2 * T), 2)
                for hf in range(2):
                    wt, wk = win_tile(l, O_U + hf)
                    for (c0, c1, w) in BLOCKS:
                        pp, pk = proj_mm(wt, wk, c0, c1)
                        S.op('act', ACT(u[:, hf, c0:c1], pp[:, 0:c1 - c0], AF.Copy), reads=[pk], writes=tk('u%d' % hf, c0, c1))
                dump('u', u[:, :, :], [128, 2, T], tk('u0', 0, T) + tk('u1', 0, T))
                ckpt('uproj')
                mats = AR.bf16(10240)
                ksc = AR.f32(216)
                P0 = [v3(AR.f32(2 * 290), 2) for _ in range(4)]
                P1 = [v3(AR.f32(2 * 290), 2) for _ in range(4)]
                Xb = [[v3(AR.bf16(2 * 290), 2) for _ in range(4)] for _ in range(2)]
                KT1 = [v3(AR.f32(2 * 290), 2) for _ in range(1)]
                KT2 = [v3(AR.f32(2 * 290), 2) for _ in range(1)]
                yv = [sc['f'][0], sc['f'][1]]
                (g1, g1k), (g2, g2k), (g3, g3k) = sc['f'][2], sc['f'][3], sc['f'][4]
                WUPv = mats[:, 0:4096].rearrange("p (d part s c) -> p d part s c", d=2, part=2, s=8)
                WDNv = mats[:, 4096:8192].rearrange("p (d gpl part n c) -> p d gpl part n c", d=2, gpl=4, part=2, n=8)
                KMv = mats[:, 8192:10240].rearrange("p (d t c) -> p d t c", d=2, t=8)
                kscv = ksc.rearrange("p (d gpl k j) -> p d gpl k j", d=2, gpl=4, k=9)
                NCH = T // 8
                for hf in range(2):
                    S.dma('sp', mats, s5m_s.ap()[l, hf], reads=['s5m_s%d' % l], writes=['mats'])
                    S.dma('sp', ksc, s5k_s.ap()[l, hf], reads=['s5k_s%d' % l], writes=['ksc'])
                    for d_ in range(2):
                        chains = []
                        for gpl in range(4):
                            chain = []
                            xk = 'X%d' % gpl
                            xbk = 'Xb%d_%d' % (d_, gpl)
                            ke = 'dve'
                            S.op(ke, MS(Xb[d_][gpl][:, :, 0:1] if d_ == 0 else Xb[d_][gpl][:, :, 32:33], 0.0), writes=[xbk])
                            for part in range(2):
                                pp, pk = psum()
                                for s_ in range(8):
                                    S.op('pe', MM(pp[:, 0:NCH], WUPv[32 * gpl:32 * gpl + 32, d_, part, s_, :],
                                                  u[32 * gpl:32 * gpl + 32, hf, s_:T:8], start=(s_ == 0), stop=(s_ == 7),
                                                  tp=(32 * gpl, 0)),
                                         reads=['mats'] + tk('u%d' % hf, 0, T), writes=[pk], inc=(s_ == 7))
                                if d_ == 0:
                                    S.op('act', ACT(P0[gpl][:, part, 1:289], pp[:, 0:288], AF.Copy), reads=[pk], writes=[xk])
                                else:
                                    S.op('act', ACT(P0[gpl][:, part, 0:32], pp[:, 0:32], AF.Copy), reads=[pk], writes=[xk])
                                    S.op('act', ACT(P0[gpl][:, part, 33:289], pp[:, 32:288], AF.Copy), reads=[pk], writes=[xk])

                            def ks_run(lo, W, nlev, right, final_bf, src, dst):
                                for k in range(nlev):
                                    sh = 1 << k
                                    n = W - sh
                                    last = (k == nlev - 1)
                                    a_, b_, nb_ = kscv[:, d_, gpl, k, 0:1], kscv[:, d_, gpl, k, 1:2], kscv[:, d_, gpl, k, 2:3]
                                    Dd = Xb[d_][gpl] if (last and final_bf) else dst
                                    wkeys = [xk] + ([xbk] if (last and final_bf) else [])
                                    if right:
                                        so_, do_ = slice(lo, lo + n), slice(lo + sh, lo + W)
                                        ho_ = slice(lo, lo + sh)
                                    else:
                                        so_, do_ = slice(lo + sh, lo + W), slice(lo, lo + n)
                                        ho_ = slice(lo + n, lo + W)
                                    if gpl < 3:
                                        chain.append(('dve', STT(dst[:, :, do_], src[:, :, so_], a_, src[:, :, do_], ALU.mult, ALU.add), [xk, 'ksc'], wkeys))
                                        chain.append(('dve', STT(Dd[:, 0, do_], src[:, 1, so_], nb_, dst[:, 0, do_], ALU.mult, ALU.add), [xk, 'ksc'], wkeys))
                                        chain.append(('dve', STT(Dd[:, 1, do_], src[:, 0, so_], b_, dst[:, 1, do_], ALU.mult, ALU.add), [xk, 'ksc'], wkeys))
                                        chain.append(('dve', CP(Dd[:, :, ho_], src[:, :, ho_]), [xk], wkeys))
                                    else:
                                        t1_, t1k = KT1[0], 'kt1_%d' % gpl
                                        t2_, t2k = KT2[0], 'kt2_%d' % gpl
                                        chain.append(('act', ACT(t1_[:, :, 0:n], src[:, :, so_], AF.Identity, scale=a_), [xk, 'ksc'], [t1k]))
                                        chain.append(('act', ACT(t2_[:, 0, 0:n], src[:, 1, so_], AF.Identity, scale=nb_), [xk, 'ksc'], [t2k]))
                                        chain.append(('act', ACT(t2_[:, 1, 0:n], src[:, 0, so_], AF.Identity, scale=b_), [xk, 'ksc'], [t2k]))
                                        chain.append(('pool', TT(dst[:, :, do_], t1_[:, :, 0:n], src[:, :, do_], ALU.add), [xk, t1k], wkeys))
                                        chain.append(('pool', TT(Dd[:, :, do_], t2_[:, :, 0:n], dst[:, :, do_], ALU.add), [xk, t2k], wkeys))
                                        chain.append(('pool', CP(Dd[:, :, ho_], src[:, :, ho_]), [xk], wkeys))
                                    src, dst = dst, src
                                return src
                            if d_ == 0:
                                ks_run(1, 288, 9, True, True, P0[gpl], P1[gpl])
                            else:
                                res = ks_run(0, 32, 5, False, False, P0[gpl], P1[gpl])
                                ce_ = 'dve' if gpl < 3 else 'pool'
                                chain.append((ce_, CP(Xb[d_][gpl][:, :, 0:32], res[:, :, 0:32]), [xk], [xk, xbk]))
                                chain.append((ce_, CP(P0[gpl][:, :, 289:290], res[:, :, 0:1]), [xk], [xk]))
                                ks_run(33, 257, 9, False, True, P0[gpl], P1[gpl])
                            chains.append(chain)
                        for i_ in range(max(len(c) for c in chains)):
                            for c in chains:
                                if i_ < len(c):
                                    S.op(c[i_][0], c[i_][1], reads=c[i_][2], writes=c[i_][3])
                    for bi, (c0, c1, w) in enumerate(BLOCKS):
                        n = c1 - c0
                        nch = n // 8
                        ch0 = c0 // 8
                        py, pyk = psum()
                        for s_ in range(8):
                            ops = []
                            for d_ in range(2):
                                srange = range(0, s_ + 1) if d_ == 0 else range(s_, 8)
                                for sp_ in srange:
                                    ops.append((py[:, s_:n:8], KMv[:, d_, abs(s_ - sp_), :], u[:, hf, c0 + sp_:c1:8], None,
                                                ['mats'] + tk('u%d' % hf, c0, c1), False))
                                nidx = s_ if d_ == 0 else 7 - s_
                                e0 = ch0 if d_ == 0 else (ch0 + 1 if w else ch0 + 2)
                                for gpl in range(4):
                                    for part in range(2):
                                        ops.append((py[32 * gpl:32 * gpl + 32, s_:n:8], WDNv[:, d_, gpl, part, nidx, :],
                                                    Xb[d_][gpl][:, part, e0:e0 + nch], (0, 32 * gpl),
                                                    ['mats', 'Xb%d_%d' % (d_, gpl)], (d_ == 1 and part == 1)))
                            for i_, (o_, l_, r_, tp_, rd_, st_) in enumerate(ops):
                                S.op('pe', MM(o_, l_, r_, start=(i_ == 0), stop=st_, tp=tp_),
                                     reads=rd_, writes=[pyk], inc=(i_ == len(ops) - 1))
                        y_, yk = yv[bi % 2]
                        S.op('dve', STT(y_[:, 0:n], u[:, hf, c0:c1], SD(l, hf), py[:, 0:n], ALU.mult, ALU.add),
                             reads=[pyk, 'vecs'] + tk('u%d' % hf, c0, c1), writes=[yk])
                        S.op('pool', TT(g1[:, 0:n], y_[:, 0:n], y_[:, 0:n], ALU.mult), reads=[yk], writes=[g1k])
                        S.op('pool', TS2(g1[:, 0:n], g1[:, 0:n], 0.044715, 1.0, ALU.mult, ALU.add), reads=[g1k], writes=[g1k])
                        S.op('pool', TT(g2[:, 0:n], g1[:, 0:n], y_[:, 0:n], ALU.mult), reads=[g1k, yk], writes=[g2k])
                        S.op('act', ACT(g3[:, 0:n], g2[:, 0:n], AF.Sigmoid, scale=1.5957691216057308), reads=[g2k], writes=[g3k])
                        S.op('dve', TT(sso[:, hf, c0:c1], y_[:, 0:n], g3[:, 0:n], ALU.mult), reads=[yk, g3k],
                             writes=tk('sso%d' % hf, c0, c1))
                dump('g', sso[:, :, :], [128, 2, T], tk('sso0', 0, T) + tk('sso1', 0, T))
                wg = v3(AR.bf16(512), 2)
                S.dma('sp', wg, wglu_s.ap()[l], reads=['wglu_s%d' % l], writes=['wg'])
                for bi, (c0, c1, w) in enumerate(BLOCKS):
                    n = c1 - c0
                    zs = []
                    for ho in range(2):
                        pz, pzk = psum()
                        for hf in range(2):
                            S.op('pe', MM(pz[:, 0:n], wg[:, hf, ho * 128:(ho + 1) * 128], sso[:, hf, c0:c1], start=(hf == 0), stop=(hf == 1)),
                                 reads=['wg'] + tk('sso%d' % hf, c0, c1), writes=[pzk], inc=(hf == 1))
                        zs.append((pz, pzk))
                    for ho in range(2):
                        pz, pzk = zs[ho]
                        gt, gk = (g1, g1k) if ho == 0 else (g2, g2k)
                        S.op('act', ACT(gt[:, 0:n], pz[:, 0:n], AF.Sigmoid, bias=BG(l, ho)), reads=[pzk, 'vecs'], writes=[gk])
                        S.op('pool', TT(sso[:, ho, c0:c1], sso[:, ho, c0:c1], gt[:, 0:n], ALU.mult),
                             reads=[gk] + tk('sso%d' % ho, c0, c1), writes=tk('sso%d' % ho, c0, c1))
                dump('ssm', sso[:, :, :], [128, 2, T], tk('sso0', 0, T) + tk('sso1', 0, T))
                ckpt('s5')
                S.barrier()
                AR.top = pm0
                cvo = v3(AR.bf16(2 * T), 2)
                pm1 = AR.top
                mk_ring('win', 3, KT * 128, lambda a: v3(a, KT))
                CW_ = T + 4
                ccx = v3(AR.bf16(2 * CW_), 2)
                cb = v3(AR.bf16(2 * T), 2)
                cct = [AR.bf16(512) for _ in range(2)]
                o1t = [AR.f32(512) for _ in range(2)]
                for col in (0, 257, 258, CW_ - 1):
                    S.op('pool', MS(ccx[:, :, col:col + 1], 0.0), writes=['ccxpad'])

                def coff(c0):
                    return c0 + 1 if c0 < LC else c0 + 3
                for hf in range(2):
                    wcb, kcb = win_tile(l, O_CB + hf)
                    wcc, kcc = win_tile(l, O_CC + hf)
                    wcx, kcx = win_tile(l, O_CX + hf)
                    for bi, (c0, c1, w) in enumerate(BLOCKS):
                        n = c1 - c0
                        pp, pk = proj_mm(wcb, kcb, c0, c1)
                        S.op('act', ACT(cb[:, hf, c0:c1], pp[:, 0:n], AF.Copy), reads=[pk], writes=tk('cb%d' % hf, c0, c1))
                        pp, pk = proj_mm(wcc, kcc, c0, c1)
                        ct_, ctk = cct[bi % 2], 'cct%d' % (bi % 2)
                        S.op('act', ACT(ct_[:, 0:n], pp[:, 0:n], AF.Copy), reads=[pk], writes=[ctk])
                        pp, pk = proj_mm(wcx, kcx, c0, c1)
                        S.op('dve', TT(ccx[:, hf, coff(c0):coff(c0) + n], pp[:, 0:n], ct_[:, 0:n], ALU.mult), reads=[pk, ctk],
                             writes=['ccx%d' % hf])
                for hf in range(2):
                    for bi, (c0, c1, w) in enumerate(BLOCKS):
                        n = c1 - c0
                        b0 = coff(c0)
                        ot, otk = o1t[bi % 2], 'o1t%d' % (bi % 2)
                        S.op('pool', TS1(ot[:, 0:n], ccx[:, hf, b0 - 1:b0 - 1 + n], CW(l, 0, hf), ALU.mult),
                             reads=['ccx%d' % hf, 'ccxpad', 'vecs'], writes=[otk])
                        S.op('dve', STT(ot[:, 0:n], ccx[:, hf, b0:b0 + n], CW(l, 1, hf), ot[:, 0:n], ALU.mult, ALU.add),
                             reads=['ccx%d' % hf, 'vecs', otk], writes=[otk])
                        S.op('dve', STT(ot[:, 0:n], ccx[:, hf, b0 + 1:b0 + 1 + n], CW(l, 2, hf), ot[:, 0:n], ALU.mult, ALU.add),
                             reads=['ccx%d' % hf, 'ccxpad', 'vecs', otk], writes=[otk])
                        S.op('dve', TT(cvo[:, hf, c0:c1], ot[:, 0:n], cb[:, hf, c0:c1], ALU.mult),
                             reads=[otk] + tk('cb%d' % hf, c0, c1), writes=tk('cvo%d' % hf, c0, c1))
                dump('conv', cvo[:, :, :], [128, 2, T], tk('cvo0', 0, T) + tk('cvo1', 0, T))
                ckpt('conv')
                S.barrier()
                AR.top = pm1
                q = v3(AR.bf16(4 * T), 4)
                kd = v3(AR.bf16(2 * T), 2)
                Vt = v4(AR.bf16(18 * 2 * 128), 18, 2)
                pm2 = AR.top
                mk_ring('win', 4, KT * 128, lambda a: v3(a, KT))
                ropec = AR.f32(L)
                ropes = AR.f32(L)
                S.dma('sp', ropec, ropec_t.ap(), writes=['ropec'])
                S.dma('sp', ropes, ropes_t.ap(), writes=['ropes'])
                rt = [AR.f32(512) for _ in range(2)]
                for (dst, dnm, o_pl, o_sw, cnt) in ((q, 'q', O_Q, O_QS, 4), (kd, 'kd', O_K, O_KS, 2)):
                    for i in range(cnt):
                        wp, kp = win_tile(l, o_pl + i)
                        wsw, ksw = win_tile(l, o_sw + i)
                        for (c0, c1, w) in BLOCKS:
                            n = c1 - c0
                            pp, pk = proj_mm(wp, kp, c0, c1)
                            wkeys = tk('%s%d' % (dnm, i), c0, c1)
                            if w:
                                S.op('act', ACT(dst[:, i, c0:c1], pp[:, 0:n], AF.Copy), reads=[pk], writes=wkeys)
                            else:
                                p2, pk2 = proj_mm(wsw, ksw, c0, c1)
                                lc0 = c0 - LC
                                S.op('dve', TT(rt[0][:, 0:n], pp[:, 0:n], ropec[:, lc0:lc0 + n], ALU.mult), reads=[pk, 'ropec'], writes=['rt0'])
                                S.op('dve', TT(rt[1][:, 0:n], p2[:, 0:n], ropes[:, lc0:lc0 + n], ALU.mult), reads=[pk2, 'ropes'], writes=['rt1'])
                                S.op('pool', TT(dst[:, i, c0:c1], rt[0][:, 0:n], rt[1][:, 0:n], ALU.add), reads=['rt0', 'rt1'], writes=wkeys)
                S.op('pool', MS(Vt[:, :, :, 64:128], 1.0), writes=['Vones'])
                wv, kv = win_tile(l, O_V)
                for t4 in range(0, 18, 4):
                    nt = min(4, 18 - t4)
                    pp, pk = psum()
                    for j in range(nt):
                        tt = t4 + j
                        for kt in range(KT):
                            S.op('pe', MM(pp[:, j * 128:(j + 1) * 128], A[:, kt, tt * 128:(tt + 1) * 128], wv[:, kt, :],
                                          start=(kt == 0), stop=(kt == KT - 1)),
                                 reads=[kv] + ak(kt, tt * 128, tt * 128 + 128), writes=[pk], inc=(kt == KT - 1))
                    S.op('act', ACT(Vt[:, t4:t4 + nt, :, 0:64], pp[:, 0:nt * 128].rearrange("p (t g c) -> p t g c", t=nt, g=2), AF.Copy),
                         reads=[pk], writes=['V%d' % (t4 + j) for j in range(nt)])
                dump('q', q[:, :, :], [128, 4, T], sum([tk('q%d' % i, 0, T) for i in range(4)], []))
                dump('kd', kd[:, :, :], [128, 2, T], tk('kd0', 0, T) + tk('kd1', 0, T))
                ckpt('qkv')
                S.barrier()
                AR.top = pm2
                Pt = [v3(AR.bf16(5 * 512), 5) for _ in range(2)]
                rc = AR.f32(512)
                nblk = [0]

                def attend(qc0, key_tiles, l_):
                    for g in range(2):
                        P_ = Pt[nblk[0] % 2]
                        pkey = 'Pt%d' % (nblk[0] % 2)
                        nblk[0] += 1
                        nk = len(key_tiles)
                        for half in range(2):
                            rows = slice(64 * half, 64 * half + 64)
                            for kp in range(0, nk, 2):
                                nkk = min(2, nk - kp)
                                ps_, psk = psum()
                                for j in range(nkk):
                                    kc0 = key_tiles[kp + j][0] * 128
                                    for a_ in range(2):
                                        hh = 2 * a_ + half
                                        hd = 4 * g + hh
                                        qi = hd // 2
                                        S.op('pe', MM(ps_[:, (2 * j + a_) * 128:(2 * j + a_ + 1) * 128], kd[rows, g, kc0:kc0 + 128],
                                                      q[rows, qi, qc0:qc0 + 128], start=True, stop=True, tp=(64 * half, 0)),
                                             reads=tk('kd%d' % g, kc0, kc0 + 128) + tk('q%d' % qi, qc0, qc0 + 128), writes=[psk],
                                             inc=(j == nkk - 1 and a_ == 1))
                                S.op('act', ACT(P_[:, kp:kp + nkk, half * 256:(half + 1) * 256], v3(ps_[:, 0:nkk * 256], nkk), AF.Exp, scale=0.125),
                                     reads=[psk], writes=[pkey + '_%d' % (kp + j) for j in range(nkk)])
                        for ki_, (ktile, msk) in enumerate(key_tiles):
                            if msk is not None:
                                mk_, mkk = (maskp, 'maskp') if msk == 'p' else (maskn, 'maskn')
                                S.op('pool', TT(v3(P_[:, ki_, :], 4), v3(P_[:, ki_, :], 4),
                                                mk_.unsqueeze(1).broadcast_to([128, 4, 128]), ALU.mult),
                                     reads=[pkey + '_%d' % ki_, mkk], writes=[pkey + '_%d' % ki_])
                        if ATT_LEVEL < 1:
                            continue
                        po, pok = psum()
                        for ki_, (ktile, msk) in enumerate(key_tiles):
                            S.op('pe', MM(po[:, :], Vt[:, ktile, g, :], P_[:, ki_, :], start=(ki_ == 0), stop=(ki_ == nk - 1)),
                                 reads=['V%d' % ktile, 'Vones', pkey + '_%d' % ki_], writes=[pok], inc=(ki_ == nk - 1))
                        if ATT_LEVEL < 2:
                            continue
                        S.op('dve', TT(v3(rc[0:64, :], 4), v3(po[64:128, :], 4),
                                       sinkexp[64:128, l_ * 8 + 4 * g:l_ * 8 + 4 * g + 4].unsqueeze(2).broadcast_to([64, 4, 128]), ALU.add),
                             reads=[pok, 'sinkexp'], writes=['rc'])
                        S.op('dve', RCP(rc[0:64, :], rc[0:64, :]), reads=['rc'], writes=['rc'])
                        if ATT_LEVEL < 3:
                            continue
                        for half in range(2):
                            S.op('dve', TT(A[64 * half:64 * half + 64, 2 * g:2 * g + 2, qc0:qc0 + 128],
                                           v3(po[0:64, half * 256:(half + 1) * 256], 2), v3(rc[0:64, half * 256:(half + 1) * 256], 2), ALU.mult),
                                 reads=[pok, 'rc'], writes=ak(2 * g, qc0, qc0 + 128) + ak(2 * g + 1, qc0, qc0 + 128))
                for qb in range(ATT_NQB):
                    kts = [(0, None), (1, None)]
                    if qb > 0:
                        kts.append((2 + qb - 1, 'p'))
                    kts.append((2 + qb, None))
                    if qb < 15:
                        kts.append((2 + qb + 1, 'n'))
                    attend(LC + qb * 128, kts, l)
                if l < DEPTH - 1:
                    for qb in range(2):
                        attend(qb * 128, [(0, None), (1, None)], l)
                dump('attn', A[:, 0:4, :], [128, 4, T], sum([ak(kt, 0, T) for kt in range(4)], []))
                ckpt('attn')
                S.barrier()
                AR.top = pm1
                wo = [v3(AR.bf16(KT * 128), KT) for _ in range(KT)]
                for m in range(KT):
                    S.dma('sp', wo[m], wout_s.ap()[l, m], reads=['wout_s%d' % l], writes=['wo%d' % m])
                mblk = v3(AR.f32(KT * 512), KT)
                sqs = [(AR.bf16(512), 'osq%d' % i) for i in range(2)]
                tmp = AR.f32(512)
                rstd = AR.f32(512)
                tts = [AR.f32(512) for _ in range(2)]

                def mixk(kt, c0, c1):
                    if kt < 4:
                        return A[:, kt, c0:c1], ak(kt, c0, c1)
                    if kt < 6:
                        return cvo[:, kt - 4, c0:c1], tk('cvo%d' % (kt - 4), c0, c1)
                    return sso[:, kt - 6, c0:c1], tk('sso%d' % (kt - 6), c0, c1)
                for bi, (c0, c1, w) in enumerate(BLOCKS):
                    n = c1 - c0
                    for m in range(KT):
                        pp, pk = psum()
                        for kt in range(KT):
                            rap, rkeys = mixk(kt, c0, c1)
                            S.op('pe', MM(pp[:, 0:n], wo[m][:, kt, :], rap, start=(kt == 0), stop=(kt == KT - 1)),
                                 reads=['wo%d' % m] + rkeys, writes=[pk], inc=(kt == KT - 1))
                        S.op('act', ACT(mblk[:, m, 0:n], pp[:, 0:n], AF.Copy), reads=[pk], writes=['mblk%d' % m])
                    pss, kss = rms_stats(lambda kt: mblk[:, kt, 0:n], lambda kt: ['mblk%d' % kt], n, sqs)
                    rstd_from(pss, kss, n, tmp, 'otmp', rstd, 'orstd')
                    for m in range(KT):
                        t_, tkk = tts[m % 2], 'ott%d' % (m % 2)
                        S.op('pool' if m % 4 == 3 else 'dve', TT(t_[:, 0:n], mblk[:, m, 0:n], rstd[:, 0:n], ALU.mult), reads=['mblk%d' % m, 'orstd'], writes=[tkk])
                        S.op('dve', STT(h[:, m, c0:c1], t_[:, 0:n], dv[:, w, 2, m:m + 1], h[:, m, c0:c1], ALU.mult, ALU.add),
                             reads=[tkk, 'dv'] + hk(m, c0, c1), writes=hk(m, c0, c1))
                dump('h1', h[:, :, :], [128, KT, T], sum([hk(kt, 0, T) for kt in range(KT)], []))
                ckpt('oproj')
                S.barrier()
                AR.top = lm0
                sc = mk_scratch()
                norm_to_A(3, 4, sc)
                NB = 576
                HB = 288
                hid = v3(AR.bf16(32 * NB), 32)
                mk_ring('w1', 3, KT * 128, lambda a: v3(a, KT))
                mk_ring('w2', 2, 32 * 128, lambda a: v3(a, 32))
                fblk = v3(AR.f32(KT * NB), KT)
                sqs = sc['sq']
                tmp, tmpk = sc['f'][2]
                rstds = [sc['f'][3], sc['f'][4]]
                tts = [sc['f'][0], sc['f'][1]]
                for b4 in range(T // NB):
                    bc0 = b4 * NB
                    for j in range(32):
                        wt, wk = load_w('w1', None, None, w1_s.ap()[l, j], 'w1_s%d' % l)
                        for hh in range(2):
                            c0 = bc0 + hh * HB
                            pp, pk = proj_mm(wt, wk, c0, c0 + HB)
                            hkey = 'hid%d_%d' % (j, hh)
                            S.op('act', ACT(hid[:, j, hh * HB:(hh + 1) * HB], pp[:, 0:HB], AF.Relu), reads=[pk], writes=[hkey])
                            S.op('pool', TT(hid[:, j, hh * HB:(hh + 1) * HB], hid[:, j, hh * HB:(hh + 1) * HB],
                                            hid[:, j, hh * HB:(hh + 1) * HB], ALU.mult), reads=[hkey], writes=[hkey])
                    for m in range(KT):
                        wt2, wk2 = load_w('w2', None, None, w2_s.ap()[l, m], 'w2_s%d' % l)
                        for hh in range(2):
                            pp, pk = psum()
                            for j in range(32):
                                S.op('pe', MM(pp[:, 0:HB], wt2[:, j, :], hid[:, j, hh * HB:(hh + 1) * HB], start=(j == 0), stop=(j == 31)),
                                     reads=[wk2, 'hid%d_%d' % (j, hh)], writes=[pk], inc=(j == 31))
                            S.op('act', ACT(fblk[:, m, hh * HB:(hh + 1) * HB], pp[:, 0:HB], AF.Copy), reads=[pk], writes=['fblk%d_%d' % (m, hh)])
                    for hh in range(2):
                        c0 = bc0 + hh * HB
                        pss, kss = rms_stats(lambda kt: fblk[:, kt, hh * HB:(hh + 1) * HB], lambda kt: ['fblk%d_%d' % (kt, hh)], HB, sqs)
                        rstd, rk = rstds[hh]
                        rstd_from(pss, kss, HB, tmp, tmpk, rstd, rk)
                        segs = []
                        if c0 < LC:
                            e = min(LC, c0 + HB)
                            segs.append((c0, e, 1))
                            if e < c0 + HB:
                                segs.append((e, c0 + HB, 0))
                        else:
                            segs.append((c0, c0 + HB, 0))
                        for m in range(KT):
                            t_, tkk = tts[m % 2]
                            S.op('pool' if m % 4 == 3 else 'dve', TT(t_[:, 0:HB], fblk[:, m, hh * HB:(hh + 1) * HB], rstd[:, 0:HB], ALU.mult),
                                 reads=['fblk%d_%d' % (m, hh), rk], writes=[tkk])
                            for (a0, a1, w) in segs:
                                S.op('dve', STT(h[:, m, a0:a1], t_[:, a0 - c0:a1 - c0], dv[:, w, 5, m:m + 1], h[:, m, a0:a1], ALU.mult, ALU.add),
                                     reads=[tkk, 'dv'] + hk(m, a0, a1), writes=hk(m, a0, a1))
                S.barrier()
                AR.top = lm0

            for s in range(nseq):
                load_seq(s)
                ckpt('load')
                for li_, l in enumerate(layers):
                    if s == 0 and li_ + 1 < len(layers):
                        wcast_layer(layers[li_ + 1])
                    layer(l, s)
                store_seq(s)

        except _Stop:
            pass
        S.emit()
    return nc, dbg_t


def _consts():
    f32 = np.float32
    n = np.arange(L)
    row = (n // 64).astype(f32)
    col = (n % 64).astype(f32)
    freqs = (np.float32(10000.0) ** (-np.arange(16, dtype=f32) / np.float32(16))).astype(f32)
    cosT = np.zeros((128, L), f32)
    sinT = np.zeros((128, L), f32)
    for p in range(128):
        dd = p % 64
        i = dd % 16
        pos = row if dd < 32 else col
        ang = (pos * freqs[i]).astype(f32)
        cosT[p] = np.cos(ang).astype(f32)
        sgn = -1.0 if (dd % 32) < 16 else 1.0
        sinT[p] = (sgn * np.sin(ang)).astype(f32)
    ii = np.arange(128)[:, None]
    jj = np.arange(128)[None, :]
    maskp = (ii >= jj).astype(f32)
    maskn = (ii <= jj).astype(f32)
    return dict(ropec=cosT, ropes=sinT, maskp=maskp, maskn=maskn, ident=np.eye(128, dtype=f32))


_WKEYS = ['w_ada', 'b_ada', 'norm_g', 'w_in', 'conv_w', 'attn_sink', 'ssm_lam_re', 'ssm_lam_im', 'ssm_log_dt',
          'ssm_b_re', 'ssm_b_im', 'ssm_c_re', 'ssm_c_im', 'ssm_d', 'w_glu', 'b_glu', 'w_out', 'w_mlp_in', 'w_mlp_out']

LAUNCH_PLAN = [[0, 1, 2, 3]]


def kernel(**inputs):
    f32 = np.float32
    inp = {k: np.ascontiguousarray(np.asarray(v, dtype=f32)) for k, v in inputs.items()}
    consts = _consts()
    hx = inp['x']
    hctx = inp['ctx']
    B = hx.shape[0]
    per = B // NCORES
    for li, layers in enumerate(LAUNCH_PLAN):
        last = (li == len(LAUNCH_PLAN) - 1)
        nc, _ = build_program(layers, nseq=per, out_hc=not last)
        in_maps = []
        for c in range(NCORES):
            sl = slice(c * per, (c + 1) * per)
            m = {'x': np.ascontiguousarray(hx[sl]), 'ctx': np.ascontiguousarray(hctx[sl]),
                 'cc': np.ascontiguousarray(np.concatenate([inp['c'][sl], inp['c_ctx'][None, :]], axis=0))}
            for k in _WKEYS:
                m[k] = inp[k]
            m.update(consts)
            in_maps.append(m)
        res = run_bass_kernel_spmd(nc, in_maps, core_ids=list(range(NCORES)))
        hx = np.concatenate([r['out'] for r in res.results], axis=0)
        if not last:
            hctx = np.concatenate([r['hc_out'] for r in res.results], axis=0)
    return hx.astype(f32)
```

```python
import os
import numpy as np
from contextlib import ExitStack
import concourse.bass as bass
import concourse.mybir as mybir
from concourse.bass_utils import run_bass_kernel_spmd

F32 = mybir.dt.float32
BF16 = mybir.dt.bfloat16
I32 = mybir.dt.int32
AF = mybir.ActivationFunctionType
ALU = mybir.AluOpType
ENGS = ('pe', 'act', 'dve', 'pool', 'sp')

D = 1024
KT = 8
L = 2048
LC = 256
T = L + LC
DEPTH = 4
NSEQ = 4
NCORES = 8
EPS = 1e-6
ARENA_F32 = 53000
PI = float(np.pi)

O_Q, O_QS, O_K, O_KS, O_CB, O_CC, O_CX, O_U, O_V = 0, 4, 8, 10, 12, 14, 16, 18, 20
N_WIN = 21
BLOCKS = [(0, 256, 1)] + [(256 + 512 * i, 256 + 512 * (i + 1), 0) for i in range(4)]


class Sched:
    LIMIT = 16000

    def __init__(self, nc, es):
        self.nc, self.es = nc, es
        self.q = {e: [] for e in ENGS}
        self.cur = {}
        self.waited = {e: {} for e in ENGS}
        self.lastw = {}
        self.rds = {}
        self.dsem = {}
        self.dcnt = {}
        self.nsem = 0
        self.semobj = {}
        self.pending = {}

    def newsem(self):
        self.nsem += 1
        s = self.es.enter_context(self.nc.semaphore("s%d" % self.nsem))
        self.semobj[self.nsem] = s
        return self.nsem

    def _deps(self, eng, reads, writes):
        deps = []
        for b in reads:
            ev = self.lastw.get(b)
            if ev is not None:
                deps.append(ev)
        for b in writes:
            ev = self.lastw.get(b)
            if ev is not None:
                deps.append(ev)
            deps.extend(self.rds.get(b, ()))
        waits = []
        w = self.waited[eng]
        for (sem, val, src) in deps:
            if src == eng and eng == 'pe':
                continue
            if w.get(sem, 0) >= val:
                continue
            w[sem] = val
            waits.append((sem, val))
        return waits

    def _commit(self, ev, reads, writes):
        for b in reads:
            self.rds.setdefault(b, []).append(ev)
        for b in writes:
            self.lastw[b] = ev
            self.rds[b] = []

    def op(self, eng, fn, reads=(), writes=(), inc=True):
        waits = self._deps(eng, reads, writes)
        sem, c = self.cur.get(eng, (None, 0))
        if sem is None or (c >= self.LIMIT and not self.pending.get(eng, False)):
            sem, c = self.newsem(), 0
        self.pending[eng] = not inc
        if inc:
            c += 1
            self.cur[eng] = (sem, c)
            ev = (sem, c, eng)
        else:
            self.cur[eng] = (sem, c)
            ev = (sem, c + 1, eng)
        self.q[eng].append((waits, fn, sem, 1 if inc else 0))
        self._commit(ev, reads, writes)

    def dma(self, eng, out, in_, reads=(), writes=(), key=None, slow=False):
        key = key if key is not None else (writes[0] if writes else reads[0])
        waits = self._deps(eng, reads, writes)
        if key not in self.dsem:
            self.dsem[key] = self.newsem()
            self.dcnt[key] = 0
        self.dcnt[key] += 16
        sem = self.dsem[key]
        ev = (sem, self.dcnt[key], 'dma')
        if slow:
            fn = lambda e: e.dma_start(out=out, in_=in_, allow_slow_non_contiguous=True)
        else:
            fn = lambda e: e.dma_start(out=out, in_=in_)
        self.q[eng].append((waits, fn, sem, 16))
        self._commit(ev, reads, writes)

    def barrier(self, skip_pool=False):
        evs = []
        for eng, (sem, c) in self.cur.items():
            if skip_pool and eng == 'pool':
                continue
            if c > 0:
                evs.append((sem, c))
        for key, sem in self.dsem.items():
            if skip_pool and isinstance(key, str) and key.startswith(('win_s', 'wout_s', 'w1_s', 'w2_s', 'wglu_s')):
                continue
            evs.append((sem, self.dcnt[key]))
        for eng in ENGS:
            if skip_pool and eng == 'pool':
                continue
            w = self.waited[eng]
            waits = []
            for (sem, val) in evs:
                if w.get(sem, 0) >= val:
                    continue
                w[sem] = val
                waits.append((sem, val))
            if waits:
                self.q[eng].append((waits, None, None, 0))

    def emit(self):
        self.barrier()
        so = self.semobj
        with self.nc.Block() as block:
            def run(name):
                def f(e):
                    for (waits, fn, sem, inc) in self.q[name]:
                        for (s, v) in waits:
                            e.wait_ge(so[s], v)
                        if fn is not None:
                            ins = fn(e)
                            if inc:
                                ins.then_inc(so[sem], inc)
                return f
            block.tensor(run('pe'))
            block.scalar(run('act'))
            block.vector(run('dve'))
            block.gpsimd(run('pool'))
            block.sync(run('sp'))


def MM(out, lhsT, rhs, start=True, stop=True, tp=None):
    if tp is None:
        return lambda e: e.matmul(out, lhsT, rhs, start=start, stop=stop)
    return lambda e: e.matmul(out, lhsT, rhs, start=start, stop=stop, tile_position=tp)


def TR(out, in_, ident):
    return lambda e: e.transpose(out, in_, ident)


def ACT(out, in_, func, bias=None, scale=None):
    kw = {}
    if bias is not None:
        kw['bias'] = bias
    if scale is not None:
        kw['scale'] = scale
    return lambda e: e.activation(out, in_, func, **kw)


def TT(out, a, b, op):
    return lambda e: e.tensor_tensor(out, a, b, op)


def TS2(out, a, s1, s2, op0, op1):
    return lambda e: e.tensor_scalar(out, a, s1, s2, op0, op1)


def TS1(out, a, s, op):
    return lambda e: e.tensor_single_scalar(out, a, s, op)


def STT(out, a, s, b, op0, op1):
    return lambda e: e.scalar_tensor_tensor(out, a, s, b, op0, op1)


def CP(out, a):
    return lambda e: e.tensor_copy(out, a)


def MS(out, v):
    return lambda e: e.memset(out, v)


def RCP(out, a):
    return lambda e: e.reciprocal(out, a)


class _Stop(Exception):
    pass


STAGE = os.environ.get('MK_STAGE', '')
ATT_LEVEL = int(os.environ.get('MK_ATT', '3'))
ATT_NQB = int(os.environ.get('MK_NQB', '16'))


def ckpt(name):
    if STAGE == name:
        raise _Stop()


class Arena:
    def __init__(self, ap, n):
        self.ap, self.n, self.top = ap, n, 0

    def f32(self, n):
        off = self.top
        self.top += n
        assert self.top <= self.n, ("arena overflow", self.top)
        return self.ap[:, off:off + n]

    def bf16(self, n):
        nf = (n + 1) // 2
        v = self.f32(nf).bitcast(BF16)
        return v[:, 0:n]

    def i32(self, n):
        return self.f32(n).bitcast(I32)


def v3(ap, a):
    return ap.rearrange("p (a b) -> p a b", a=a)


def v4(ap, a, b):
    return ap.rearrange("p (a b c) -> p a b c", a=a, b=b)


def v5(ap, a, b, c):
    return ap.rearrange("p (a b c d) -> p a b c d", a=a, b=b, c=c)


def tk(name, c0, c1):
    return [("%s_%d" % (name, i)) for i in range(c0 // 128, (c1 - 1) // 128 + 1)]


def build_program(layers, nseq=NSEQ, out_hc=False, dbg=None):
    nc = bass.Bass("TRN2", target_bir_lowering=False)
    es = ExitStack()
    dbg = dbg or []
    NL = len(layers)

    def din(name, shape, dt=F32):
        return nc.dram_tensor(name, list(shape), dt, kind="ExternalInput")

    x_t = din("x", [nseq, L, D])
    ctx_t = din("ctx", [nseq, LC, D])
    cc_t = din("cc", [nseq + 1, D])
    wada_t = din("w_ada", [DEPTH, D, 6 * D])
    bada_t = din("b_ada", [DEPTH, 6 * D])
    ng_t = din("norm_g", [DEPTH, 4, D])
    win_t = din("w_in", [DEPTH, D, 1792])
    convw_t = din("conv_w", [DEPTH, 3, 256])
    sink_t = din("attn_sink", [DEPTH, 8])
    lre_t = din("ssm_lam_re", [DEPTH, 2, 16, 64])
    lim_t = din("ssm_lam_im", [DEPTH, 2, 16, 64])
    ldt_t = din("ssm_log_dt", [DEPTH, 2, 16])
    bre_t = din("ssm_b_re", [DEPTH, 2, 16, 64, 16])
    bim_t = din("ssm_b_im", [DEPTH, 2, 16, 64, 16])
    cre_t = din("ssm_c_re", [DEPTH, 2, 16, 16, 64])
    cim_t = din("ssm_c_im", [DEPTH, 2, 16, 16, 64])
    sd_t = din("ssm_d", [DEPTH, 256])
    wglu_t = din("w_glu", [DEPTH, 256, 256])
    bglu_t = din("b_glu", [DEPTH, 256])
    wout_t = din("w_out", [DEPTH, D, D])
    w1_t = din("w_mlp_in", [DEPTH, D, 4 * D])
    w2_t = din("w_mlp_out", [DEPTH, 4 * D, D])
    ropec_t = din("ropec", [128, L])
    ropes_t = din("ropes", [128, L])
    maskp_t = din("maskp", [128, 128])
    maskn_t = din("maskn", [128, 128])
    ident_t = din("ident", [128, 128])
    out_t = nc.dram_tensor("out", [nseq, L, D], F32, kind="ExternalOutput")
    hc_t = nc.dram_tensor("hc_out", [nseq, LC, D], F32, kind="ExternalOutput") if out_hc else None
    win_s = nc.dram_tensor("win_s", [DEPTH, N_WIN, 128, KT, 128], BF16)
    wout_s = nc.dram_tensor("wout_s", [DEPTH, KT, 128, KT, 128], BF16)
    w1_s = nc.dram_tensor("w1_s", [DEPTH, 32, 128, KT, 128], BF16)
    w2_s = nc.dram_tensor("w2_s", [DEPTH, KT, 128, 32, 128], BF16)
    wglu_s = nc.dram_tensor("wglu_s", [DEPTH, 128, 2, 256], BF16)
    s5m_s = nc.dram_tensor("s5m_s", [DEPTH, 2, 128, 10240], BF16)
    s5k_s = nc.dram_tensor("s5k_s", [DEPTH, 2, 128, 216], F32)
    dbg_t = {}

    def dap(t, offset, ap):
        return bass.AP(tensor=t, offset=offset, ap=[list(a) for a in ap])

    with es:
        S = Sched(nc, es)
        arena_t = es.enter_context(nc.sbuf_tensor("arena", [128, ARENA_F32], F32))
        AR = Arena(arena_t[:, :], ARENA_F32)
        PS = [es.enter_context(nc.psum_tensor("ps%d" % i, [128, 512], F32)) for i in range(8)]
        psn = [0]

        def psum():
            i = psn[0] % 8
            psn[0] += 1
            return PS[i], "ps%d" % i

        def dump(name, ap, shape, key):
            if name not in dbg:
                return
            t = nc.dram_tensor("dbg_" + name, list(shape), ap.dtype, kind="ExternalOutput")
            dbg_t[name] = t
            S.dma('sp', t.ap(), ap, reads=key, writes=["dbg_" + name])

        identF = AR.f32(128)
        ones_bf = AR.bf16(128)
        maskp = AR.bf16(128)
        maskn = AR.bf16(128)
        vecs = AR.f32(256)
        mods = v5(AR.f32(DEPTH * 6 * KT * 5), DEPTH, 6, KT)
        sinkexp = AR.f32(32)
        dv = v4(AR.f32(2 * 6 * KT), 2, 6)
        smark = AR.top
        h = v3(AR.f32(KT * T), KT)
        A = v3(AR.bf16(KT * T), KT)
        gmark = AR.top
        AR.top = smark

        def hk(kt, c0, c1):
            return tk("h%d" % kt, c0, c1)

        def ak(kt, c0, c1):
            return tk("A%d" % kt, c0, c1)

        try:
            S.dma('sp', identF, ident_t.ap(), writes=['identF'])
            S.dma('pool', maskp, maskp_t.ap(), writes=['maskp'])
            S.dma('pool', maskn, maskn_t.ap(), writes=['maskn'])
            S.op('pool', MS(ones_bf, 1.0), writes=['ones_bf'])
            for sl_ in range(4):
                hh_ = (0, 2, 1, 3)[sl_]
                S.dma('sp', sinkexp[:, sl_:32:4], dap(sink_t, hh_, [[0, 128], [4, 8]]), writes=['sinkexp'], slow=True)
            S.op('act', ACT(sinkexp, sinkexp, AF.Exp), reads=['sinkexp'], writes=['sinkexp'])
            ckpt('c1')

            def wcast(dst_ap, src_ap, key):
                S.dma('pool', dst_ap, src_ap, writes=[key])

            def wcast_layer(l):
                wi = win_t.ap()
                ws = win_s.ap()

                def colsrc(c0):
                    return wi[l, :, c0:c0 + 128].rearrange("(kt p) c -> p kt c", p=128)

                def sw_cast(dst_tile, dcol0, c0, nblk):
                    for b_ in range(nblk):
                        for half in range(2):
                            sc = c0 + 32 * b_ + 16 * (1 - half)
                            dc = dcol0 + 32 * b_ + 16 * half
                            src = wi[l, :, sc:sc + 16].rearrange("(kt p) c -> p kt c", p=128)
                            wcast(dst_tile[:, :, dc:dc + 16], src, key)
                key = 'win_s%d' % l
                for i in range(4):
                    wcast(ws[l, O_Q + i], colsrc(128 * i), key)
                    sw_cast(ws[l, O_QS + i], 0, 128 * i, 4)
                for g in range(2):
                    c0 = 512 + 64 * g
                    for dup in range(2):
                        src = wi[l, :, c0:c0 + 64].rearrange("(kt p) c -> p kt c", p=128)
                        wcast(ws[l, O_K + g][:, :, dup * 64:(dup + 1) * 64], src, key)
                        sw_cast(ws[l, O_KS + g], dup * 64, c0, 2)
                for j, (o, c0) in enumerate([(O_CB, 768), (O_CB + 1, 896), (O_CC, 1024), (O_CC + 1, 1152),
                                             (O_CX, 1280), (O_CX + 1, 1408), (O_U, 1536), (O_U + 1, 1664), (O_V, 640)]):
                    wcast(ws[l, o], colsrc(c0), key)
                for m in range(KT):
                    wcast(wout_s.ap()[l, m], wout_t.ap()[l, :, m * 128:(m + 1) * 128].rearrange("(kt p) c -> p kt c", p=128),
                          'wout_s%d' % l)
                for j in range(32):
                    wcast(w1_s.ap()[l, j], w1_t.ap()[l, :, j * 128:(j + 1) * 128].rearrange("(kt p) c -> p kt c", p=128),
                          'w1_s%d' % l)
                for m in range(KT):
                    for jh in range(2):
                        wcast(w2_s.ap()[l, m][:, jh * 16:(jh + 1) * 16, :],
                              w2_t.ap()[l, jh * 2048:(jh + 1) * 2048, m * 128:(m + 1) * 128].rearrange("(j p) c -> p j c", p=128),
                              'w2_s%d' % l)
                wcast(wglu_s.ap()[l], wglu_t.ap()[l].rearrange("(kt p) c -> p kt c", p=128), 'wglu_s%d' % l)


            for l_ in layers:
                wcast_layer(l_)
            pm = AR.top
            st1 = AR.f32(128)
            st2 = AR.f32(128)
            S.op('dve', MS(st2, 0.0), writes=['st2'])
            S.dma('sp', st1, ng_t.ap().rearrange("l k (kt p) -> (l k kt) p", p=128), writes=['st1'])
            S.dma('sp', st2[0:24, :], convw_t.ap().rearrange("l k (hf p) -> (l k hf) p", p=128), writes=['st2'])
            S.dma('sp', st2[32:40, :], sd_t.ap().rearrange("l (hf p) -> (l hf) p", p=128), writes=['st2'])
            S.dma('sp', st2[64:72, :], bglu_t.ap().rearrange("l (hf p) -> (l hf) p", p=128), writes=['st2'])
            p0, k0 = psum()
            S.op('pe', TR(p0[:, 0:128], st1, identF), reads=['st1', 'identF'], writes=[k0])
            S.op('pe', TR(p0[:, 128:256], st2, identF), reads=['st2', 'identF'], writes=[k0])
            S.op('dve', CP(vecs, p0[:, 0:256]), reads=[k0], writes=['vecs'])
            ckpt('c2')

            def NG(l, k):
                return vecs[:, (l * 4 + k) * 8:(l * 4 + k) * 8 + 8]

            def CW(l, k, hf):
                c = 128 + (l * 3 + k) * 2 + hf
                return vecs[:, c:c + 1]

            def SD(l, hf):
                c = 160 + l * 2 + hf
                return vecs[:, c:c + 1]

            def BG(l, hf):
                c = 192 + l * 2 + hf
                return vecs[:, c:c + 1]

            R = nseq + 1
            ccs = AR.f32(D)
            cactT = v3(AR.f32(KT * R), KT)
            S.op('dve', MS(ccs, 0.0), writes=['ccs'])
            S.dma('sp', ccs[0:R, :], cc_t.ap(), writes=['ccs'])
            S.op('act', ACT(ccs, ccs, AF.Silu), reads=['ccs'], writes=['ccs'])
            for g4 in range(2):
                p0, k0 = psum()
                for j in range(4):
                    kt = g4 * 4 + j
                    S.op('pe', TR(p0[:, j * 128:(j + 1) * 128], ccs[:, kt * 128:(kt + 1) * 128], identF),
                         reads=['ccs', 'identF'], writes=[k0])
                S.op('dve', CP(cactT[:, g4 * 4:g4 * 4 + 4, :], v3(p0[:, :], 4)[:, :, 0:R]), reads=[k0], writes=['cactT'])
            ckpt('c3')
            badaT = AR.f32(DEPTH * 48)
            stb = AR.f32(128)
            S.op('dve', MS(stb, 0.0), writes=['stb'])
            for hb in range(2):
                S.dma('sp', stb[0:96, :], dap(bada_t, hb * 96 * 128, [[128, 96], [1, 128]]), writes=['stb'])
                p0, k0 = psum()
                S.op('pe', TR(p0[:, 0:128], stb, identF), reads=['stb', 'identF'], writes=[k0])
                S.op('dve', CP(badaT[:, hb * 96:(hb + 1) * 96], p0[:, 0:96]), reads=[k0], writes=['badaT'])
            wr = [v3(AR.f32(KT * 512), KT) for _ in range(2)]
            ckpt('c4')
            nld = 0
            for l in layers:
                pmod, kmod = psum()
                for cb in range(12):
                    slot = nld % 2
                    nld += 1
                    S.dma('sp', wr[slot], wada_t.ap()[l, :, cb * 512:(cb + 1) * 512].rearrange("(kt p) c -> p kt c", p=128),
                          writes=['wr%d' % slot])
                    for ct in range(4):
                        ctg = cb * 4 + ct
                        for kt in range(KT):
                            S.op('pe', MM(pmod[:, ctg * R:(ctg + 1) * R], wr[slot][:, kt, ct * 128:(ct + 1) * 128],
                                          cactT[:, kt, :], start=(kt == 0), stop=(kt == KT - 1)),
                                 reads=['wr%d' % slot, 'cactT'], writes=[kmod], inc=(kt == KT - 1))
                for m in range(6):
                    S.op('dve', TT(mods[:, l, m, :, 0:R], v3(pmod[:, m * 8 * R:(m + 1) * 8 * R], 8),
                                   badaT[:, l * 48 + m * 8:l * 48 + m * 8 + 8].unsqueeze(2).broadcast_to([128, 8, R]), ALU.add),
                         reads=[kmod, 'badaT'], writes=['mods'])
            S.barrier(skip_pool=True)
            AR.top = pm
            ckpt('mods')

            ckpt('wcast')
            def s5_precompute(l):
                m0 = AR.top
                f = AR.f32
                LRe, LIm, DT = f(16), f(16), f(16)
                BR, BI = v3(f(256), 16), v3(f(256), 16)
                CR, CI = v3(f(256), 16), v3(f(256), 16)
                for two in range(2):
                    ps_ = slice(64 * two, 64 * two + 64)
                    S.dma('sp', v3(LRe[ps_, :], 2), dap(lre_t, l * 2048 + two * 64, [[1, 64], [1024, 2], [128, 8]]),
                          writes=['LRe'], slow=True)
                    S.dma('sp', v3(LIm[ps_, :], 2), dap(lim_t, l * 2048 + two * 64, [[1, 64], [1024, 2], [128, 8]]),
                          writes=['LIm'], slow=True)
                    S.dma('sp', v3(DT[ps_, :], 2), dap(ldt_t, l * 32 + two, [[0, 64], [16, 2], [2, 8]]),
                          writes=['DT'], slow=True)
                    S.dma('sp', v4(BR[ps_, :, :].rearrange("p a b -> p (a b)"), 2, 8),
                          dap(bre_t, l * 32768 + two * 1024, [[16, 64], [16384, 2], [2048, 8], [1, 16]]), writes=['BR'])
                    S.dma('sp', v4(BI[ps_, :, :].rearrange("p a b -> p (a b)"), 2, 8),
                          dap(bim_t, l * 32768 + two * 1024, [[16, 64], [16384, 2], [2048, 8], [1, 16]]), writes=['BI'])
                CT = f(512)
                cst = f(128)
                S.op('dve', MS(cst, 0.0), writes=['cst'])
                for (src_t, dstC, nm) in ((cre_t, CR, 'CR'), (cim_t, CI, 'CI')):
                    for tq in range(4):
                        S.dma('sp', cst[:, 0:64], dap(src_t, l * 32768 + tq * 8192, [[64, 128], [1, 64]]), writes=['cst'])
                        p0, k0 = psum()
                        S.op('pe', TR(p0[:, 0:128], cst, identF), reads=['cst', 'identF'], writes=[k0])
                        S.op('dve', CP(CT[0:64, tq * 128:(tq + 1) * 128], p0[0:64, 0:128]), reads=[k0], writes=['CT'])
                    ctv = CT[0:64, :].rearrange("p (d gp two i) -> p d gp two i", d=2, gp=8, two=2)
                    for d_ in range(2):
                        S.op('dve', CP(dstC[0:64, d_ * 8:(d_ + 1) * 8, :], ctv[:, d_, :, 0, :]), reads=['CT'], writes=[nm])
                        S.op('dve', CP(dstC[64:128, d_ * 8:(d_ + 1) * 8, :], ctv[:, d_, :, 1, :]), reads=['CT'], writes=[nm])
                E = 'dve'

                def o1(fn, rd, wr_):
                    S.op(E, fn, reads=rd, writes=wr_)
                ar, ai, mag, sn, cs = f(16), f(16), f(16), f(16), f(16)
                t1, t2 = f(16), f(16)
                ki = AR.i32(16)
                S.op('act', ACT(DT, DT, AF.Exp), reads=['DT'], writes=['DT'])
                o1(TT(ar, LRe, DT, ALU.mult), ['LRe', 'DT'], ['ar'])
                o1(TT(ai, LIm, DT, ALU.mult), ['LIm', 'DT'], ['ai'])
                S.op('act', ACT(mag, ar, AF.Exp), reads=['ar'], writes=['mag'])
                for (dst, shift, nm) in ((sn, 0.0, 'sn'), (cs, PI / 2, 'cs')):
                    o1(TS1(t1, ai, shift, ALU.add), ['ai'], ['t1'])
                    o1(TS1(t2, t1, 1.0 / (2 * PI), ALU.mult), ['t1'], ['t2'])
                    o1(CP(ki, t2), ['t2'], ['ki'])
                    o1(CP(t2, ki), ['ki'], ['t2'])
                    o1(STT(t1, t2, -2 * PI, t1, ALU.mult, ALU.add), ['t1', 't2'], ['t1'])
                    o1(TS1(t2, t1, PI, ALU.is_gt), ['t1'], ['t2'])
                    o1(STT(t1, t2, -2 * PI, t1, ALU.mult, ALU.add), ['t1', 't2'], ['t1'])
                    o1(TS1(t2, t1, -PI, ALU.is_lt), ['t1'], ['t2'])
                    o1(STT(t1, t2, 2 * PI, t1, ALU.mult, ALU.add), ['t1', 't2'], ['t1'])
                    S.op('act', ACT(dst, t1, AF.Sin), reads=['t1'], writes=[nm])
                PWr, PWi = v3(f(9 * 16), 9), v3(f(9 * 16), 9)
                o1(MS(PWr[:, 0, :], 1.0), [], ['PW'])
                o1(MS(PWi[:, 0, :], 0.0), [], ['PW'])
                o1(TT(PWr[:, 1, :], mag, cs, ALU.mult), ['mag', 'cs'], ['PW'])
                o1(TT(PWi[:, 1, :], mag, sn, ALU.mult), ['mag', 'sn'], ['PW'])
                xr, den, cr, ci = f(16), f(16), f(16), f(16)
                o1(TS1(xr, PWr[:, 1, :], -1.0, ALU.add), ['PW'], ['xr'])
                o1(TT(den, LRe, LRe, ALU.mult), ['LRe'], ['den'])
                o1(TT(t1, LIm, LIm, ALU.mult), ['LIm'], ['t1'])
                o1(TT(den, den, t1, ALU.add), ['den', 't1'], ['den'])
                o1(RCP(den, den), ['den'], ['den'])
                o1(TT(cr, xr, LRe, ALU.mult), ['xr', 'LRe'], ['cr'])
                o1(TT(t1, PWi[:, 1, :], LIm, ALU.mult), ['PW', 'LIm'], ['t1'])
                o1(TT(cr, cr, t1, ALU.add), ['cr', 't1'], ['cr'])
                o1(TT(cr, cr, den, ALU.mult), ['cr', 'den'], ['cr'])
                o1(TT(ci, PWi[:, 1, :], LRe, ALU.mult), ['PW', 'LRe'], ['ci'])
                o1(TT(t1, xr, LIm, ALU.mult), ['xr', 'LIm'], ['t1'])
                o1(TT(ci, ci, t1, ALU.subtract), ['ci', 't1'], ['ci'])
                o1(TT(ci, ci, den, ALU.mult), ['ci', 'den'], ['ci'])
                for n in range(1, 8):
                    o1(TT(PWr[:, n + 1, :], PWr[:, n, :], PWr[:, 1, :], ALU.mult), ['PW'], ['PW'])
                    o1(TT(t1, PWi[:, n, :], PWi[:, 1, :], ALU.mult), ['PW'], ['t1'])
                    o1(TT(PWr[:, n + 1, :], PWr[:, n + 1, :], t1, ALU.subtract), ['PW', 't1'], ['PW'])
                    o1(TT(PWi[:, n + 1, :], PWr[:, n, :], PWi[:, 1, :], ALU.mult), ['PW'], ['PW'])
                    o1(TT(t1, PWi[:, n, :], PWr[:, 1, :], ALU.mult), ['PW'], ['t1'])
                    o1(TT(PWi[:, n + 1, :], PWi[:, n + 1, :], t1, ALU.add), ['PW', 't1'], ['PW'])
                Qr, Qi, Qn = v3(f(9 * 16), 9), v3(f(9 * 16), 9), v3(f(9 * 16), 9)
                o1(CP(Qr[:, 0, :], PWr[:, 8, :]), ['PW'], ['Q'])
                o1(CP(Qi[:, 0, :], PWi[:, 8, :]), ['PW'], ['Q'])
                for k in range(8):
                    o1(TT(Qr[:, k + 1, :], Qr[:, k, :], Qr[:, k, :], ALU.mult), ['Q'], ['Q'])
                    o1(TT(t1, Qi[:, k, :], Qi[:, k, :], ALU.mult), ['Q'], ['t1'])
                    o1(TT(Qr[:, k + 1, :], Qr[:, k + 1, :], t1, ALU.subtract), ['Q', 't1'], ['Q'])
                    o1(TT(t1, Qr[:, k, :], Qi[:, k, :], ALU.mult), ['Q'], ['t1'])
                    o1(TS1(Qi[:, k + 1, :], t1, 2.0, ALU.mult), ['t1'], ['Q'])
                o1(TS1(Qn, Qi, -1.0, ALU.mult), ['Q'], ['Q'])
                KSc = v5(f(2 * 2 * 4 * 9 * 3), 2, 2, 4)
                for hf in range(2):
                    for j, Qx in enumerate((Qr, Qi, Qn)):
                        src = Qx.rearrange("p k (d hf gpl) -> p d hf gpl k", d=2, hf=2)[:, :, hf, :, :]
                        dst = KSc[:, hf, :, :, :].rearrange("p d gpl (k j) -> p d gpl k j", j=3)[:, :, :, :, j]
                        o1(CP(dst, src), ['Q'], ['KSc'])
                    S.dma('sp', s5k_s.ap()[l, hf], KSc[:, hf, :, :, :].rearrange("p d gpl x -> p (d gpl x)"),
                          reads=['KSc'], writes=['s5k_s%d' % l])
                Bbr, Bbi, tB = v3(f(256), 16), v3(f(256), 16), v3(f(256), 16)

                def bc16(v):
                    return v.unsqueeze(2).broadcast_to([128, 16, 16])
                o1(TT(Bbr, BR, bc16(cr), ALU.mult), ['BR', 'cr'], ['Bbr'])
                o1(TT(tB, BI, bc16(ci), ALU.mult), ['BI', 'ci'], ['tB'])
                o1(TT(Bbr, Bbr, tB, ALU.subtract), ['Bbr', 'tB'], ['Bbr'])
                o1(TT(Bbi, BI, bc16(cr), ALU.mult), ['BI', 'cr'], ['Bbi'])
                o1(TT(tB, BR, bc16(ci), ALU.mult), ['BR', 'ci'], ['tB'])
                o1(TT(Bbi, Bbi, tB, ALU.add), ['Bbi', 'tB'], ['Bbi'])
                EBr, EBi = v3(f(2048), 16), v3(f(2048), 16)
                EWr, EWi = v3(f(2048), 16), v3(f(2048), 16)
                for (Ex, nm) in ((EBr, 'EBr'), (EBi, 'EBi'), (EWr, 'EWr'), (EWi, 'EWi')):
                    S.op('dve', MS(Ex, 0.0), writes=[nm])

                def expand(Ex, val, nm, vnm):
                    ev = Ex.rearrange("p (d hf gpl) c -> p d hf gpl c", d=2, hf=2)
                    vv = val.rearrange("p (d hf gpl) j -> p d hf gpl j", d=2, hf=2)
                    for two in range(2):
                        ps_ = slice(64 * two, 64 * two + 64)
                        for gpl in range(4):
                            for d_ in range(2):
                                o1(CP(ev[ps_, d_, :, gpl, 32 * gpl + 16 * two:32 * gpl + 16 * two + 16], vv[ps_, d_, :, gpl, :]),
                                   [vnm], [nm])
                expand(EBr, Bbr, 'EBr', 'Bbr')
                expand(EBi, Bbi, 'EBi', 'Bbi')
                WUP = [v4(AR.bf16(4096), 2, 2) .rearrange("p d part (s c) -> p d part s c", s=8) for _ in range(2)]
                WDN = [AR.bf16(4096) for _ in range(2)]
                KM = [v3(AR.bf16(2048), 2).rearrange("p d (t c) -> p d t c", t=8) for _ in range(2)]
                for hf in range(2):
                    S.op('dve', MS(WDN[hf], 0.0), writes=['WDN%d' % hf])
                WDNv = [w.rearrange("p (d gpl part n c) -> p d gpl part n c", d=2, gpl=4, part=2, n=8) for w in WDN]
                Wre, Wim, BPr, BPi, tW = (v3(f(256), 16) for _ in range(5))
                WUT = v3(f(8 * 128), 8)
                for (nm,) in (('WUT',),):
                    S.op('dve', MS(WUT, 0.0), writes=[nm])
                for n in range(9):
                    prn, pin = bc16(PWr[:, n, :]), bc16(PWi[:, n, :])
                    o1(TT(Wre, CR, prn, ALU.mult), ['CR', 'PW'], ['Wre'])
                    o1(TT(tW, CI, pin, ALU.mult), ['CI', 'PW'], ['tW'])
                    o1(TT(Wre, Wre, tW, ALU.subtract), ['Wre', 'tW'], ['Wre'])
                    o1(TT(Wim, CR, pin, ALU.mult), ['CR', 'PW'], ['Wim'])
                    o1(TT(tW, CI, prn, ALU.mult), ['CI', 'PW'], ['tW'])
                    o1(TT(Wim, Wim, tW, ALU.add), ['Wim', 'tW'], ['Wim'])
                    o1(TS1(Wim, Wim, -1.0, ALU.mult), ['Wim'], ['Wim'])
                    if n >= 1:
                        for part, Wx, wnm in ((0, Wre, 'Wre'), (1, Wim, 'Wim')):
                            wv = Wx.rearrange("p (d hf gpl) i -> p d hf gpl i", d=2, hf=2)
                            for hf in range(2):
                                for two in range(2):
                                    ps_ = slice(64 * two, 64 * two + 64)
                                    o1(CP(WDNv[hf][ps_, :, :, part, n - 1, 16 * two:16 * two + 16], wv[ps_, :, hf, :, :]),
                                       [wnm], ['WDN%d' % hf])
                    if n <= 7:
                        expand(EWr, Wre, 'EWr', 'Wre')
                        expand(EWi, Wim, 'EWi', 'Wim')
                        o1(TT(BPr, Bbr, prn, ALU.mult), ['Bbr', 'PW'], ['BPr'])
                        o1(TT(tW, Bbi, pin, ALU.mult), ['Bbi', 'PW'], ['tW'])
                        o1(TT(BPr, BPr, tW, ALU.subtract), ['BPr', 'tW'], ['BPr'])
                        o1(TT(BPi, Bbr, pin, ALU.mult), ['Bbr', 'PW'], ['BPi'])
                        o1(TT(tW, Bbi, prn, ALU.mult), ['Bbi', 'PW'], ['tW'])
                        o1(TT(BPi, BPi, tW, ALU.add), ['BPi', 'tW'], ['BPi'])
                        wut = WUT.rearrange("p (d hf part) (gpl c) -> p d hf part gpl c", d=2, hf=2, gpl=4)
                        for part, Bx, bnm in ((0, BPr, 'BPr'), (1, BPi, 'BPi')):
                            bv = Bx.rearrange("p (d hf gpl) j -> p d hf gpl j", d=2, hf=2)
                            for two in range(2):
                                ps_ = slice(64 * two, 64 * two + 64)
                                for d_ in range(2):
                                    o1(CP(wut[ps_, d_, :, part, :, 16 * two:16 * two + 16], bv[ps_, d_, :, :, :]),
                                       [bnm], ['WUT'])
                        for d_ in range(2):
                            s_idx = (7 - n) if d_ == 0 else n
                            for hf in range(2):
                                p0, k0 = psum()
                                for part in range(2):
                                    idx = (d_ * 2 + hf) * 2 + part
                                    S.op('pe', TR(p0[:, part * 128:(part + 1) * 128], WUT[:, idx, :], identF),
                                         reads=['WUT', 'identF'], writes=[k0])
                                S.op('act', ACT(WUP[hf][:, d_, :, s_idx, :], v3(p0[:, 0:256], 2), AF.Copy),
                                     reads=[k0], writes=['WUP%d' % hf])
                        for d_ in range(2):
                            for hf in range(2):
                                p0, k0 = psum()
                                cnt = 0
                                for gpl in range(4):
                                    cidx = d_ * 8 + hf * 4 + gpl
                                    for (Eb, Ew, bn, wn) in ((EBr, EWr, 'EBr', 'EWr'), (EBi, EWi, 'EBi', 'EWi')):
                                        S.op('pe', MM(p0[:, 0:128], Eb[:, cidx, :], Ew[:, cidx, :], start=(cnt == 0), stop=(cnt == 7)),
                                             reads=[bn, wn], writes=[k0], inc=(cnt == 7))
                                        cnt += 1
                                S.op('act', ACT(KM[hf][:, d_, n, :], p0[:, 0:128], AF.Copy), reads=[k0], writes=['KM%d' % hf])
                for hf in range(2):
                    dst = s5m_s.ap()[l, hf]
                    S.dma('sp', dst[:, 0:4096], WUP[hf].rearrange("p d part s c -> p (d part s c)"),
                          reads=['WUP%d' % hf], writes=['s5m_s%d' % l])
                    S.dma('sp', dst[:, 4096:8192], WDN[hf], reads=['WDN%d' % hf], writes=['s5m_s%d' % l])
                    S.dma('sp', dst[:, 8192:10240], KM[hf].rearrange("p d t c -> p (d t c)"),
                          reads=['KM%d' % hf], writes=['s5m_s%d' % l])
                S.barrier(skip_pool=True)
                AR.top = m0

            for l in layers:
                s5_precompute(l)
            assert AR.top == smark
            AR.top = gmark
            ckpt('s5pre')

            ring = {}

            def load_w(name, slots, shape_ap_fn, src_ap, skey):
                st = ring[name]
                i = st['n'] % len(st['slots'])
                st['n'] += 1
                ap = st['slots'][i]
                key = '%s_%d' % (name, i)
                S.dma('sp', ap, src_ap, reads=[skey], writes=[key])
                return ap, key

            def mk_ring(name, nslots, nelem_bf16, shaper):
                ring[name] = {'n': 0, 'slots': [shaper(AR.bf16(nelem_bf16)) for _ in range(nslots)]}

            def load_seq(s):
                m0 = AR.top
                stg = [AR.f32(D) for _ in range(2)]
                for tt in range(T // 128):
                    sl = tt % 2
                    if tt < 2:
                        src = ctx_t.ap()[s, tt * 128:(tt + 1) * 128, :]
                    else:
                        src = x_t.ap()[s, (tt - 2) * 128:(tt - 1) * 128, :]
                    S.dma('sp', stg[sl], src, writes=['stg%d' % sl])
                    for g4 in range(2):
                        p0, k0 = psum()
                        for j in range(4):
                            kt = g4 * 4 + j
                            S.op('pe', TR(p0[:, j * 128:(j + 1) * 128], stg[sl][:, kt * 128:(kt + 1) * 128], identF),
                                 reads=['stg%d' % sl, 'identF'], writes=[k0])
                        wk = []
                        for j in range(4):
                            wk += hk(g4 * 4 + j, tt * 128, tt * 128 + 128)
                        eng = 'act' if (g4 == 0) else 'dve'
                        if eng == 'act':
                            S.op('act', ACT(h[:, g4 * 4:g4 * 4 + 4, tt * 128:(tt + 1) * 128], v3(p0[:, :], 4), AF.Copy),
                                 reads=[k0], writes=wk)
                        else:
                            S.op('dve', CP(h[:, g4 * 4:g4 * 4 + 4, tt * 128:(tt + 1) * 128], v3(p0[:, :], 4)), reads=[k0], writes=wk)
                S.barrier()
                AR.top = m0

            def store_seq(s):
                m0 = AR.top
                stg = [AR.f32(D) for _ in range(2)]
                tts = list(range(2, T // 128)) + (list(range(2)) if out_hc else [])
                for n_, tt in enumerate(tts):
                    sl = n_ % 2
                    for g4 in range(2):
                        p0, k0 = psum()
                        rk = []
                        for j in range(4):
                            kt = g4 * 4 + j
                            S.op('pe', TR(p0[:, j * 128:(j + 1) * 128], h[:, kt, tt * 128:(tt + 1) * 128], identF),
                                 reads=hk(kt, tt * 128, tt * 128 + 128) + ['identF'], writes=[k0])
                        if g4 == 0:
                            S.op('act', ACT(stg[sl][:, 0:512], p0[:, :], AF.Copy), reads=[k0], writes=['ostg%d' % sl])
                        else:
                            S.op('dve', CP(stg[sl][:, 512:1024], p0[:, :]), reads=[k0], writes=['ostg%d' % sl])
                    if tt >= 2:
                        dst = out_t.ap()[s, (tt - 2) * 128:(tt - 1) * 128, :]
                    else:
                        dst = hc_t.ap()[s, tt * 128:(tt + 1) * 128, :]
                    S.dma('sp', dst, stg[sl], reads=['ostg%d' % sl], writes=['outd'], key='ostg%d' % sl)
                S.barrier()
                AR.top = m0

            def rms_stats(src_fn, src_keys_fn, n, tmp_sq, eng_sq='pool'):
                pss, kss = psum()
                for kt in range(KT):
                    sq, sqk = tmp_sq[kt % 2]
                    S.op(eng_sq, TT(sq[:, 0:n], src_fn(kt), src_fn(kt), ALU.mult), reads=src_keys_fn(kt), writes=[sqk])
                    S.op('pe', MM(pss[:, 0:n], ones_bf, sq[:, 0:n], start=(kt == 0), stop=(kt == KT - 1)),
                         reads=[sqk, 'ones_bf'], writes=[kss])
                return pss, kss

            def rstd_from(pss, kss, n, tmp, tmpk, rstd, rstdk):
                S.op('act', ACT(tmp[:, 0:n], pss[:, 0:n], AF.Sqrt, bias=EPS, scale=1.0 / D), reads=[kss], writes=[tmpk])
                S.op('dve', RCP(rstd[:, 0:n], tmp[:, 0:n]), reads=[tmpk], writes=[rstdk])

            def mk_scratch():
                sc = {'sq': [(AR.bf16(512), 'scq%d' % i) for i in range(2)],
                      'f': [(AR.f32(512), 'scf%d' % i) for i in range(5)]}
                return sc

            def norm_to_A(i_scale, i_shift, sc):
                sqs = sc['sq']
                tmp, tmpk = sc['f'][2]
                rstds = [sc['f'][3], sc['f'][4]]
                tts = [sc['f'][0], sc['f'][1]]
                for bi, (c0, c1, w) in enumerate(BLOCKS):
                    n = c1 - c0
                    pss, kss = rms_stats(lambda kt: h[:, kt, c0:c1], lambda kt: hk(kt, c0, c1), n, sqs)
                    rstd, rk = rstds[bi % 2]
                    rstd_from(pss, kss, n, tmp, tmpk, rstd, rk)
                    for kt in range(KT):
                        t_, tkk = tts[kt % 2]
                        S.op('dve', TT(t_[:, 0:n], h[:, kt, c0:c1], rstd[:, 0:n], ALU.mult), reads=hk(kt, c0, c1) + [rk], writes=[tkk])
                        S.op('act', ACT(A[:, kt, c0:c1], t_[:, 0:n], AF.Identity, bias=dv[:, w, i_shift, kt:kt + 1],
                                        scale=dv[:, w, i_scale, kt:kt + 1]), reads=[tkk, 'dv'], writes=ak(kt, c0, c1))

            def proj_mm(wt, wkey, c0, c1):
                pp, pk = psum()
                n = c1 - c0
                for kt in range(KT):
                    S.op('pe', MM(pp[:, 0:n], wt[:, kt, :], A[:, kt, c0:c1], start=(kt == 0), stop=(kt == KT - 1)),
                         reads=[wkey] + ak(kt, c0, c1), writes=[pk], inc=(kt == KT - 1))
                return pp, pk

            def win_tile(l, o):
                return load_w('win', None, None, win_s.ap()[l, o], 'win_s%d' % l)

            def layer(l, s):
                lm0 = AR.top
                for w, r in ((0, s), (1, nseq)):
                    M = lambda m: mods[:, l, m, :, r]
                    S.op('dve', STT(dv[:, w, 0, :], M(1), 1.0, NG(l, 0), ALU.add, ALU.mult), reads=['mods', 'vecs', 'dv'], writes=['dv'])
                    S.op('dve', CP(dv[:, w, 1, :], M(0)), reads=['mods', 'dv'], writes=['dv'])
                    S.op('dve', TT(dv[:, w, 2, :], M(2), NG(l, 1), ALU.mult), reads=['mods', 'vecs', 'dv'], writes=['dv'])
                    S.op('dve', STT(dv[:, w, 3, :], M(4), 1.0, NG(l, 2), ALU.add, ALU.mult), reads=['mods', 'vecs', 'dv'], writes=['dv'])
                    S.op('dve', CP(dv[:, w, 4, :], M(3)), reads=['mods', 'dv'], writes=['dv'])
                    S.op('dve', TT(dv[:, w, 5, :], M(5), NG(l, 3), ALU.mult), reads=['mods', 'vecs', 'dv'], writes=['dv'])
                sso = v3(AR.bf16(2 * T), 2)
                pm0 = AR.top
                sc = mk_scratch()
                norm_to_A(0, 1, sc)
                dump('a', A[:, :, :], [128, KT, T], sum([ak(kt, 0, T) for kt in range(KT)], []))
                ckpt('norm')
                mk_ring('win', 2, KT * 128, lambda a: v3(a, KT))
                u = v3(AR.bf16(2 * T), 2)
                for hf in range(2):
                    wt, wk = win_tile(l, O_U + hf)
                    for (c0, c1, w) in BLOCKS:
                        pp, pk = proj_mm(wt, wk, c0, c1)
                        S.op('act', ACT(u[:, hf, c0:c1], pp[:, 0:c1 - c0], AF.Copy), reads=[pk], writes=tk('u%d' % hf, c0, c1))
                dump('u', u[:, :, :], [128, 2, T], tk('u0', 0, T) + tk('u1', 0, T))
                ckpt('uproj')
                mats = AR.bf16(10240)
                ksc = AR.f32(216)
                P0 = [v3(AR.f32(2 * 290), 2) for _ in range(4)]
                P1 = [v3(AR.f32(2 * 290), 2) for _ in range(4)]
                Xb = [[v3(AR.bf16(2 * 290), 2) for _ in range(4)] for _ in range(2)]
                KT1 = [v3(AR.f32(2 * 290), 2) for _ in range(1)]
                KT2 = [v3(AR.f32(2 * 290), 2) for _ in range(1)]
                yv = [sc['f'][0], sc['f'][1]]
                (g1, g1k), (g2, g2k), (g3, g3k) = sc['f'][2], sc['f'][3], sc['f'][4]
                WUPv = mats[:, 0:4096].rearrange("p (d part s c) -> p d part s c", d=2, part=2, s=8)
                WDNv = mats[:, 4096:8192].rearrange("p (d gpl part n c) -> p d gpl part n c", d=2, gpl=4, part=2, n=8)
                KMv = mats[:, 8192:10240].rearrange("p (d t c) -> p d t c", d=2, t=8)
                kscv = ksc.rearrange("p (d gpl k j) -> p d gpl k j", d=2, gpl=4, k=9)
                NCH = T // 8
                for hf in range(2):
                    S.dma('sp', mats, s5m_s.ap()[l, hf], reads=['s5m_s%d' % l], writes=['mats'])
                    S.dma('sp', ksc, s5k_s.ap()[l, hf], reads=['s5k_s%d' % l], writes=['ksc'])
                    for d_ in range(2):
                        chains = []
                        for gpl in range(4):
                            chain = []
                            xk = 'X%d' % gpl
                            xbk = 'Xb%d_%d' % (d_, gpl)
                            ke = 'dve'
                            S.op(ke, MS(Xb[d_][gpl][:, :, 0:1] if d_ == 0 else Xb[d_][gpl][:, :, 32:33], 0.0), writes=[xbk])
                            for part in range(2):
                                pp, pk = psum()
                                for s_ in range(8):
                                    S.op('pe', MM(pp[:, 0:NCH], WUPv[32 * gpl:32 * gpl + 32, d_, part, s_, :],
                                                  u[32 * gpl:32 * gpl + 32, hf, s_:T:8], start=(s_ == 0), stop=(s_ == 7),
                                                  tp=(32 * gpl, 0)),
                                         reads=['mats'] + tk('u%d' % hf, 0, T), writes=[pk], inc=(s_ == 7))
                                if d_ == 0:
                                    S.op('act', ACT(P0[gpl][:, part, 1:289], pp[:, 0:288], AF.Copy), reads=[pk], writes=[xk])
                                else:
                                    S.op('act', ACT(P0[gpl][:, part, 0:32], pp[:, 0:32], AF.Copy), reads=[pk], writes=[xk])
                                    S.op('act', ACT(P0[gpl][:, part, 33:289], pp[:, 32:288], AF.Copy), reads=[pk], writes=[xk])

                            def ks_run(lo, W, nlev, right, final_bf, src, dst):
                                for k in range(nlev):
                                    sh = 1 << k
                                    n = W - sh
                                    last = (k == nlev - 1)
                                    a_, b_, nb_ = kscv[:, d_, gpl, k, 0:1], kscv[:, d_, gpl, k, 1:2], kscv[:, d_, gpl, k, 2:3]
                                    Dd = Xb[d_][gpl] if (last and final_bf) else dst
                                    wkeys = [xk] + ([xbk] if (last and final_bf) else [])
                                    if right:
                                        so_, do_ = slice(lo, lo + n), slice(lo + sh, lo + W)
                                        ho_ = slice(lo, lo + sh)
                                    else:
                                        so_, do_ = slice(lo + sh, lo + W), slice(lo, lo + n)
                                        ho_ = slice(lo + n, lo + W)
                                    if gpl < 3:
                                        chain.append(('dve', STT(dst[:, :, do_], src[:, :, so_], a_, src[:, :, do_], ALU.mult, ALU.add), [xk, 'ksc'], wkeys))
                                        chain.append(('dve', STT(Dd[:, 0, do_], src[:, 1, so_], nb_, dst[:, 0, do_], ALU.mult, ALU.add), [xk, 'ksc'], wkeys))
                                        chain.append(('dve', STT(Dd[:, 1, do_], src[:, 0, so_], b_, dst[:, 1, do_], ALU.mult, ALU.add), [xk, 'ksc'], wkeys))
                                        chain.append(('dve', CP(Dd[:, :, ho_], src[:, :, ho_]), [xk], wkeys))
                                    else:
                                        t1_, t1k = KT1[0], 'kt1_%d' % gpl
                                        t2_, t2k = KT2[0], 'kt2_%d' % gpl
                                        chain.append(('act', ACT(t1_[:, :, 0:n], src[:, :, so_], AF.Identity, scale=a_), [xk, 'ksc'], [t1k]))
                                        chain.append(('act', ACT(t2_[:, 0, 0:n], src[:, 1, so_], AF.Identity, scale=nb_), [xk, 'ksc'], [t2k]))
                                        chain.append(('act', ACT(t2_[:, 1, 0:n], src[:, 0, so_], AF.Identity, scale=b_), [xk, 'ksc'], [t2k]))
                                        chain.append(('pool', TT(dst[:, :, do_], t1_[:, :, 0:n], src[:, :, do_], ALU.add), [xk, t1k], wkeys))
                                        chain.append(('pool', TT(Dd[:, :, do_], t2_[:, :, 0:n], dst[:, :, do_], ALU.add), [xk, t2k], wkeys))
                                        chain.append(('pool', CP(Dd[:, :, ho_], src[:, :, ho_]), [xk], wkeys))
                                    src, dst = dst, src
                                return src
                            if d_ == 0:
                                ks_run(1, 288, 9, True, True, P0[gpl], P1[gpl])
                            else:
                                res = ks_run(0, 32, 5, False, False, P0[gpl], P1[gpl])
                                ce_ = 'dve' if gpl < 3 else 'pool'
                                chain.append((ce_, CP(Xb[d_][gpl][:, :, 0:32], res[:, :, 0:32]), [xk], [xk, xbk]))
                                chain.append((ce_, CP(P0[gpl][:, :, 289:290], res[:, :, 0:1]), [xk], [xk]))
                                ks_run(33, 257, 9, False, True, P0[gpl], P1[gpl])
                            chains.append(chain)
                        for i_ in range(max(len(c) for c in chains)):
                            for c in chains:
                                if i_ < len(c):
                                    S.op(c[i_][0], c[i_][1], reads=c[i_][2], writes=c[i_][3])
                    for bi, (c0, c1, w) in enumerate(BLOCKS):
                        n = c1 - c0
                        nch = n // 8
                        ch0 = c0 // 8
                        py, pyk = psum()
                        for s_ in range(8):
                            ops = []
                            for d_ in range(2):
                                srange = range(0, s_ + 1) if d_ == 0 else range(s_, 8)
                                for sp_ in srange:
                                    ops.append((py[:, s_:n:8], KMv[:, d_, abs(s_ - sp_), :], u[:, hf, c0 + sp_:c1:8], None,
                                                ['mats'] + tk('u%d' % hf, c0, c1), False))
                                nidx = s_ if d_ == 0 else 7 - s_
                                e0 = ch0 if d_ == 0 else (ch0 + 1 if w else ch0 + 2)
                                for gpl in range(4):
                                    for part in range(2):
                                        ops.append((py[32 * gpl:32 * gpl + 32, s_:n:8], WDNv[:, d_, gpl, part, nidx, :],
                                                    Xb[d_][gpl][:, part, e0:e0 + nch], (0, 32 * gpl),
                                                    ['mats', 'Xb%d_%d' % (d_, gpl)], (d_ == 1 and part == 1)))
                            for i_, (o_, l_, r_, tp_, rd_, st_) in enumerate(ops):
                                S.op('pe', MM(o_, l_, r_, start=(i_ == 0), stop=st_, tp=tp_),
                                     reads=rd_, writes=[pyk], inc=(i_ == len(ops) - 1))
                        y_, yk = yv[bi % 2]
                        S.op('dve', STT(y_[:, 0:n], u[:, hf, c0:c1], SD(l, hf), py[:, 0:n], ALU.mult, ALU.add),
                             reads=[pyk, 'vecs'] + tk('u%d' % hf, c0, c1), writes=[yk])
                        S.op('pool', TT(g1[:, 0:n], y_[:, 0:n], y_[:, 0:n], ALU.mult), reads=[yk], writes=[g1k])
                        S.op('pool', TS2(g1[:, 0:n], g1[:, 0:n], 0.044715, 1.0, ALU.mult, ALU.add), reads=[g1k], writes=[g1k])
                        S.op('pool', TT(g2[:, 0:n], g1[:, 0:n], y_[:, 0:n], ALU.mult), reads=[g1k, yk], writes=[g2k])
                        S.op('act', ACT(g3[:, 0:n], g2[:, 0:n], AF.Sigmoid, scale=1.5957691216057308), reads=[g2k], writes=[g3k])
                        S.op('dve', TT(sso[:, hf, c0:c1], y_[:, 0:n], g3[:, 0:n], ALU.mult), reads=[yk, g3k],
                             writes=tk('sso%d' % hf, c0, c1))
                dump('g', sso[:, :, :], [128, 2, T], tk('sso0', 0, T) + tk('sso1', 0, T))
                wg = v3(AR.bf16(512), 2)
                S.dma('sp', wg, wglu_s.ap()[l], reads=['wglu_s%d' % l], writes=['wg'])
                for bi, (c0, c1, w) in enumerate(BLOCKS):
                    n = c1 - c0
                    zs = []
                    for ho in range(2):
                        pz, pzk = psum()
                        for hf in range(2):
                            S.op('pe', MM(pz[:, 0:n], wg[:, hf, ho * 128:(ho + 1) * 128], sso[:, hf, c0:c1], start=(hf == 0), stop=(hf == 1)),
                                 reads=['wg'] + tk('sso%d' % hf, c0, c1), writes=[pzk], inc=(hf == 1))
                        zs.append((pz, pzk))
                    for ho in range(2):
                        pz, pzk = zs[ho]
                        gt, gk = (g1, g1k) if ho == 0 else (g2, g2k)
                        S.op('act', ACT(gt[:, 0:n], pz[:, 0:n], AF.Sigmoid, bias=BG(l, ho)), reads=[pzk, 'vecs'], writes=[gk])
                        S.op('pool', TT(sso[:, ho, c0:c1], sso[:, ho, c0:c1], gt[:, 0:n], ALU.mult),
                             reads=[gk] + tk('sso%d' % ho, c0, c1), writes=tk('sso%d' % ho, c0, c1))
                dump('ssm', sso[:, :, :], [128, 2, T], tk('sso0', 0, T) + tk('sso1', 0, T))
                ckpt('s5')
                S.barrier()
                AR.top = pm0
                cvo = v3(AR.bf16(2 * T), 2)
                pm1 = AR.top
                mk_ring('win', 3, KT * 128, lambda a: v3(a, KT))
                CW_ = T + 4
                ccx = v3(AR.bf16(2 * CW_), 2)
                cb = v3(AR.bf16(2 * T), 2)
                cct = [AR.bf16(512) for _ in range(2)]
                o1t = [AR.f32(512) for _ in range(2)]
                for col in (0, 257, 258, CW_ - 1):
                    S.op('pool', MS(ccx[:, :, col:col + 1], 0.0), writes=['ccxpad'])

                def coff(c0):
                    return c0 + 1 if c0 < LC else c0 + 3
                for hf in range(2):
                    wcb, kcb = win_tile(l, O_CB + hf)
                    wcc, kcc = win_tile(l, O_CC + hf)
                    wcx, kcx = win_tile(l, O_CX + hf)
                    for bi, (c0, c1, w) in enumerate(BLOCKS):
                        n = c1 - c0
                        pp, pk = proj_mm(wcb, kcb, c0, c1)
                        S.op('act', ACT(cb[:, hf, c0:c1], pp[:, 0:n], AF.Copy), reads=[pk], writes=tk('cb%d' % hf, c0, c1))
                        pp, pk = proj_mm(wcc, kcc, c0, c1)
                        ct_, ctk = cct[bi % 2], 'cct%d' % (bi % 2)
                        S.op('act', ACT(ct_[:, 0:n], pp[:, 0:n], AF.Copy), reads=[pk], writes=[ctk])
                        pp, pk = proj_mm(wcx, kcx, c0, c1)
                        S.op('dve', TT(ccx[:, hf, coff(c0):coff(c0) + n], pp[:, 0:n], ct_[:, 0:n], ALU.mult), reads=[pk, ctk],
                             writes=['ccx%d' % hf])
                for hf in range(2):
                    for bi, (c0, c1, w) in enumerate(BLOCKS):
                        n = c1 - c0
                        b0 = coff(c0)
                        ot, otk = o1t[bi % 2], 'o1t%d' % (bi % 2)
                        S.op('pool', TS1(ot[:, 0:n], ccx[:, hf, b0 - 1:b0 - 1 + n], CW(l, 0, hf), ALU.mult),
                             reads=['ccx%d' % hf, 'ccxpad', 'vecs'], writes=[otk])
                        S.op('dve', STT(ot[:, 0:n], ccx[:, hf, b0:b0 + n], CW(l, 1, hf), ot[:, 0:n], ALU.mult, ALU.add),
                             reads=['ccx%d' % hf, 'vecs', otk], writes=[otk])
                        S.op('dve', STT(ot[:, 0:n], ccx[:, hf, b0 + 1:b0 + 1 + n], CW(l, 2, hf), ot[:, 0:n], ALU.mult, ALU.add),
                             reads=['ccx%d' % hf, 'ccxpad', 'vecs', otk], writes=[otk])
                        S.op('dve', TT(cvo[:, hf, c0:c1], ot[:, 0:n], cb[:, hf, c0:c1], ALU.mult),
                             reads=[otk] + tk('cb%d' % hf, c0, c1), writes=tk('cvo%d' % hf, c0, c1))
                dump('conv', cvo[:, :, :], [128, 2, T], tk('cvo0', 0, T) + tk('cvo1', 0, T))
                ckpt('conv')
                S.barrier()
                AR.top = pm1
                q = v3(AR.bf16(4 * T), 4)
                kd = v3(AR.bf16(2 * T), 2)
                Vt = v4(AR.bf16(18 * 2 * 128), 18, 2)
                pm2 = AR.top
                mk_ring('win', 4, KT * 128, lambda a: v3(a, KT))
                ropec = AR.f32(L)
                ropes = AR.f32(L)
                S.dma('sp', ropec, ropec_t.ap(), writes=['ropec'])
                S.dma('sp', ropes, ropes_t.ap(), writes=['ropes'])
                rt = [AR.f32(512) for _ in range(2)]
                for (dst, dnm, o_pl, o_sw, cnt) in ((q, 'q', O_Q, O_QS, 4), (kd, 'kd', O_K, O_KS, 2)):
                    for i in range(cnt):
                        wp, kp = win_tile(l, o_pl + i)
                        wsw, ksw = win_tile(l, o_sw + i)
                        for (c0, c1, w) in BLOCKS:
                            n = c1 - c0
                            pp, pk = proj_mm(wp, kp, c0, c1)
                            wkeys = tk('%s%d' % (dnm, i), c0, c1)
                            if w:
                                S.op('act', ACT(dst[:, i, c0:c1], pp[:, 0:n], AF.Copy), reads=[pk], writes=wkeys)
                            else:
                                p2, pk2 = proj_mm(wsw, ksw, c0, c1)
                                lc0 = c0 - LC
                                S.op('dve', TT(rt[0][:, 0:n], pp[:, 0:n], ropec[:, lc0:lc0 + n], ALU.mult), reads=[pk, 'ropec'], writes=['rt0'])
                                S.op('dve', TT(rt[1][:, 0:n], p2[:, 0:n], ropes[:, lc0:lc0 + n], ALU.mult), reads=[pk2, 'ropes'], writes=['rt1'])
                                S.op('pool', TT(dst[:, i, c0:c1], rt[0][:, 0:n], rt[1][:, 0:n], ALU.add), reads=['rt0', 'rt1'], writes=wkeys)
                S.op('pool', MS(Vt[:, :, :, 64:128], 1.0), writes=['Vones'])
                wv, kv = win_tile(l, O_V)
                for t4 in range(0, 18, 4):
                    nt = min(4, 18 - t4)
                    pp, pk = psum()
                    for j in range(nt):
                        tt = t4 + j
                        for kt in range(KT):
                            S.op('pe', MM(pp[:, j * 128:(j + 1) * 128], A[:, kt, tt * 128:(tt + 1) * 128], wv[:, kt, :],
                                          start=(kt == 0), stop=(kt == KT - 1)),
                                 reads=[kv] + ak(kt, tt * 128, tt * 128 + 128), writes=[pk], inc=(kt == KT - 1))
                    S.op('act', ACT(Vt[:, t4:t4 + nt, :, 0:64], pp[:, 0:nt * 128].rearrange("p (t g c) -> p t g c", t=nt, g=2), AF.Copy),
                         reads=[pk], writes=['V%d' % (t4 + j) for j in range(nt)])
                dump('q', q[:, :, :], [128, 4, T], sum([tk('q%d' % i, 0, T) for i in range(4)], []))
                dump('kd', kd[:, :, :], [128, 2, T], tk('kd0', 0, T) + tk('kd1', 0, T))
                ckpt('qkv')
                S.barrier()
                AR.top = pm2
                Pt = [v3(AR.bf16(5 * 512), 5) for _ in range(2)]
                rc = AR.f32(512)
                nblk = [0]

                def attend(qc0, key_tiles, l_):
                    for g in range(2):
                        P_ = Pt[nblk[0] % 2]
                        pkey = 'Pt%d' % (nblk[0] % 2)
                        nblk[0] += 1
                        nk = len(key_tiles)
                        for half in range(2):
                            rows = slice(64 * half, 64 * half + 64)
                            for kp in range(0, nk, 2):
                                nkk = min(2, nk - kp)
                                ps_, psk = psum()
                                for j in range(nkk):
                                    kc0 = key_tiles[kp + j][0] * 128
                                    for a_ in range(2):
                                        hh = 2 * a_ + half
                                        hd = 4 * g + hh
                                        qi = hd // 2
                                        S.op('pe', MM(ps_[:, (2 * j + a_) * 128:(2 * j + a_ + 1) * 128], kd[rows, g, kc0:kc0 + 128],
                                                      q[rows, qi, qc0:qc0 + 128], start=True, stop=True, tp=(64 * half, 0)),
                                             reads=tk('kd%d' % g, kc0, kc0 + 128) + tk('q%d' % qi, qc0, qc0 + 128), writes=[psk],
                                             inc=(j == nkk - 1 and a_ == 1))
                                S.op('act', ACT(P_[:, kp:kp + nkk, half * 256:(half + 1) * 256], v3(ps_[:, 0:nkk * 256], nkk), AF.Exp, scale=0.125),
                                     reads=[psk], writes=[pkey + '_%d' % (kp + j) for j in range(nkk)])
                        for ki_, (ktile, msk) in enumerate(key_tiles):
                            if msk is not None:
                                mk_, mkk = (maskp, 'maskp') if msk == 'p' else (maskn, 'maskn')
                                S.op('pool', TT(v3(P_[:, ki_, :], 4), v3(P_[:, ki_, :], 4),
                                                mk_.unsqueeze(1).broadcast_to([128, 4, 128]), ALU.mult),
                                     reads=[pkey + '_%d' % ki_, mkk], writes=[pkey + '_%d' % ki_])
                        if ATT_LEVEL < 1:
                            continue
                        po, pok = psum()
                        for ki_, (ktile, msk) in enumerate(key_tiles):
                            S.op('pe', MM(po[:, :], Vt[:, ktile, g, :], P_[:, ki_, :], start=(ki_ == 0), stop=(ki_ == nk - 1)),
                                 reads=['V%d' % ktile, 'Vones', pkey + '_%d' % ki_], writes=[pok], inc=(ki_ == nk - 1))
                        if ATT_LEVEL < 2:
                            continue
                        S.op('dve', TT(v3(rc[0:64, :], 4), v3(po[64:128, :], 4),
                                       sinkexp[64:128, l_ * 8 + 4 * g:l_ * 8 + 4 * g + 4].unsqueeze(2).broadcast_to([64, 4, 128]), ALU.add),
                             reads=[pok, 'sinkexp'], writes=['rc'])
                        S.op('dve', RCP(rc[0:64, :], rc[0:64, :]), reads=['rc'], writes=['rc'])
                        if ATT_LEVEL < 3:
                            continue
                        for half in range(2):
                            S.op('dve', TT(A[64 * half:64 * half + 64, 2 * g:2 * g + 2, qc0:qc0 + 128],
                                           v3(po[0:64, half * 256:(half + 1) * 256], 2), v3(rc[0:64, half * 256:(half + 1) * 256], 2), ALU.mult),
                                 reads=[pok, 'rc'], writes=ak(2 * g, qc0, qc0 + 128) + ak(2 * g + 1, qc0, qc0 + 128))
                for qb in range(ATT_NQB):
                    kts = [(0, None), (1, None)]
                    if qb > 0:
                        kts.append((2 + qb - 1, 'p'))
                    kts.append((2 + qb, None))
                    if qb < 15:
                        kts.append((2 + qb + 1, 'n'))
                    attend(LC + qb * 128, kts, l)
                if l < DEPTH - 1:
                    for qb in range(2):
                        attend(qb * 128, [(0, None), (1, None)], l)
                dump('attn', A[:, 0:4, :], [128, 4, T], sum([ak(kt, 0, T) for kt in range(4)], []))
                ckpt('attn')
                S.barrier()
                AR.top = pm1
                wo = [v3(AR.bf16(KT * 128), KT) for _ in range(KT)]
                for m in range(KT):
                    S.dma('sp', wo[m], wout_s.ap()[l, m], reads=['wout_s%d' % l], writes=['wo%d' % m])
                mblk = v3(AR.f32(KT * 512), KT)
                sqs = [(AR.bf16(512), 'osq%d' % i) for i in range(2)]
                tmp = AR.f32(512)
                rstd = AR.f32(512)
                tts = [AR.f32(512) for _ in range(2)]

                def mixk(kt, c0, c1):
                    if kt < 4:
                        return A[:, kt, c0:c1], ak(kt, c0, c1)
                    if kt < 6:
                        return cvo[:, kt - 4, c0:c1], tk('cvo%d' % (kt - 4), c0, c1)
                    return sso[:, kt - 6, c0:c1], tk('sso%d' % (kt - 6), c0, c1)
                for bi, (c0, c1, w) in enumerate(BLOCKS):
                    n = c1 - c0
                    for m in range(KT):
                        pp, pk = psum()
                        for kt in range(KT):
                            rap, rkeys = mixk(kt, c0, c1)
                            S.op('pe', MM(pp[:, 0:n], wo[m][:, kt, :], rap, start=(kt == 0), stop=(kt == KT - 1)),
                                 reads=['wo%d' % m] + rkeys, writes=[pk], inc=(kt == KT - 1))
                        S.op('act', ACT(mblk[:, m, 0:n], pp[:, 0:n], AF.Copy), reads=[pk], writes=['mblk%d' % m])
                    pss, kss = rms_stats(lambda kt: mblk[:, kt, 0:n], lambda kt: ['mblk%d' % kt], n, sqs)
                    rstd_from(pss, kss, n, tmp, 'otmp', rstd, 'orstd')
                    for m in range(KT):
                        t_, tkk = tts[m % 2], 'ott%d' % (m % 2)
                        S.op('pool' if m % 4 == 3 else 'dve', TT(t_[:, 0:n], mblk[:, m, 0:n], rstd[:, 0:n], ALU.mult), reads=['mblk%d' % m, 'orstd'], writes=[tkk])
                        S.op('dve', STT(h[:, m, c0:c1], t_[:, 0:n], dv[:, w, 2, m:m + 1], h[:, m, c0:c1], ALU.mult, ALU.add),
                             reads=[tkk, 'dv'] + hk(m, c0, c1), writes=hk(m, c0, c1))
                dump('h1', h[:, :, :], [128, KT, T], sum([hk(kt, 0, T) for kt in range(KT)], []))
                ckpt('oproj')
                S.barrier()
                AR.top = lm0
                sc = mk_scratch()
                norm_to_A(3, 4, sc)
                NB = 576
                HB = 288
                hid = v3(AR.bf16(32 * NB), 32)
                mk_ring('w1', 3, KT * 128, lambda a: v3(a, KT))
                mk_ring('w2', 2, 32 * 128, lambda a: v3(a, 32))
                fblk = v3(AR.f32(KT * NB), KT)
                sqs = sc['sq']
                tmp, tmpk = sc['f'][2]
                rstds = [sc['f'][3], sc['f'][4]]
                tts = [sc['f'][0], sc['f'][1]]
                for b4 in range(T // NB):
                    bc0 = b4 * NB
                    for j in range(32):
                        wt, wk = load_w('w1', None, None, w1_s.ap()[l, j], 'w1_s%d' % l)
                        for hh in range(2):
                            c0 = bc0 + hh * HB
                            pp, pk = proj_mm(wt, wk, c0, c0 + HB)
                            hkey = 'hid%d_%d' % (j, hh)
                            S.op('act', ACT(hid[:, j, hh * HB:(hh + 1) * HB], pp[:, 0:HB], AF.Relu), reads=[pk], writes=[hkey])
                            S.op('pool', TT(hid[:, j, hh * HB:(hh + 1) * HB], hid[:, j, hh * HB:(hh + 1) * HB],
                                            hid[:, j, hh * HB:(hh + 1) * HB], ALU.mult), reads=[hkey], writes=[hkey])
                    for m in range(KT):
                        wt2, wk2 = load_w('w2', None, None, w2_s.ap()[l, m], 'w2_s%d' % l)
                        for hh in range(2):
                            pp, pk = psum()
                            for j in range(32):
                                S.op('pe', MM(pp[:, 0:HB], wt2[:, j, :], hid[:, j, hh * HB:(hh + 1) * HB], start=(j == 0), stop=(j == 31)),
                                     reads=[wk2, 'hid%d_%d' % (j, hh)], writes=[pk], inc=(j == 31))
                            S.op('act', ACT(fblk[:, m, hh * HB:(hh + 1) * HB], pp[:, 0:HB], AF.Copy), reads=[pk], writes=['fblk%d_%d' % (m, hh)])
                    for hh in range(2):
                        c0 = bc0 + hh * HB
                        pss, kss = rms_stats(lambda kt: fblk[:, kt, hh * HB:(hh + 1) * HB], lambda kt: ['fblk%d_%d' % (kt, hh)], HB, sqs)
                        rstd, rk = rstds[hh]
                        rstd_from(pss, kss, HB, tmp, tmpk, rstd, rk)
                        segs = []
                        if c0 < LC:
                            e = min(LC, c0 + HB)
                            segs.append((c0, e, 1))
                            if e < c0 + HB:
                                segs.append((e, c0 + HB, 0))
                        else:
                            segs.append((c0, c0 + HB, 0))
                        for m in range(KT):
                            t_, tkk = tts[m % 2]
                            S.op('pool' if m % 4 == 3 else 'dve', TT(t_[:, 0:HB], fblk[:, m, hh * HB:(hh + 1) * HB], rstd[:, 0:HB], ALU.mult),
                                 reads=['fblk%d_%d' % (m, hh), rk], writes=[tkk])
                            for (a0, a1, w) in segs:
                                S.op('dve', STT(h[:, m, a0:a1], t_[:, a0 - c0:a1 - c0], dv[:, w, 5, m:m + 1], h[:, m, a0:a1], ALU.mult, ALU.add),
                                     reads=[tkk, 'dv'] + hk(m, a0, a1), writes=hk(m, a0, a1))
                S.barrier()
                AR.top = lm0

            for s in range(nseq):
                load_seq(s)
                ckpt('load')
                for li_, l in enumerate(layers):
                    layer(l, s)
                store_seq(s)

        except _Stop:
            pass
        S.emit()
    return nc, dbg_t


def _consts():
    f32 = np.float32
    n = np.arange(L)
    row = (n // 64).astype(f32)
    col = (n % 64).astype(f32)
    freqs = (np.float32(10000.0) ** (-np.arange(16, dtype=f32) / np.float32(16))).astype(f32)
    cosT = np.zeros((128, L), f32)
    sinT = np.zeros((128, L), f32)
    for p in range(128):
        dd = p % 64
        i = dd % 16
        pos = row if dd < 32 else col
        ang = (pos * freqs[i]).astype(f32)
        cosT[p] = np.cos(ang).astype(f32)
        sgn = -1.0 if (dd % 32) < 16 else 1.0
        sinT[p] = (sgn * np.sin(ang)).astype(f32)
    ii = np.arange(128)[:, None]
    jj = np.arange(128)[None, :]
    maskp = (ii >= jj).astype(f32)
    maskn = (ii <= jj).astype(f32)
    return dict(ropec=cosT, ropes=sinT, maskp=maskp, maskn=maskn, ident=np.eye(128, dtype=f32))


_WKEYS = ['w_ada', 'b_ada', 'norm_g', 'w_in', 'conv_w', 'attn_sink', 'ssm_lam_re', 'ssm_lam_im', 'ssm_log_dt',
          'ssm_b_re', 'ssm_b_im', 'ssm_c_re', 'ssm_c_im', 'ssm_d', 'w_glu', 'b_glu', 'w_out', 'w_mlp_in', 'w_mlp_out']

LAUNCH_PLAN = [[0, 1, 2, 3]]


def kernel(**inputs):
    f32 = np.float32
    inp = {k: np.ascontiguousarray(np.asarray(v, dtype=f32)) for k, v in inputs.items()}
    consts = _consts()
    hx = inp['x']
    hctx = inp['ctx']
    B = hx.shape[0]
    per = B // NCORES
    for li, layers in enumerate(LAUNCH_PLAN):
        last = (li == len(LAUNCH_PLAN) - 1)
        nc, _ = build_program(layers, nseq=per, out_hc=not last)
        in_maps = []
        for c in range(NCORES):
            sl = slice(c * per, (c + 1) * per)
            m = {'x': np.ascontiguousarray(hx[sl]), 'ctx': np.ascontiguousarray(hctx[sl]),
                 'cc': np.ascontiguousarray(np.concatenate([inp['c'][sl], inp['c_ctx'][None, :]], axis=0))}
            for k in _WKEYS:
                m[k] = inp[k]
            m.update(consts)
            in_maps.append(m)
        res = run_bass_kernel_spmd(nc, in_maps, core_ids=list(range(NCORES)))
        hx = np.concatenate([r['out'] for r in res.results], axis=0)
        if not last:
            hctx = np.concatenate([r['hc_out'] for r in res.results], axis=0)
    return hx.astype(f32)
```

```python
import os
import numpy as np
from contextlib import ExitStack
import concourse.bass as bass
import concourse.mybir as mybir
from concourse.bass_utils import run_bass_kernel_spmd

F32 = mybir.dt.float32
BF16 = mybir.dt.bfloat16
I32 = mybir.dt.int32
AF = mybir.ActivationFunctionType
ALU = mybir.AluOpType
ENGS = ('pe', 'act', 'dve', 'pool', 'sp')

D = 1024
KT = 8
L = 2048
LC = 256
T = L + LC
DEPTH = 4
NSEQ = 4
NCORES = 8
EPS = 1e-6
ARENA_F32 = 53000
PI = float(np.pi)

O_Q, O_QS, O_K, O_KS, O_CB, O_CC, O_CX, O_U, O_V = 0, 4, 8, 10, 12, 14, 16, 18, 20
N_WIN = 21
BLOCKS = [(0, 256, 1)] + [(256 + 512 * i, 256 + 512 * (i + 1), 0) for i in range(4)]


class Sched:
    LIMIT = 16000

    def __init__(self, nc, es):
        self.nc, self.es = nc, es
        self.q = {e: [] for e in ENGS}
        self.cur = {}
        self.waited = {e: {} for e in ENGS}
        self.lastw = {}
        self.rds = {}
        self.dsem = {}
        self.dcnt = {}
        self.nsem = 0
        self.semobj = {}
        self.pending = {}

    def newsem(self):
        self.nsem += 1
        s = self.es.enter_context(self.nc.semaphore("s%d" % self.nsem))
        self.semobj[self.nsem] = s
        return self.nsem

    def _deps(self, eng, reads, writes):
        deps = []
        for b in reads:
            ev = self.lastw.get(b)
            if ev is not None:
                deps.append(ev)
        for b in writes:
            ev = self.lastw.get(b)
            if ev is not None:
                deps.append(ev)
            deps.extend(self.rds.get(b, ()))
        waits = []
        w = self.waited[eng]
        for (sem, val, src) in deps:
            if src == eng and eng == 'pe':
                continue
            if w.get(sem, 0) >= val:
                continue
            w[sem] = val
            waits.append((sem, val))
        return waits

    def _commit(self, ev, reads, writes):
        for b in reads:
            self.rds.setdefault(b, []).append(ev)
        for b in writes:
            self.lastw[b] = ev
            self.rds[b] = []

    def op(self, eng, fn, reads=(), writes=(), inc=True):
        waits = self._deps(eng, reads, writes)
        sem, c = self.cur.get(eng, (None, 0))
        if sem is None or (c >= self.LIMIT and not self.pending.get(eng, False)):
            sem, c = self.newsem(), 0
        self.pending[eng] = not inc
        if inc:
            c += 1
            self.cur[eng] = (sem, c)
            ev = (sem, c, eng)
        else:
            self.cur[eng] = (sem, c)
            ev = (sem, c + 1, eng)
        self.q[eng].append((waits, fn, sem, 1 if inc else 0))
        self._commit(ev, reads, writes)

    def dma(self, eng, out, in_, reads=(), writes=(), key=None, slow=False):
        key = key if key is not None else (writes[0] if writes else reads[0])
        waits = self._deps(eng, reads, writes)
        if key not in self.dsem:
            self.dsem[key] = self.newsem()
            self.dcnt[key] = 0
        self.dcnt[key] += 16
        sem = self.dsem[key]
        ev = (sem, self.dcnt[key], 'dma')
        if slow:
            fn = lambda e: e.dma_start(out=out, in_=in_, allow_slow_non_contiguous=True)
        else:
            fn = lambda e: e.dma_start(out=out, in_=in_)
        self.q[eng].append((waits, fn, sem, 16))
        self._commit(ev, reads, writes)

    def barrier(self, skip_pool=False):
        evs = []
        for eng, (sem, c) in self.cur.items():
            if skip_pool and eng == 'pool':
                continue
            if c > 0:
                evs.append((sem, c))
        for key, sem in self.dsem.items():
            if skip_pool and isinstance(key, str) and key.startswith(('win_s', 'wout_s', 'w1_s', 'w2_s', 'wglu_s')):
                continue
            evs.append((sem, self.dcnt[key]))
        for eng in ENGS:
            if skip_pool and eng == 'pool':
                continue
            w = self.waited[eng]
            waits = []
            for (sem, val) in evs:
                if w.get(sem, 0) >= val:
                    continue
                w[sem] = val
                waits.append((sem, val))
            if waits:
                self.q[eng].append((waits, None, None, 0))

    def emit(self):
        self.barrier()
        so = self.semobj
        with self.nc.Block() as block:
            def run(name):
                def f(e):
                    for (waits, fn, sem, inc) in self.q[name]:
                        for (s, v) in waits:
                            e.wait_ge(so[s], v)
                        if fn is not None:
                            ins = fn(e)
                            if inc:
                                ins.then_inc(so[sem], inc)
                return f
            block.tensor(run('pe'))
            block.scalar(run('act'))
            block.vector(run('dve'))
            block.gpsimd(run('pool'))
            block.sync(run('sp'))


def MM(out, lhsT, rhs, start=True, stop=True, tp=None):
    if tp is None:
        return lambda e: e.matmul(out, lhsT, rhs, start=start, stop=stop)
    return lambda e: e.matmul(out, lhsT, rhs, start=start, stop=stop, tile_position=tp)


def TR(out, in_, ident):
    return lambda e: e.transpose(out, in_, ident)


def ACT(out, in_, func, bias=None, scale=None):
    kw = {}
    if bias is not None:
        kw['bias'] = bias
    if scale is not None:
        kw['scale'] = scale
    return lambda e: e.activation(out, in_, func, **kw)


def TT(out, a, b, op):
    return lambda e: e.tensor_tensor(out, a, b, op)


def TS2(out, a, s1, s2, op0, op1):
    return lambda e: e.tensor_scalar(out, a, s1, s2, op0, op1)


def TS1(out, a, s, op):
    return lambda e: e.tensor_single_scalar(out, a, s, op)


def STT(out, a, s, b, op0, op1):
    return lambda e: e.scalar_tensor_tensor(out, a, s, b, op0, op1)


def CP(out, a):
    return lambda e: e.tensor_copy(out, a)


def MS(out, v):
    return lambda e: e.memset(out, v)


def RCP(out, a):
    return lambda e: e.reciprocal(out, a)


class _Stop(Exception):
    pass


STAGE = os.environ.get('MK_STAGE', '')
ATT_LEVEL = int(os.environ.get('MK_ATT', '3'))
ATT_NQB = int(os.environ.get('MK_NQB', '16'))


def ckpt(name):
    if STAGE == name:
        raise _Stop()


class Arena:
    def __init__(self, ap, n):
        self.ap, self.n, self.top = ap, n, 0

    def f32(self, n):
        off = self.top
        self.top += n
        assert self.top <= self.n, ("arena overflow", self.top)
        return self.ap[:, off:off + n]

    def bf16(self, n):
        nf = (n + 1) // 2
        v = self.f32(nf).bitcast(BF16)
        return v[:, 0:n]

    def i32(self, n):
        return self.f32(n).bitcast(I32)


def v3(ap, a):
    return ap.rearrange("p (a b) -> p a b", a=a)


def v4(ap, a, b):
    return ap.rearrange("p (a b c) -> p a b c", a=a, b=b)


def v5(ap, a, b, c):
    return ap.rearrange("p (a b c d) -> p a b c d", a=a, b=b, c=c)


def tk(name, c0, c1):
    return [("%s_%d" % (name, i)) for i in range(c0 // 128, (c1 - 1) // 128 + 1)]


def build_program(layers, nseq=NSEQ, out_hc=False, dbg=None):
    nc = bass.Bass("TRN2", target_bir_lowering=False)
    es = ExitStack()
    dbg = dbg or []
    NL = len(layers)

    def din(name, shape, dt=F32):
        return nc.dram_tensor(name, list(shape), dt, kind="ExternalInput")

    x_t = din("x", [nseq, L, D])
    ctx_t = din("ctx", [nseq, LC, D])
    cc_t = din("cc", [nseq + 1, D])
    wada_t = din("w_ada", [DEPTH, D, 6 * D])
    bada_t = din("b_ada", [DEPTH, 6 * D])
    ng_t = din("norm_g", [DEPTH, 4, D])
    win_t = din("w_in", [DEPTH, D, 1792])
    convw_t = din("conv_w", [DEPTH, 3, 256])
    sink_t = din("attn_sink", [DEPTH, 8])
    lre_t = din("ssm_lam_re", [DEPTH, 2, 16, 64])
    lim_t = din("ssm_lam_im", [DEPTH, 2, 16, 64])
    ldt_t = din("ssm_log_dt", [DEPTH, 2, 16])
    bre_t = din("ssm_b_re", [DEPTH, 2, 16, 64, 16])
    bim_t = din("ssm_b_im", [DEPTH, 2, 16, 64, 16])
    cre_t = din("ssm_c_re", [DEPTH, 2, 16, 16, 64])
    cim_t = din("ssm_c_im", [DEPTH, 2, 16, 16, 64])
    sd_t = din("ssm_d", [DEPTH, 256])
    wglu_t = din("w_glu", [DEPTH, 256, 256])
    bglu_t = din("b_glu", [DEPTH, 256])
    wout_t = din("w_out", [DEPTH, D, D])
    w1_t = din("w_mlp_in", [DEPTH, D, 4 * D])
    w2_t = din("w_mlp_out", [DEPTH, 4 * D, D])
    ropec_t = din("ropec", [128, L])
    ropes_t = din("ropes", [128, L])
    maskp_t = din("maskp", [128, 128])
    maskn_t = din("maskn", [128, 128])
    ident_t = din("ident", [128, 128])
    out_t = nc.dram_tensor("out", [nseq, L, D], F32, kind="ExternalOutput")
    hc_t = nc.dram_tensor("hc_out", [nseq, LC, D], F32, kind="ExternalOutput") if out_hc else None
    win_s = nc.dram_tensor("win_s", [DEPTH, N_WIN, 128, KT, 128], BF16)
    wout_s = nc.dram_tensor("wout_s", [DEPTH, KT, 128, KT, 128], BF16)
    w1_s = nc.dram_tensor("w1_s", [DEPTH, 32, 128, KT, 128], BF16)
    w2_s = nc.dram_tensor("w2_s", [DEPTH, KT, 128, 32, 128], BF16)
    wglu_s = nc.dram_tensor("wglu_s", [DEPTH, 128, 2, 256], BF16)
    s5m_s = nc.dram_tensor("s5m_s", [DEPTH, 2, 128, 10240], BF16)
    s5k_s = nc.dram_tensor("s5k_s", [DEPTH, 2, 128, 216], F32)
    dbg_t = {}

    def dap(t, offset, ap):
        return bass.AP(tensor=t, offset=offset, ap=[list(a) for a in ap])

    with es:
        S = Sched(nc, es)
        arena_t = es.enter_context(nc.sbuf_tensor("arena", [128, ARENA_F32], F32))
        AR = Arena(arena_t[:, :], ARENA_F32)
        PS = [es.enter_context(nc.psum_tensor("ps%d" % i, [128, 512], F32)) for i in range(8)]
        psn = [0]

        def psum():
            i = psn[0] % 8
            psn[0] += 1
            return PS[i], "ps%d" % i

        def dump(name, ap, shape, key):
            if name not in dbg:
                return
            t = nc.dram_tensor("dbg_" + name, list(shape), ap.dtype, kind="ExternalOutput")
            dbg_t[name] = t
            S.dma('sp', t.ap(), ap, reads=key, writes=["dbg_" + name])

        identF = AR.f32(128)
        ones_bf = AR.bf16(128)
        maskp = AR.bf16(128)
        maskn = AR.bf16(128)
        vecs = AR.f32(256)
        mods = v5(AR.f32(DEPTH * 6 * KT * 5), DEPTH, 6, KT)
        sinkexp = AR.f32(32)
        dv = v4(AR.f32(2 * 6 * KT), 2, 6)
        smark = AR.top
        h = v3(AR.f32(KT * T), KT)
        A = v3(AR.bf16(KT * T), KT)
        gmark = AR.top
        AR.top = smark

        def hk(kt, c0, c1):
            return tk("h%d" % kt, c0, c1)

        def ak(kt, c0, c1):
            return tk("A%d" % kt, c0, c1)

        try:
            S.dma('sp', identF, ident_t.ap(), writes=['identF'])
            S.dma('pool', maskp, maskp_t.ap(), writes=['maskp'])
            S.dma('pool', maskn, maskn_t.ap(), writes=['maskn'])
            S.op('pool', MS(ones_bf, 1.0), writes=['ones_bf'])
            for sl_ in range(4):
                hh_ = (0, 2, 1, 3)[sl_]
                S.dma('sp', sinkexp[:, sl_:32:4], dap(sink_t, hh_, [[0, 128], [4, 8]]), writes=['sinkexp'], slow=True)
            S.op('act', ACT(sinkexp, sinkexp, AF.Exp), reads=['sinkexp'], writes=['sinkexp'])
            ckpt('c1')

            def wcast(dst_ap, src_ap, key):
                S.dma('pool', dst_ap, src_ap, writes=[key])

            def wcast_layer(l):
                wi = win_t.ap()
                ws = win_s.ap()

                def colsrc(c0):
                    return wi[l, :, c0:c0 + 128].rearrange("(kt p) c -> p kt c", p=128)

                def sw_cast(dst_tile, dcol0, c0, nblk):
                    for b_ in range(nblk):
                        for half in range(2):
                            sc = c0 + 32 * b_ + 16 * (1 - half)
                            dc = dcol0 + 32 * b_ + 16 * half
                            src = wi[l, :, sc:sc + 16].rearrange("(kt p) c -> p kt c", p=128)
                            wcast(dst_tile[:, :, dc:dc + 16], src, key)
                key = 'win_s%d' % l
                for i in range(4):
                    wcast(ws[l, O_Q + i], colsrc(128 * i), key)
                    sw_cast(ws[l, O_QS + i], 0, 128 * i, 4)
                for g in range(2):
                    c0 = 512 + 64 * g
                    for dup in range(2):
                        src = wi[l, :, c0:c0 + 64].rearrange("(kt p) c -> p kt c", p=128)
                        wcast(ws[l, O_K + g][:, :, dup * 64:(dup + 1) * 64], src, key)
                        sw_cast(ws[l, O_KS + g], dup * 64, c0, 2)
                for j, (o, c0) in enumerate([(O_CB, 768), (O_CB + 1, 896), (O_CC, 1024), (O_CC + 1, 1152),
                                             (O_CX, 1280), (O_CX + 1, 1408), (O_U, 1536), (O_U + 1, 1664), (O_V, 640)]):
                    wcast(ws[l, o], colsrc(c0), key)
                for m in range(KT):
                    wcast(wout_s.ap()[l, m], wout_t.ap()[l, :, m * 128:(m + 1) * 128].rearrange("(kt p) c -> p kt c", p=128),
                          'wout_s%d' % l)
                for j in range(32):
                    wcast(w1_s.ap()[l, j], w1_t.ap()[l, :, j * 128:(j + 1) * 128].rearrange("(kt p) c -> p kt c", p=128),
                          'w1_s%d' % l)
                for m in range(KT):
                    for jh in range(2):
                        wcast(w2_s.ap()[l, m][:, jh * 16:(jh + 1) * 16, :],
                              w2_t.ap()[l, jh * 2048:(jh + 1) * 2048, m * 128:(m + 1) * 128].rearrange("(j p) c -> p j c", p=128),
                              'w2_s%d' % l)
                wcast(wglu_s.ap()[l], wglu_t.ap()[l].rearrange("(kt p) c -> p kt c", p=128), 'wglu_s%d' % l)


            for l_ in layers:
                wcast_layer(l_)
            pm = AR.top
            st1 = AR.f32(128)
            st2 = AR.f32(128)
            S.op('dve', MS(st2, 0.0), writes=['st2'])
            S.dma('sp', st1, ng_t.ap().rearrange("l k (kt p) -> (l k kt) p", p=128), writes=['st1'])
            S.dma('sp', st2[0:24, :], convw_t.ap().rearrange("l k (hf p) -> (l k hf) p", p=128), writes=['st2'])
            S.dma('sp', st2[32:40, :], sd_t.ap().rearrange("l (hf p) -> (l hf) p", p=128), writes=['st2'])
            S.dma('sp', st2[64:72, :], bglu_t.ap().rearrange("l (hf p) -> (l hf) p", p=128), writes=['st2'])
            p0, k0 = psum()
            S.op('pe', TR(p0[:, 0:128], st1, identF), reads=['st1', 'identF'], writes=[k0])
            S.op('pe', TR(p0[:, 128:256], st2, identF), reads=['st2', 'identF'], writes=[k0])
            S.op('dve', CP(vecs, p0[:, 0:256]), reads=[k0], writes=['vecs'])
            ckpt('c2')

            def NG(l, k):
                return vecs[:, (l * 4 + k) * 8:(l * 4 + k) * 8 + 8]

            def CW(l, k, hf):
                c = 128 + (l * 3 + k) * 2 + hf
                return vecs[:, c:c + 1]

            def SD(l, hf):
                c = 160 + l * 2 + hf
                return vecs[:, c:c + 1]

            def BG(l, hf):
                c = 192 + l * 2 + hf
                return vecs[:, c:c + 1]

            R = nseq + 1
            ccs = AR.f32(D)
            cactT = v3(AR.f32(KT * R), KT)
            S.op('dve', MS(ccs, 0.0), writes=['ccs'])
            S.dma('sp', ccs[0:R, :], cc_t.ap(), writes=['ccs'])
            S.op('act', ACT(ccs, ccs, AF.Silu), reads=['ccs'], writes=['ccs'])
            for g4 in range(2):
                p0, k0 = psum()
                for j in range(4):
                    kt = g4 * 4 + j
                    S.op('pe', TR(p0[:, j * 128:(j + 1) * 128], ccs[:, kt * 128:(kt + 1) * 128], identF),
                         reads=['ccs', 'identF'], writes=[k0])
                S.op('dve', CP(cactT[:, g4 * 4:g4 * 4 + 4, :], v3(p0[:, :], 4)[:, :, 0:R]), reads=[k0], writes=['cactT'])
            ckpt('c3')
            badaT = AR.f32(DEPTH * 48)
            stb = AR.f32(128)
            S.op('dve', MS(stb, 0.0), writes=['stb'])
            for hb in range(2):
                S.dma('sp', stb[0:96, :], dap(bada_t, hb * 96 * 128, [[128, 96], [1, 128]]), writes=['stb'])
                p0, k0 = psum()
                S.op('pe', TR(p0[:, 0:128], stb, identF), reads=['stb', 'identF'], writes=[k0])
                S.op('dve', CP(badaT[:, hb * 96:(hb + 1) * 96], p0[:, 0:96]), reads=[k0], writes=['badaT'])
            wr = [v3(AR.f32(KT * 512), KT) for _ in range(2)]
            ckpt('c4')
            nld = 0
            for l in layers:
                pmod, kmod = psum()
                for cb in range(12):
                    slot = nld % 2
                    nld += 1
                    S.dma('sp', wr[slot], wada_t.ap()[l, :, cb * 512:(cb + 1) * 512].rearrange("(kt p) c -> p kt c", p=128),
                          writes=['wr%d' % slot])
                    for ct in range(4):
                        ctg = cb * 4 + ct
                        for kt in range(KT):
                            S.op('pe', MM(pmod[:, ctg * R:(ctg + 1) * R], wr[slot][:, kt, ct * 128:(ct + 1) * 128],
                                          cactT[:, kt, :], start=(kt == 0), stop=(kt == KT - 1)),
                                 reads=['wr%d' % slot, 'cactT'], writes=[kmod], inc=(kt == KT - 1))
                for m in range(6):
                    S.op('dve', TT(mods[:, l, m, :, 0:R], v3(pmod[:, m * 8 * R:(m + 1) * 8 * R], 8),
                                   badaT[:, l * 48 + m * 8:l * 48 + m * 8 + 8].unsqueeze(2).broadcast_to([128, 8, R]), ALU.add),
                         reads=[kmod, 'badaT'], writes=['mods'])
            S.barrier(skip_pool=True)
            AR.top = pm
            ckpt('mods')

            ckpt('wcast')
            def s5_precompute(l):
                m0 = AR.top
                f = AR.f32
                LRe, LIm, DT = f(16), f(16), f(16)
                BR, BI = v3(f(256), 16), v3(f(256), 16)
                CR, CI = v3(f(256), 16), v3(f(256), 16)
                for two in range(2):
                    ps_ = slice(64 * two, 64 * two + 64)
                    S.dma('sp', v3(LRe[ps_, :], 2), dap(lre_t, l * 2048 + two * 64, [[1, 64], [1024, 2], [128, 8]]),
                          writes=['LRe'], slow=True)
                    S.dma('sp', v3(LIm[ps_, :], 2), dap(lim_t, l * 2048 + two * 64, [[1, 64], [1024, 2], [128, 8]]),
                          writes=['LIm'], slow=True)
                    S.dma('sp', v3(DT[ps_, :], 2), dap(ldt_t, l * 32 + two, [[0, 64], [16, 2], [2, 8]]),
                          writes=['DT'], slow=True)
                    S.dma('sp', v4(BR[ps_, :, :].rearrange("p a b -> p (a b)"), 2, 8),
                          dap(bre_t, l * 32768 + two * 1024, [[16, 64], [16384, 2], [2048, 8], [1, 16]]), writes=['BR'])
                    S.dma('sp', v4(BI[ps_, :, :].rearrange("p a b -> p (a b)"), 2, 8),
                          dap(bim_t, l * 32768 + two * 1024, [[16, 64], [16384, 2], [2048, 8], [1, 16]]), writes=['BI'])
                CT = f(512)
                cst = f(128)
                S.op('dve', MS(cst, 0.0), writes=['cst'])
                for (src_t, dstC, nm) in ((cre_t, CR, 'CR'), (cim_t, CI, 'CI')):
                    for tq in range(4):
                        S.dma('sp', cst[:, 0:64], dap(src_t, l * 32768 + tq * 8192, [[64, 128], [1, 64]]), writes=['cst'])
                        p0, k0 = psum()
                        S.op('pe', TR(p0[:, 0:128], cst, identF), reads=['cst', 'identF'], writes=[k0])
                        S.op('dve', CP(CT[0:64, tq * 128:(tq + 1) * 128], p0[0:64, 0:128]), reads=[k0], writes=['CT'])
                    ctv = CT[0:64, :].rearrange("p (d gp two i) -> p d gp two i", d=2, gp=8, two=2)
                    for d_ in range(2):
                        S.op('dve', CP(dstC[0:64, d_ * 8:(d_ + 1) * 8, :], ctv[:, d_, :, 0, :]), reads=['CT'], writes=[nm])
                        S.op('dve', CP(dstC[64:128, d_ * 8:(d_ + 1) * 8, :], ctv[:, d_, :, 1, :]), reads=['CT'], writes=[nm])
                E = 'dve'

                def o1(fn, rd, wr_):
                    S.op(E, fn, reads=rd, writes=wr_)
                ar, ai, mag, sn, cs = f(16), f(16), f(16), f(16), f(16)
                t1, t2 = f(16), f(16)
                ki = AR.i32(16)
                S.op('act', ACT(DT, DT, AF.Exp), reads=['DT'], writes=['DT'])
                o1(TT(ar, LRe, DT, ALU.mult), ['LRe', 'DT'], ['ar'])
                o1(TT(ai, LIm, DT, ALU.mult), ['LIm', 'DT'], ['ai'])
                S.op('act', ACT(mag, ar, AF.Exp), reads=['ar'], writes=['mag'])
                for (dst, shift, nm) in ((sn, 0.0, 'sn'), (cs, PI / 2, 'cs')):
                    o1(TS1(t1, ai, shift, ALU.add), ['ai'], ['t1'])
                    o1(TS1(t2, t1, 1.0 / (2 * PI), ALU.mult), ['t1'], ['t2'])
                    o1(CP(ki, t2), ['t2'], ['ki'])
                    o1(CP(t2, ki), ['ki'], ['t2'])
                    o1(STT(t1, t2, -2 * PI, t1, ALU.mult, ALU.add), ['t1', 't2'], ['t1'])
                    o1(TS1(t2, t1, PI, ALU.is_gt), ['t1'], ['t2'])
                    o1(STT(t1, t2, -2 * PI, t1, ALU.mult, ALU.add), ['t1', 't2'], ['t1'])
                    o1(TS1(t2, t1, -PI, ALU.is_lt), ['t1'], ['t2'])
                    o1(STT(t1, t2, 2 * PI, t1, ALU.mult, ALU.add), ['t1', 't2'], ['t1'])
                    S.op('act', ACT(dst, t1, AF.Sin), reads=['t1'], writes=[nm])
                PWr, PWi = v3(f(9 * 16), 9), v3(f(9 * 16), 9)
                o1(MS(PWr[:, 0, :], 1.0), [], ['PW'])
                o1(MS(PWi[:, 0, :], 0.0), [], ['PW'])
                o1(TT(PWr[:, 1, :], mag, cs, ALU.mult), ['mag', 'cs'], ['PW'])
                o1(TT(PWi[:, 1, :], mag, sn, ALU.mult), ['mag', 'sn'], ['PW'])
                xr, den, cr, ci = f(16), f(16), f(16), f(16)
                o1(TS1(xr, PWr[:, 1, :], -1.0, ALU.add), ['PW'], ['xr'])
                o1(TT(den, LRe, LRe, ALU.mult), ['LRe'], ['den'])
                o1(TT(t1, LIm, LIm, ALU.mult), ['LIm'], ['t1'])
                o1(TT(den, den, t1, ALU.add), ['den', 't1'], ['den'])
                o1(RCP(den, den), ['den'], ['den'])
                o1(TT(cr, xr, LRe, ALU.mult), ['xr', 'LRe'], ['cr'])
                o1(TT(t1, PWi[:, 1, :], LIm, ALU.mult), ['PW', 'LIm'], ['t1'])
                o1(TT(cr, cr, t1, ALU.add), ['cr', 't1'], ['cr'])
                o1(TT(cr, cr, den, ALU.mult), ['cr', 'den'], ['cr'])
                o1(TT(ci, PWi[:, 1, :], LRe, ALU.mult), ['PW', 'LRe'], ['ci'])
                o1(TT(t1, xr, LIm, ALU.mult), ['xr', 'LIm'], ['t1'])
                o1(TT(ci, ci, t1, ALU.subtract), ['ci', 't1'], ['ci'])
                o1(TT(ci, ci, den, ALU.mult), ['ci', 'den'], ['ci'])
                for n in range(1, 8):
                    o1(TT(PWr[:, n + 1, :], PWr[:, n, :], PWr[:, 1, :], ALU.mult), ['PW'], ['PW'])
                    o1(TT(t1, PWi[:, n, :], PWi[:, 1, :], ALU.mult), ['PW'], ['t1'])
                    o1(TT(PWr[:, n + 1, :], PWr[:, n + 1, :], t1, ALU.subtract), ['PW', 't1'], ['PW'])
                    o1(TT(PWi[:, n + 1, :], PWr[:, n, :], PWi[:, 1, :], ALU.mult), ['PW'], ['PW'])
                    o1(TT(t1, PWi[:, n, :], PWr[:, 1, :], ALU.mult), ['PW'], ['t1'])
                    o1(TT(PWi[:, n + 1, :], PWi[:, n + 1, :], t1, ALU.add), ['PW', 't1'], ['PW'])
                Qr, Qi, Qn = v3(f(9 * 16), 9), v3(f(9 * 16), 9), v3(f(9 * 16), 9)
                o1(CP(Qr[:, 0, :], PWr[:, 8, :]), ['PW'], ['Q'])
                o1(CP(Qi[:, 0, :], PWi[:, 8, :]), ['PW'], ['Q'])
                for k in range(8):
                    o1(TT(Qr[:, k + 1, :], Qr[:, k, :], Qr[:, k, :], ALU.mult), ['Q'], ['Q'])
                    o1(TT(t1, Qi[:, k, :], Qi[:, k, :], ALU.mult), ['Q'], ['t1'])
                    o1(TT(Qr[:, k + 1, :], Qr[:, k + 1, :], t1, ALU.subtract), ['Q', 't1'], ['Q'])
                    o1(TT(t1, Qr[:, k, :], Qi[:, k, :], ALU.mult), ['Q'], ['t1'])
                    o1(TS1(Qi[:, k + 1, :], t1, 2.0, ALU.mult), ['t1'], ['Q'])
                o1(TS1(Qn, Qi, -1.0, ALU.mult), ['Q'], ['Q'])
                KSc = v5(f(2 * 2 * 4 * 9 * 3), 2, 2, 4)
                for hf in range(2):
                    for j, Qx in enumerate((Qr, Qi, Qn)):
                        src = Qx.rearrange("p k (d hf gpl) -> p d hf gpl k", d=2, hf=2)[:, :, hf, :, :]
                        dst = KSc[:, hf, :, :, :].rearrange("p d gpl (k j) -> p d gpl k j", j=3)[:, :, :, :, j]
                        o1(CP(dst, src), ['Q'], ['KSc'])
                    S.dma('sp', s5k_s.ap()[l, hf], KSc[:, hf, :, :, :].rearrange("p d gpl x -> p (d gpl x)"),
                          reads=['KSc'], writes=['s5k_s%d' % l])
                Bbr, Bbi, tB = v3(f(256), 16), v3(f(256), 16), v3(f(256), 16)

                def bc16(v):
                    return v.unsqueeze(2).broadcast_to([128, 16, 16])
                o1(TT(Bbr, BR, bc16(cr), ALU.mult), ['BR', 'cr'], ['Bbr'])
                o1(TT(tB, BI, bc16(ci), ALU.mult), ['BI', 'ci'], ['tB'])
                o1(TT(Bbr, Bbr, tB, ALU.subtract), ['Bbr', 'tB'], ['Bbr'])
                o1(TT(Bbi, BI, bc16(cr), ALU.mult), ['BI', 'cr'], ['Bbi'])
                o1(TT(tB, BR, bc16(ci), ALU.mult), ['BR', 'ci'], ['tB'])
                o1(TT(Bbi, Bbi, tB, ALU.add), ['Bbi', 'tB'], ['Bbi'])
                EBr, EBi = v3(f(2048), 16), v3(f(2048), 16)
                EWr, EWi = v3(f(2048), 16), v3(f(2048), 16)
                for (Ex, nm) in ((EBr, 'EBr'), (EBi, 'EBi'), (EWr, 'EWr'), (EWi, 'EWi')):
                    S.op('dve', MS(Ex, 0.0), writes=[nm])

                def expand(Ex, val, nm, vnm):
                    ev = Ex.rearrange("p (d hf gpl) c -> p d hf gpl c", d=2, hf=2)
                    vv = val.rearrange("p (d hf gpl) j -> p d hf gpl j", d=2, hf=2)
                    for two in range(2):
                        ps_ = slice(64 * two, 64 * two + 64)
                        for gpl in range(4):
                            for d_ in range(2):
                                o1(CP(ev[ps_, d_, :, gpl, 32 * gpl + 16 * two:32 * gpl + 16 * two + 16], vv[ps_, d_, :, gpl, :]),
                                   [vnm], [nm])
                expand(EBr, Bbr, 'EBr', 'Bbr')
                expand(EBi, Bbi, 'EBi', 'Bbi')
                WUP = [v4(AR.bf16(4096), 2, 2) .rearrange("p d part (s c) -> p d part s c", s=8) for _ in range(2)]
                WDN = [AR.bf16(4096) for _ in range(2)]
                KM = [v3(AR.bf16(2048), 2).rearrange("p d (t c) -> p d t c", t=8) for _ in range(2)]
                for hf in range(2):
                    S.op('dve', MS(WDN[hf], 0.0), writes=['WDN%d' % hf])
                WDNv = [w.rearrange("p (d gpl part n c) -> p d gpl part n c", d=2, gpl=4, part=2, n=8) for w in WDN]
                Wre, Wim, BPr, BPi, tW = (v3(f(256), 16) for _ in range(5))
                WUT = v3(f(8 * 128), 8)
                for (nm,) in (('WUT',),):
                    S.op('dve', MS(WUT, 0.0), writes=[nm])
                for n in range(9):
                    prn, pin = bc16(PWr[:, n, :]), bc16(PWi[:, n, :])
                    o1(TT(Wre, CR, prn, ALU.mult), ['CR', 'PW'], ['Wre'])
                    o1(TT(tW, CI, pin, ALU.mult), ['CI', 'PW'], ['tW'])
                    o1(TT(Wre, Wre, tW, ALU.subtract), ['Wre', 'tW'], ['Wre'])
                    o1(TT(Wim, CR, pin, ALU.mult), ['CR', 'PW'], ['Wim'])
                    o1(TT(tW, CI, prn, ALU.mult), ['CI', 'PW'], ['tW'])
                    o1(TT(Wim, Wim, tW, ALU.add), ['Wim', 'tW'], ['Wim'])
                    o1(TS1(Wim, Wim, -1.0, ALU.mult), ['Wim'], ['Wim'])
                    if n >= 1:
                        for part, Wx, wnm in ((0, Wre, 'Wre'), (1, Wim, 'Wim')):
                            wv = Wx.rearrange("p (d hf gpl) i -> p d hf gpl i", d=2, hf=2)
                            for hf in range(2):
                                for two in range(2):
                                    ps_ = slice(64 * two, 64 * two + 64)
                                    o1(CP(WDNv[hf][ps_, :, :, part, n - 1, 16 * two:16 * two + 16], wv[ps_, :, hf, :, :]),
                                       [wnm], ['WDN%d' % hf])
                    if n <= 7:
                        expand(EWr, Wre, 'EWr', 'Wre')
                        expand(EWi, Wim, 'EWi', 'Wim')
                        o1(TT(BPr, Bbr, prn, ALU.mult), ['Bbr', 'PW'], ['BPr'])
                        o1(TT(tW, Bbi, pin, ALU.mult), ['Bbi', 'PW'], ['tW'])
                        o1(TT(BPr, BPr, tW, ALU.subtract), ['BPr', 'tW'], ['BPr'])
                        o1(TT(BPi, Bbr, pin, ALU.mult), ['Bbr', 'PW'], ['BPi'])
                        o1(TT(tW, Bbi, prn, ALU.mult), ['Bbi', 'PW'], ['tW'])
                        o1(TT(BPi, BPi, tW, ALU.add), ['BPi', 'tW'], ['BPi'])
                        wut = WUT.rearrange("p (d hf part) (gpl c) -> p d hf part gpl c", d=2, hf=2, gpl=4)
                        for part, Bx, bnm in ((0, BPr, 'BPr'), (1, BPi, 'BPi')):
                            bv = Bx.rearrange("p (d hf gpl) j -> p d hf gpl j", d=2, hf=2)
                            for two in range(2):
                                ps_ = slice(64 * two, 64 * two + 64)
                                for d_ in range(2):
                                    o1(CP(wut[ps_, d_, :, part, :, 16 * two:16 * two + 16], bv[ps_, d_, :, :, :]),
                                       [bnm], ['WUT'])
                        for d_ in range(2):
                            s_idx = (7 - n) if d_ == 0 else n
                            for hf in range(2):
                                p0, k0 = psum()
                                for part in range(2):
                                    idx = (d_ * 2 + hf) * 2 + part
                                    S.op('pe', TR(p0[:, part * 128:(part + 1) * 128], WUT[:, idx, :], identF),
                                         reads=['WUT', 'identF'], writes=[k0])
                                S.op('act', ACT(WUP[hf][:, d_, :, s_idx, :], v3(p0[:, 0:256], 2), AF.Copy),
                                     reads=[k0], writes=['WUP%d' % hf])
                        for d_ in range(2):
                            for hf in range(2):
                                p0, k0 = psum()
                                cnt = 0
                                for gpl in range(4):
                                    cidx = d_ * 8 + hf * 4 + gpl
                                    for (Eb, Ew, bn, wn) in ((EBr, EWr, 'EBr', 'EWr'), (EBi, EWi, 'EBi', 'EWi')):
                                        S.op('pe', MM(p0[:, 0:128], Eb[:, cidx, :], Ew[:, cidx, :], start=(cnt == 0), stop=(cnt == 7)),
                                             reads=[bn, wn], writes=[k0], inc=(cnt == 7))
                                        cnt += 1
                                S.op('act', ACT(KM[hf][:, d_, n, :], p0[:, 0:128], AF.Copy), reads=[k0], writes=['KM%d' % hf])
                for hf in range(2):
                    dst = s5m_s.ap()[l, hf]
                    S.dma('sp', dst[:, 0:4096], WUP[hf].rearrange("p d part s c -> p (d part s c)"),
                          reads=['WUP%d' % hf], writes=['s5m_s%d' % l])
                    S.dma('sp', dst[:, 4096:8192], WDN[hf], reads=['WDN%d' % hf], writes=['s5m_s%d' % l])
                    S.dma('sp', dst[:, 8192:10240], KM[hf].rearrange("p d t c -> p (d t c)"),
                          reads=['KM%d' % hf], writes=['s5m_s%d' % l])
                S.barrier(skip_pool=True)
                AR.top = m0

            for l in layers:
                s5_precompute(l)
            assert AR.top == smark
            AR.top = gmark
            ckpt('s5pre')

            ring = {}

            def load_w(name, slots, shape_ap_fn, src_ap, skey):
                st = ring[name]
                i = st['n'] % len(st['slots'])
                st['n'] += 1
                ap = st['slots'][i]
                key = '%s_%d' % (name, i)
                S.dma('sp', ap, src_ap, reads=[skey], writes=[key])
                return ap, key

            def mk_ring(name, nslots, nelem_bf16, shaper):
                ring[name] = {'n': 0, 'slots': [shaper(AR.bf16(nelem_bf16)) for _ in range(nslots)]}

            def load_seq(s):
                m0 = AR.top
                stg = [AR.f32(D) for _ in range(2)]
                for tt in range(T // 128):
                    sl = tt % 2
                    if tt < 2:
                        src = ctx_t.ap()[s, tt * 128:(tt + 1) * 128, :]
                    else:
                        src = x_t.ap()[s, (tt - 2) * 128:(tt - 1) * 128, :]
                    S.dma('sp', stg[sl], src, writes=['stg%d' % sl])
                    for g4 in range(2):
                        p0, k0 = psum()
                        for j in range(4):
                            kt = g4 * 4 + j
                            S.op('pe', TR(p0[:, j * 128:(j + 1) * 128], stg[sl][:, kt * 128:(kt + 1) * 128], identF),
                                 reads=['stg%d' % sl, 'identF'], writes=[k0])
                        wk = []
                        for j in range(4):
                            wk += hk(g4 * 4 + j, tt * 128, tt * 128 + 128)
                        eng = 'act' if (g4 == 0) else 'dve'
                        if eng == 'act':
                            S.op('act', ACT(h[:, g4 * 4:g4 * 4 + 4, tt * 128:(tt + 1) * 128], v3(p0[:, :], 4), AF.Copy),
                                 reads=[k0], writes=wk)
                        else:
                            S.op('dve', CP(h[:, g4 * 4:g4 * 4 + 4, tt * 128:(tt + 1) * 128], v3(p0[:, :], 4)), reads=[k0], writes=wk)
                S.barrier()
                AR.top = m0

            def store_seq(s):
                m0 = AR.top
                stg = [AR.f32(D) for _ in range(2)]
                tts = list(range(2, T // 128)) + (list(range(2)) if out_hc else [])
                for n_, tt in enumerate(tts):
                    sl = n_ % 2
                    for g4 in range(2):
                        p0, k0 = psum()
                        rk = []
                        for j in range(4):
                            kt = g4 * 4 + j
                            S.op('pe', TR(p0[:, j * 128:(j + 1) * 128], h[:, kt, tt * 128:(tt + 1) * 128], identF),
                                 reads=hk(kt, tt * 128, tt * 128 + 128) + ['identF'], writes=[k0])
                        if g4 == 0:
                            S.op('act', ACT(stg[sl][:, 0:512], p0[:, :], AF.Copy), reads=[k0], writes=['ostg%d' % sl])
                        else:
                            S.op('dve', CP(stg[sl][:, 512:1024], p0[:, :]), reads=[k0], writes=['ostg%d' % sl])
                    if tt >= 2:
                        dst = out_t.ap()[s, (tt - 2) * 128:(tt - 1) * 128, :]
                    else:
                        dst = hc_t.ap()[s, tt * 128:(tt + 1) * 128, :]
                    S.dma('sp', dst, stg[sl], reads=['ostg%d' % sl], writes=['outd'], key='ostg%d' % sl)
                S.barrier()
                AR.top = m0

            def rms_stats(src_fn, src_keys_fn, n, tmp_sq, eng_sq='pool'):
                pss, kss = psum()
                for kt in range(KT):
                    sq, sqk = tmp_sq[kt % 2]
                    S.op(eng_sq, TT(sq[:, 0:n], src_fn(kt), src_fn(kt), ALU.mult), reads=src_keys_fn(kt), writes=[sqk])
                    S.op('pe', MM(pss[:, 0:n], ones_bf, sq[:, 0:n], start=(kt == 0), stop=(kt == KT - 1)),
                         reads=[sqk, 'ones_bf'], writes=[kss])
                return pss, kss

            def rstd_from(pss, kss, n, tmp, tmpk, rstd, rstdk):
                S.op('act', ACT(tmp[:, 0:n], pss[:, 0:n], AF.Sqrt, bias=EPS, scale=1.0 / D), reads=[kss], writes=[tmpk])
                S.op('dve', RCP(rstd[:, 0:n], tmp[:, 0:n]), reads=[tmpk], writes=[rstdk])

            def mk_scratch():
                sc = {'sq': [(AR.bf16(512), 'scq%d' % i) for i in range(2)],
                      'f': [(AR.f32(512), 'scf%d' % i) for i in range(5)]}
                return sc

            def norm_block(bi, i_scale, i_shift, sc):
                sqs = sc['sq']
                tmp, tmpk = sc['f'][2]
                rstds = [sc['f'][3], sc['f'][4]]
                tts = [sc['f'][0], sc['f'][1]]
                (c0, c1, w) = BLOCKS[bi]
                n = c1 - c0
                pss, kss = rms_stats(lambda kt: h[:, kt, c0:c1], lambda kt: hk(kt, c0, c1), n, sqs)
                rstd, rk = rstds[bi % 2]
                rstd_from(pss, kss, n, tmp, tmpk, rstd, rk)
                for kt in range(KT):
                    t_, tkk = tts[kt % 2]
                    S.op('dve', TT(t_[:, 0:n], h[:, kt, c0:c1], rstd[:, 0:n], ALU.mult), reads=hk(kt, c0, c1) + [rk], writes=[tkk])
                    S.op('act', ACT(A[:, kt, c0:c1], t_[:, 0:n], AF.Identity, bias=dv[:, w, i_shift, kt:kt + 1],
                                    scale=dv[:, w, i_scale, kt:kt + 1]), reads=[tkk, 'dv'], writes=ak(kt, c0, c1))

            def norm_to_A(i_scale, i_shift, sc):
                for bi in range(len(BLOCKS)):
                    norm_block(bi, i_scale, i_shift, sc)

            def calc_dv(l, s, idxs):
                for w, r in ((0, s), (1, nseq)):
                    M = lambda m: mods[:, l, m, :, r]
                    if 0 in idxs:
                        S.op('dve', STT(dv[:, w, 0, :], M(1), 1.0, NG(l, 0), ALU.add, ALU.mult), reads=['mods', 'vecs', 'dv'], writes=['dv'])
                        S.op('dve', CP(dv[:, w, 1, :], M(0)), reads=['mods', 'dv'], writes=['dv'])
                    if 2 in idxs:
                        S.op('dve', TT(dv[:, w, 2, :], M(2), NG(l, 1), ALU.mult), reads=['mods', 'vecs', 'dv'], writes=['dv'])
                        S.op('dve', STT(dv[:, w, 3, :], M(4), 1.0, NG(l, 2), ALU.add, ALU.mult), reads=['mods', 'vecs', 'dv'], writes=['dv'])
                        S.op('dve', CP(dv[:, w, 4, :], M(3)), reads=['mods', 'dv'], writes=['dv'])
                        S.op('dve', TT(dv[:, w, 5, :], M(5), NG(l, 3), ALU.mult), reads=['mods', 'vecs', 'dv'], writes=['dv'])

            def proj_mm(wt, wkey, c0, c1):
                pp, pk = psum()
                n = c1 - c0
                for kt in range(KT):
                    S.op('pe', MM(pp[:, 0:n], wt[:, kt, :], A[:, kt, c0:c1], start=(kt == 0), stop=(kt == KT - 1)),
                         reads=[wkey] + ak(kt, c0, c1), writes=[pk], inc=(kt == KT - 1))
                return pp, pk

            def win_tile(l, o):
                return load_w('win', None, None, win_s.ap()[l, o], 'win_s%d' % l)

            def layer(l, s, pre_normed, l_next):
                lm0 = AR.top
                calc_dv(l, s, (2,) if pre_normed else (0, 2))
                sso = v3(AR.bf16(2 * T), 2)
                pm0 = AR.top
                sc = mk_scratch()
                if not pre_normed:
                    norm_to_A(0, 1, sc)
                dump('a', A[:, :, :], [128, KT, T], sum([ak(kt, 0, T) for kt in range(KT)], []))
                ckpt('norm')
                mk_ring('win', 2, KT * 128, lambda a: v3(a, KT))
                u = v3(AR.bf16(2 * T), 2)
                for hf in range(2):
                    wt, wk = win_tile(l, O_U + hf)
                    for (c0, c1, w) in BLOCKS:
                        pp, pk = proj_mm(wt, wk, c0, c1)
                        S.op('act', ACT(u[:, hf, c0:c1], pp[:, 0:c1 - c0], AF.Copy), reads=[pk], writes=tk('u%d' % hf, c0, c1))
                dump('u', u[:, :, :], [128, 2, T], tk('u0', 0, T) + tk('u1', 0, T))
                ckpt('uproj')
                mats = AR.bf16(10240)
                ksc = AR.f32(216)
                P0 = [v3(AR.f32(2 * 290), 2) for _ in range(4)]
                P1 = [v3(AR.f32(2 * 290), 2) for _ in range(4)]
                Xb = [[v3(AR.bf16(2 * 290), 2) for _ in range(4)] for _ in range(2)]
                KT1 = [v3(AR.f32(2 * 290), 2) for _ in range(1)]
                KT2 = [v3(AR.f32(2 * 290), 2) for _ in range(1)]
                yv = [sc['f'][0], sc['f'][1]]
                (g1, g1k), (g2, g2k), (g3, g3k) = sc['f'][2], sc['f'][3], sc['f'][4]
                WUPv = mats[:, 0:4096].rearrange("p (d part s c) -> p d part s c", d=2, part=2, s=8)
                WDNv = mats[:, 4096:8192].rearrange("p (d gpl part n c) -> p d gpl part n c", d=2, gpl=4, part=2, n=8)
                KMv = mats[:, 8192:10240].rearrange("p (d t c) -> p d t c", d=2, t=8)
                kscv = ksc.rearrange("p (d gpl k j) -> p d gpl k j", d=2, gpl=4, k=9)
                NCH = T // 8
                for hf in range(2):
                    S.dma('sp', mats, s5m_s.ap()[l, hf], reads=['s5m_s%d' % l], writes=['mats'])
                    S.dma('sp', ksc, s5k_s.ap()[l, hf], reads=['s5k_s%d' % l], writes=['ksc'])
                    for d_ in range(2):
                        chains = []
                        for gpl in range(4):
                            chain = []
                            xk = 'X%d' % gpl
                            xbk = 'Xb%d_%d' % (d_, gpl)
                            ke = 'dve'
                            S.op(ke, MS(Xb[d_][gpl][:, :, 0:1] if d_ == 0 else Xb[d_][gpl][:, :, 32:33], 0.0), writes=[xbk])
                            for part in range(2):
                                pp, pk = psum()
                                for s_ in range(8):
                                    S.op('pe', MM(pp[:, 0:NCH], WUPv[32 * gpl:32 * gpl + 32, d_, part, s_, :],
                                                  u[32 * gpl:32 * gpl + 32, hf, s_:T:8], start=(s_ == 0), stop=(s_ == 7),
                                                  tp=(32 * gpl, 0)),
                                         reads=['mats'] + tk('u%d' % hf, 0, T), writes=[pk], inc=(s_ == 7))
                                if d_ == 0:
                                    S.op('act', ACT(P0[gpl][:, part, 1:289], pp[:, 0:288], AF.Copy), reads=[pk], writes=[xk])
                                else:
                                    S.op('act', ACT(P0[gpl][:, part, 0:32], pp[:, 0:32], AF.Copy), reads=[pk], writes=[xk])
                                    S.op('act', ACT(P0[gpl][:, part, 33:289], pp[:, 32:288], AF.Copy), reads=[pk], writes=[xk])

                            def ks_run(lo, W, nlev, right, final_bf, src, dst):
                                for k in range(nlev):
                                    sh = 1 << k
                                    n = W - sh
                                    last = (k == nlev - 1)
                                    a_, b_, nb_ = kscv[:, d_, gpl, k, 0:1], kscv[:, d_, gpl, k, 1:2], kscv[:, d_, gpl, k, 2:3]
                                    Dd = Xb[d_][gpl] if (last and final_bf) else dst
                                    wkeys = [xk] + ([xbk] if (last and final_bf) else [])
                                    if right:
                                        so_, do_ = slice(lo, lo + n), slice(lo + sh, lo + W)
                                        ho_ = slice(lo, lo + sh)
                                    else:
                                        so_, do_ = slice(lo + sh, lo + W), slice(lo, lo + n)
                                        ho_ = slice(lo + n, lo + W)
                                    if gpl < 3:
                                        chain.append(('dve', STT(dst[:, :, do_], src[:, :, so_], a_, src[:, :, do_], ALU.mult, ALU.add), [xk, 'ksc'], wkeys))
                                        chain.append(('dve', STT(Dd[:, 0, do_], src[:, 1, so_], nb_, dst[:, 0, do_], ALU.mult, ALU.add), [xk, 'ksc'], wkeys))
                                        chain.append(('dve', STT(Dd[:, 1, do_], src[:, 0, so_], b_, dst[:, 1, do_], ALU.mult, ALU.add), [xk, 'ksc'], wkeys))
                                        chain.append(('dve', CP(Dd[:, :, ho_], src[:, :, ho_]), [xk], wkeys))
                                    else:
                                        t1_, t1k = KT1[0], 'kt1_%d' % gpl
                                        t2_, t2k = KT2[0], 'kt2_%d' % gpl
                                        chain.append(('act', ACT(t1_[:, :, 0:n], src[:, :, so_], AF.Identity, scale=a_), [xk, 'ksc'], [t1k]))
                                        chain.append(('act', ACT(t2_[:, 0, 0:n], src[:, 1, so_], AF.Identity, scale=nb_), [xk, 'ksc'], [t2k]))
                                        chain.append(('act', ACT(t2_[:, 1, 0:n], src[:, 0, so_], AF.Identity, scale=b_), [xk, 'ksc'], [t2k]))
                                        chain.append(('pool', TT(dst[:, :, do_], t1_[:, :, 0:n], src[:, :, do_], ALU.add), [xk, t1k], wkeys))
                                        chain.append(('pool', TT(Dd[:, :, do_], t2_[:, :, 0:n], dst[:, :, do_], ALU.add), [xk, t2k], wkeys))
                                        chain.append(('pool', CP(Dd[:, :, ho_], src[:, :, ho_]), [xk], wkeys))
                                    src, dst = dst, src
                                return src
                            if d_ == 0:
                                ks_run(1, 288, 9, True, True, P0[gpl], P1[gpl])
                            else:
                                res = ks_run(0, 32, 5, False, False, P0[gpl], P1[gpl])
                                ce_ = 'dve' if gpl < 3 else 'pool'
                                chain.append((ce_, CP(Xb[d_][gpl][:, :, 0:32], res[:, :, 0:32]), [xk], [xk, xbk]))
                                chain.append((ce_, CP(P0[gpl][:, :, 289:290], res[:, :, 0:1]), [xk], [xk]))
                                ks_run(33, 257, 9, False, True, P0[gpl], P1[gpl])
                            chains.append(chain)
                        for i_ in range(max(len(c) for c in chains)):
                            for c in chains:
                                if i_ < len(c):
                                    S.op(c[i_][0], c[i_][1], reads=c[i_][2], writes=c[i_][3])
                    for bi, (c0, c1, w) in enumerate(BLOCKS):
                        n = c1 - c0
                        nch = n // 8
                        ch0 = c0 // 8
                        py, pyk = psum()
                        for s_ in range(8):
                            ops = []
                            for d_ in range(2):
                                srange = range(0, s_ + 1) if d_ == 0 else range(s_, 8)
                                for sp_ in srange:
                                    ops.append((py[:, s_:n:8], KMv[:, d_, abs(s_ - sp_), :], u[:, hf, c0 + sp_:c1:8], None,
                                                ['mats'] + tk('u%d' % hf, c0, c1), False))
                                nidx = s_ if d_ == 0 else 7 - s_
                                e0 = ch0 if d_ == 0 else (ch0 + 1 if w else ch0 + 2)
                                for gpl in range(4):
                                    for part in range(2):
                                        ops.append((py[32 * gpl:32 * gpl + 32, s_:n:8], WDNv[:, d_, gpl, part, nidx, :],
                                                    Xb[d_][gpl][:, part, e0:e0 + nch], (0, 32 * gpl),
                                                    ['mats', 'Xb%d_%d' % (d_, gpl)], (d_ == 1 and part == 1)))
                            for i_, (o_, l_, r_, tp_, rd_, st_) in enumerate(ops):
                                S.op('pe', MM(o_, l_, r_, start=(i_ == 0), stop=st_, tp=tp_),
                                     reads=rd_, writes=[pyk], inc=(i_ == len(ops) - 1))
                        y_, yk = yv[bi % 2]
                        S.op('dve', STT(y_[:, 0:n], u[:, hf, c0:c1], SD(l, hf), py[:, 0:n], ALU.mult, ALU.add),
                             reads=[pyk, 'vecs'] + tk('u%d' % hf, c0, c1), writes=[yk])
                        S.op('pool', TT(g1[:, 0:n], y_[:, 0:n], y_[:, 0:n], ALU.mult), reads=[yk], writes=[g1k])
                        S.op('pool', TS2(g1[:, 0:n], g1[:, 0:n], 0.044715, 1.0, ALU.mult, ALU.add), reads=[g1k], writes=[g1k])
                        S.op('pool', TT(g2[:, 0:n], g1[:, 0:n], y_[:, 0:n], ALU.mult), reads=[g1k, yk], writes=[g2k])
                        S.op('act', ACT(g3[:, 0:n], g2[:, 0:n], AF.Sigmoid, scale=1.5957691216057308), reads=[g2k], writes=[g3k])
                        S.op('dve', TT(sso[:, hf, c0:c1], y_[:, 0:n], g3[:, 0:n], ALU.mult), reads=[yk, g3k],
                             writes=tk('sso%d' % hf, c0, c1))
                dump('g', sso[:, :, :], [128, 2, T], tk('sso0', 0, T) + tk('sso1', 0, T))
                wg = v3(AR.bf16(512), 2)
                S.dma('sp', wg, wglu_s.ap()[l], reads=['wglu_s%d' % l], writes=['wg'])
                for bi, (c0, c1, w) in enumerate(BLOCKS):
                    n = c1 - c0
                    zs = []
                    for ho in range(2):
                        pz, pzk = psum()
                        for hf in range(2):
                            S.op('pe', MM(pz[:, 0:n], wg[:, hf, ho * 128:(ho + 1) * 128], sso[:, hf, c0:c1], start=(hf == 0), stop=(hf == 1)),
                                 reads=['wg'] + tk('sso%d' % hf, c0, c1), writes=[pzk], inc=(hf == 1))
                        zs.append((pz, pzk))
                    for ho in range(2):
                        pz, pzk = zs[ho]
                        gt, gk = (g1, g1k) if ho == 0 else (g2, g2k)
                        S.op('act', ACT(gt[:, 0:n], pz[:, 0:n], AF.Sigmoid, bias=BG(l, ho)), reads=[pzk, 'vecs'], writes=[gk])
                        S.op('pool', TT(sso[:, ho, c0:c1], sso[:, ho, c0:c1], gt[:, 0:n], ALU.mult),
                             reads=[gk] + tk('sso%d' % ho, c0, c1), writes=tk('sso%d' % ho, c0, c1))
                dump('ssm', sso[:, :, :], [128, 2, T], tk('sso0', 0, T) + tk('sso1', 0, T))
                ckpt('s5')
                S.barrier()
                AR.top = pm0
                cvo = v3(AR.bf16(2 * T), 2)
                pm1 = AR.top
                mk_ring('win', 3, KT * 128, lambda a: v3(a, KT))
                CW_ = T + 4
                ccx = v3(AR.bf16(2 * CW_), 2)
                cb = v3(AR.bf16(2 * T), 2)
                cct = [AR.bf16(512) for _ in range(2)]
                o1t = [AR.f32(512) for _ in range(2)]
                for col in (0, 257, 258, CW_ - 1):
                    S.op('pool', MS(ccx[:, :, col:col + 1], 0.0), writes=['ccxpad'])

                def coff(c0):
                    return c0 + 1 if c0 < LC else c0 + 3
                for hf in range(2):
                    wcb, kcb = win_tile(l, O_CB + hf)
                    wcc, kcc = win_tile(l, O_CC + hf)
                    wcx, kcx = win_tile(l, O_CX + hf)
                    for bi, (c0, c1, w) in enumerate(BLOCKS):
                        n = c1 - c0
                        pp, pk = proj_mm(wcb, kcb, c0, c1)
                        S.op('act', ACT(cb[:, hf, c0:c1], pp[:, 0:n], AF.Copy), reads=[pk], writes=tk('cb%d' % hf, c0, c1))
                        pp, pk = proj_mm(wcc, kcc, c0, c1)
                        ct_, ctk = cct[bi % 2], 'cct%d' % (bi % 2)
                        S.op('act', ACT(ct_[:, 0:n], pp[:, 0:n], AF.Copy), reads=[pk], writes=[ctk])
                        pp, pk = proj_mm(wcx, kcx, c0, c1)
                        S.op('dve', TT(ccx[:, hf, coff(c0):coff(c0) + n], pp[:, 0:n], ct_[:, 0:n], ALU.mult), reads=[pk, ctk],
                             writes=['ccx%d' % hf])
                for hf in range(2):
                    for bi, (c0, c1, w) in enumerate(BLOCKS):
                        n = c1 - c0
                        b0 = coff(c0)
                        ot, otk = o1t[bi % 2], 'o1t%d' % (bi % 2)
                        S.op('pool', TS1(ot[:, 0:n], ccx[:, hf, b0 - 1:b0 - 1 + n], CW(l, 0, hf), ALU.mult),
                             reads=['ccx%d' % hf, 'ccxpad', 'vecs'], writes=[otk])
                        S.op('dve', STT(ot[:, 0:n], ccx[:, hf, b0:b0 + n], CW(l, 1, hf), ot[:, 0:n], ALU.mult, ALU.add),
                             reads=['ccx%d' % hf, 'vecs', otk], writes=[otk])
                        S.op('dve', STT(ot[:, 0:n], ccx[:, hf, b0 + 1:b0 + 1 + n], CW(l, 2, hf), ot[:, 0:n], ALU.mult, ALU.add),
                             reads=['ccx%d' % hf, 'ccxpad', 'vecs', otk], writes=[otk])
                        S.op('dve', TT(cvo[:, hf, c0:c1], ot[:, 0:n], cb[:, hf, c0:c1], ALU.mult),
                             reads=[otk] + tk('cb%d' % hf, c0, c1), writes=tk('cvo%d' % hf, c0, c1))
                dump('conv', cvo[:, :, :], [128, 2, T], tk('cvo0', 0, T) + tk('cvo1', 0, T))
                ckpt('conv')
                S.barrier()
                AR.top = pm1
                q = v3(AR.bf16(4 * T), 4)
                kd = v3(AR.bf16(2 * T), 2)
                Vt = v4(AR.bf16(18 * 2 * 128), 18, 2)
                pm2 = AR.top
                mk_ring('win', 4, KT * 128, lambda a: v3(a, KT))
                ropec = AR.f32(L)
                ropes = AR.f32(L)
                S.dma('sp', ropec, ropec_t.ap(), writes=['ropec'])
                S.dma('sp', ropes, ropes_t.ap(), writes=['ropes'])
                rt = [AR.f32(512) for _ in range(2)]
                for (dst, dnm, o_pl, o_sw, cnt) in ((q, 'q', O_Q, O_QS, 4), (kd, 'kd', O_K, O_KS, 2)):
                    for i in range(cnt):
                        wp, kp = win_tile(l, o_pl + i)
                        wsw, ksw = win_tile(l, o_sw + i)
                        for (c0, c1, w) in BLOCKS:
                            n = c1 - c0
                            pp, pk = proj_mm(wp, kp, c0, c1)
                            wkeys = tk('%s%d' % (dnm, i), c0, c1)
                            if w:
                                S.op('act', ACT(dst[:, i, c0:c1], pp[:, 0:n], AF.Copy), reads=[pk], writes=wkeys)
                            else:
                                p2, pk2 = proj_mm(wsw, ksw, c0, c1)
                                lc0 = c0 - LC
                                S.op('dve', TT(rt[0][:, 0:n], pp[:, 0:n], ropec[:, lc0:lc0 + n], ALU.mult), reads=[pk, 'ropec'], writes=['rt0'])
                                S.op('dve', TT(rt[1][:, 0:n], p2[:, 0:n], ropes[:, lc0:lc0 + n], ALU.mult), reads=[pk2, 'ropes'], writes=['rt1'])
                                S.op('pool', TT(dst[:, i, c0:c1], rt[0][:, 0:n], rt[1][:, 0:n], ALU.add), reads=['rt0', 'rt1'], writes=wkeys)
                S.op('pool', MS(Vt[:, :, :, 64:128], 1.0), writes=['Vones'])
                wv, kv = win_tile(l, O_V)
                for t4 in range(0, 18, 4):
                    nt = min(4, 18 - t4)
                    pp, pk = psum()
                    for j in range(nt):
                        tt = t4 + j
                        for kt in range(KT):
                            S.op('pe', MM(pp[:, j * 128:(j + 1) * 128], A[:, kt, tt * 128:(tt + 1) * 128], wv[:, kt, :],
                                          start=(kt == 0), stop=(kt == KT - 1)),
                                 reads=[kv] + ak(kt, tt * 128, tt * 128 + 128), writes=[pk], inc=(kt == KT - 1))
                    S.op('act', ACT(Vt[:, t4:t4 + nt, :, 0:64], pp[:, 0:nt * 128].rearrange("p (t g c) -> p t g c", t=nt, g=2), AF.Copy),
                         reads=[pk], writes=['V%d' % (t4 + j) for j in range(nt)])
                dump('q', q[:, :, :], [128, 4, T], sum([tk('q%d' % i, 0, T) for i in range(4)], []))
                dump('kd', kd[:, :, :], [128, 2, T], tk('kd0', 0, T) + tk('kd1', 0, T))
                ckpt('qkv')
                S.barrier()
                AR.top = pm2
                Pt = [v3(AR.bf16(5 * 512), 5) for _ in range(2)]
                rc = AR.f32(512)
                nblk = [0]

                def attend(qc0, key_tiles, l_):
                    for g in range(2):
                        P_ = Pt[nblk[0] % 2]
                        pkey = 'Pt%d' % (nblk[0] % 2)
                        nblk[0] += 1
                        nk = len(key_tiles)
                        for half in range(2):
                            rows = slice(64 * half, 64 * half + 64)
                            for kp in range(0, nk, 2):
                                nkk = min(2, nk - kp)
                                ps_, psk = psum()
                                for j in range(nkk):
                                    kc0 = key_tiles[kp + j][0] * 128
                                    for a_ in range(2):
                                        hh = 2 * a_ + half
                                        hd = 4 * g + hh
                                        qi = hd // 2
                                        S.op('pe', MM(ps_[:, (2 * j + a_) * 128:(2 * j + a_ + 1) * 128], kd[rows, g, kc0:kc0 + 128],
                                                      q[rows, qi, qc0:qc0 + 128], start=True, stop=True, tp=(64 * half, 0)),
                                             reads=tk('kd%d' % g, kc0, kc0 + 128) + tk('q%d' % qi, qc0, qc0 + 128), writes=[psk],
                                             inc=(j == nkk - 1 and a_ == 1))
                                S.op('act', ACT(P_[:, kp:kp + nkk, half * 256:(half + 1) * 256], v3(ps_[:, 0:nkk * 256], nkk), AF.Exp, scale=0.125),
                                     reads=[psk], writes=[pkey + '_%d' % (kp + j) for j in range(nkk)])
                        for ki_, (ktile, msk) in enumerate(key_tiles):
                            if msk is not None:
                                mk_, mkk = (maskp, 'maskp') if msk == 'p' else (maskn, 'maskn')
                                S.op('pool', TT(v3(P_[:, ki_, :], 4), v3(P_[:, ki_, :], 4),
                                                mk_.unsqueeze(1).broadcast_to([128, 4, 128]), ALU.mult),
                                     reads=[pkey + '_%d' % ki_, mkk], writes=[pkey + '_%d' % ki_])
                        if ATT_LEVEL < 1:
                            continue
                        po, pok = psum()
                        for ki_, (ktile, msk) in enumerate(key_tiles):
                            S.op('pe', MM(po[:, :], Vt[:, ktile, g, :], P_[:, ki_, :], start=(ki_ == 0), stop=(ki_ == nk - 1)),
                                 reads=['V%d' % ktile, 'Vones', pkey + '_%d' % ki_], writes=[pok], inc=(ki_ == nk - 1))
                        if ATT_LEVEL < 2:
                            continue
                        S.op('dve', TT(v3(rc[0:64, :], 4), v3(po[64:128, :], 4),
                                       sinkexp[64:128, l_ * 8 + 4 * g:l_ * 8 + 4 * g + 4].unsqueeze(2).broadcast_to([64, 4, 128]), ALU.add),
                             reads=[pok, 'sinkexp'], writes=['rc'])
                        S.op('dve', RCP(rc[0:64, :], rc[0:64, :]), reads=['rc'], writes=['rc'])
                        if ATT_LEVEL < 3:
                            continue
                        for half in range(2):
                            S.op('dve', TT(A[64 * half:64 * half + 64, 2 * g:2 * g + 2, qc0:qc0 + 128],
                                           v3(po[0:64, half * 256:(half + 1) * 256], 2), v3(rc[0:64, half * 256:(half + 1) * 256], 2), ALU.mult),
                                 reads=[pok, 'rc'], writes=ak(2 * g, qc0, qc0 + 128) + ak(2 * g + 1, qc0, qc0 + 128))
                for qb in range(ATT_NQB):
                    kts = [(0, None), (1, None)]
                    if qb > 0:
                        kts.append((2 + qb - 1, 'p'))
                    kts.append((2 + qb, None))
                    if qb < 15:
                        kts.append((2 + qb + 1, 'n'))
                    attend(LC + qb * 128, kts, l)
                if l < DEPTH - 1:
                    for qb in range(2):
                        attend(qb * 128, [(0, None), (1, None)], l)
                dump('attn', A[:, 0:4, :], [128, 4, T], sum([ak(kt, 0, T) for kt in range(4)], []))
                ckpt('attn')
                S.barrier()
                AR.top = pm1
                wo = [v3(AR.bf16(KT * 128), KT) for _ in range(KT)]
                for m in range(KT):
                    S.dma('sp', wo[m], wout_s.ap()[l, m], reads=['wout_s%d' % l], writes=['wo%d' % m])
                mblk = v3(AR.f32(KT * 512), KT)
                sqs = [(AR.bf16(512), 'osq%d' % i) for i in range(2)]
                tmp = AR.f32(512)
                rstd = AR.f32(512)
                tts = [AR.f32(512) for _ in range(2)]

                def mixk(kt, c0, c1):
                    if kt < 4:
                        return A[:, kt, c0:c1], ak(kt, c0, c1)
                    if kt < 6:
                        return cvo[:, kt - 4, c0:c1], tk('cvo%d' % (kt - 4), c0, c1)
                    return sso[:, kt - 6, c0:c1], tk('sso%d' % (kt - 6), c0, c1)
                for bi, (c0, c1, w) in enumerate(BLOCKS):
                    n = c1 - c0
                    for m in range(KT):
                        pp, pk = psum()
                        for kt in range(KT):
                            rap, rkeys = mixk(kt, c0, c1)
                            S.op('pe', MM(pp[:, 0:n], wo[m][:, kt, :], rap, start=(kt == 0), stop=(kt == KT - 1)),
                                 reads=['wo%d' % m] + rkeys, writes=[pk], inc=(kt == KT - 1))
                        S.op('act', ACT(mblk[:, m, 0:n], pp[:, 0:n], AF.Copy), reads=[pk], writes=['mblk%d' % m])
                    pss, kss = rms_stats(lambda kt: mblk[:, kt, 0:n], lambda kt: ['mblk%d' % kt], n, sqs)
                    rstd_from(pss, kss, n, tmp, 'otmp', rstd, 'orstd')
                    for m in range(KT):
                        t_, tkk = tts[m % 2], 'ott%d' % (m % 2)
                        S.op('pool' if m % 4 == 3 else 'dve', TT(t_[:, 0:n], mblk[:, m, 0:n], rstd[:, 0:n], ALU.mult), reads=['mblk%d' % m, 'orstd'], writes=[tkk])
                        S.op('dve', STT(h[:, m, c0:c1], t_[:, 0:n], dv[:, w, 2, m:m + 1], h[:, m, c0:c1], ALU.mult, ALU.add),
                             reads=[tkk, 'dv'] + hk(m, c0, c1), writes=hk(m, c0, c1))
                dump('h1', h[:, :, :], [128, KT, T], sum([hk(kt, 0, T) for kt in range(KT)], []))
                ckpt('oproj')
                S.barrier()
                AR.top = lm0
                sc = mk_scratch()
                norm_to_A(3, 4, sc)
                NB = 576
                HB = 288
                hid = v3(AR.bf16(32 * NB), 32)
                mk_ring('w1', 3, KT * 128, lambda a: v3(a, KT))
                mk_ring('w2', 2, 32 * 128, lambda a: v3(a, 32))
                fblk = v3(AR.f32(KT * NB), KT)
                sqs = sc['sq']
                tmp, tmpk = sc['f'][2]
                rstds = [sc['f'][3], sc['f'][4]]
                tts = [sc['f'][0], sc['f'][1]]
                if l_next is not None:
                    calc_dv(l_next, s, (0,))
                nb_done = 0
                for b4 in range(T // NB):
                    bc0 = b4 * NB
                    for j in range(32):
                        wt, wk = load_w('w1', None, None, w1_s.ap()[l, j], 'w1_s%d' % l)
                        for hh in range(2):
                            c0 = bc0 + hh * HB
                            pp, pk = proj_mm(wt, wk, c0, c0 + HB)
                            hkey = 'hid%d_%d' % (j, hh)
                            S.op('act', ACT(hid[:, j, hh * HB:(hh + 1) * HB], pp[:, 0:HB], AF.Relu), reads=[pk], writes=[hkey])
                            S.op('pool', TT(hid[:, j, hh * HB:(hh + 1) * HB], hid[:, j, hh * HB:(hh + 1) * HB],
                                            hid[:, j, hh * HB:(hh + 1) * HB], ALU.mult), reads=[hkey], writes=[hkey])
                    for m in range(KT):
                        wt2, wk2 = load_w('w2', None, None, w2_s.ap()[l, m], 'w2_s%d' % l)
                        for hh in range(2):
                            pp, pk = psum()
                            for j in range(32):
                                S.op('pe', MM(pp[:, 0:HB], wt2[:, j, :], hid[:, j, hh * HB:(hh + 1) * HB], start=(j == 0), stop=(j == 31)),
                                     reads=[wk2, 'hid%d_%d' % (j, hh)], writes=[pk], inc=(j == 31))
                            S.op('act', ACT(fblk[:, m, hh * HB:(hh + 1) * HB], pp[:, 0:HB], AF.Copy), reads=[pk], writes=['fblk%d_%d' % (m, hh)])
                    for hh in range(2):
                        c0 = bc0 + hh * HB
                        pss, kss = rms_stats(lambda kt: fblk[:, kt, hh * HB:(hh + 1) * HB], lambda kt: ['fblk%d_%d' % (kt, hh)], HB, sqs)
                        rstd, rk = rstds[hh]
                        rstd_from(pss, kss, HB, tmp, tmpk, rstd, rk)
                        segs = []
                        if c0 < LC:
                            e = min(LC, c0 + HB)
                            segs.append((c0, e, 1))
                            if e < c0 + HB:
                                segs.append((e, c0 + HB, 0))
                        else:
                            segs.append((c0, c0 + HB, 0))
                        for m in range(KT):
                            t_, tkk = tts[m % 2]
                            S.op('pool' if m % 4 == 3 else 'dve', TT(t_[:, 0:HB], fblk[:, m, hh * HB:(hh + 1) * HB], rstd[:, 0:HB], ALU.mult),
                                 reads=['fblk%d_%d' % (m, hh), rk], writes=[tkk])
                            for (a0, a1, w) in segs:
                                S.op('dve', STT(h[:, m, a0:a1], t_[:, a0 - c0:a1 - c0], dv[:, w, 5, m:m + 1], h[:, m, a0:a1], ALU.mult, ALU.add),
                                     reads=[tkk, 'dv'] + hk(m, a0, a1), writes=hk(m, a0, a1))
                    if l_next is not None:
                        while nb_done < len(BLOCKS) and BLOCKS[nb_done][1] <= bc0 + NB:
                            norm_block(nb_done, 0, 1, sc)
                            nb_done += 1
                S.barrier()
                AR.top = lm0

            for s in range(nseq):
                load_seq(s)
                ckpt('load')
                for li_, l in enumerate(layers):
                    layer(l, s, li_ > 0, layers[li_ + 1] if li_ + 1 < len(layers) else None)
                store_seq(s)

        except _Stop:
            pass
        S.emit()
    return nc, dbg_t


def _consts():
    f32 = np.float32
    n = np.arange(L)
    row = (n // 64).astype(f32)
    col = (n % 64).astype(f32)
    freqs = (np.float32(10000.0) ** (-np.arange(16, dtype=f32) / np.float32(16))).astype(f32)
    cosT = np.zeros((128, L), f32)
    sinT = np.zeros((128, L), f32)
    for p in range(128):
        dd = p % 64
        i = dd % 16
        pos = row if dd < 32 else col
        ang = (pos * freqs[i]).astype(f32)
        cosT[p] = np.cos(ang).astype(f32)
        sgn = -1.0 if (dd % 32) < 16 else 1.0
        sinT[p] = (sgn * np.sin(ang)).astype(f32)
    ii = np.arange(128)[:, None]
    jj = np.arange(128)[None, :]
    maskp = (ii >= jj).astype(f32)
    maskn = (ii <= jj).astype(f32)
    return dict(ropec=cosT, ropes=sinT, maskp=maskp, maskn=maskn, ident=np.eye(128, dtype=f32))


_WKEYS = ['w_ada', 'b_ada', 'norm_g', 'w_in', 'conv_w', 'attn_sink', 'ssm_lam_re', 'ssm_lam_im', 'ssm_log_dt',
          'ssm_b_re', 'ssm_b_im', 'ssm_c_re', 'ssm_c_im', 'ssm_d', 'w_glu', 'b_glu', 'w_out', 'w_mlp_in', 'w_mlp_out']

LAUNCH_PLAN = [[0, 1, 2, 3]]


def kernel(**inputs):
    f32 = np.float32
    inp = {k: np.ascontiguousarray(np.asarray(v, dtype=f32)) for k, v in inputs.items()}
    consts = _consts()
    hx = inp['x']
    hctx = inp['ctx']
    B = hx.shape[0]
    per = B // NCORES
    for li, layers in enumerate(LAUNCH_PLAN):
        last = (li == len(LAUNCH_PLAN) - 1)
        nc, _ = build_program(layers, nseq=per, out_hc=not last)
        in_maps = []
        for c in range(NCORES):
            sl = slice(c * per, (c + 1) * per)
            m = {'x': np.ascontiguousarray(hx[sl]), 'ctx': np.ascontiguousarray(hctx[sl]),
                 'cc': np.ascontiguousarray(np.concatenate([inp['c'][sl], inp['c_ctx'][None, :]], axis=0))}
            for k in _WKEYS:
                m[k] = inp[k]
            m.update(consts)
            in_maps.append(m)
        res = run_bass_kernel_spmd(nc, in_maps, core_ids=list(range(NCORES)))
        hx = np.concatenate([r['out'] for r in res.results], axis=0)
        if not last:
            hctx = np.concatenate([r['hc_out'] for r in res.results], axis=0)
    return hx.astype(f32)
```

```python
import os
import numpy as np
from contextlib import ExitStack
import concourse.bass as bass
import concourse.mybir as mybir
from concourse.bass_utils import run_bass_kernel_spmd

F32 = mybir.dt.float32
BF16 = mybir.dt.bfloat16
I32 = mybir.dt.int32
AF = mybir.ActivationFunctionType
ALU = mybir.AluOpType
ENGS = ('pe', 'act', 'dve', 'pool', 'sp')

D = 1024
KT = 8
L = 2048
LC = 256
T = L + LC
DEPTH = 4
NSEQ = 4
NCORES = 8
EPS = 1e-6
ARENA_F32 = 53000
PI = float(np.pi)

O_Q, O_QS, O_K, O_KS, O_CB, O_CC, O_CX, O_U, O_V = 0, 4, 8, 10, 12, 14, 16, 18, 20
N_WIN = 21
BLOCKS = [(0, 256, 1)] + [(256 + 512 * i, 256 + 512 * (i + 1), 0) for i in range(4)]


class Sched:
    LIMIT = 16000

    def __init__(self, nc, es):
        self.nc, self.es = nc, es
        self.q = {e: [] for e in ENGS}
        self.cur = {}
        self.waited = {e: {} for e in ENGS}
        self.lastw = {}
        self.rds = {}
        self.dsem = {}
        self.dcnt = {}
        self.nsem = 0
        self.semobj = {}
        self.pending = {}

    def newsem(self):
        self.nsem += 1
        s = self.es.enter_context(self.nc.semaphore("s%d" % self.nsem))
        self.semobj[self.nsem] = s
        return self.nsem

    def _deps(self, eng, reads, writes):
        deps = []
        for b in reads:
            ev = self.lastw.get(b)
            if ev is not None:
                deps.append(ev)
        for b in writes:
            ev = self.lastw.get(b)
            if ev is not None:
                deps.append(ev)
            deps.extend(self.rds.get(b, ()))
        waits = []
        w = self.waited[eng]
        for (sem, val, src) in deps:
            if src == eng and eng == 'pe':
                continue
            if w.get(sem, 0) >= val:
                continue
            w[sem] = val
            waits.append((sem, val))
        return waits

    def _commit(self, ev, reads, writes):
        for b in reads:
            self.rds.setdefault(b, []).append(ev)
        for b in writes:
            self.lastw[b] = ev
            self.rds[b] = []

    def op(self, eng, fn, reads=(), writes=(), inc=True):
        waits = self._deps(eng, reads, writes)
        sem, c = self.cur.get(eng, (None, 0))
        if sem is None or (c >= self.LIMIT and not self.pending.get(eng, False)):
            sem, c = self.newsem(), 0
        self.pending[eng] = not inc
        if inc:
            c += 1
            self.cur[eng] = (sem, c)
            ev = (sem, c, eng)
        else:
            self.cur[eng] = (sem, c)
            ev = (sem, c + 1, eng)
        self.q[eng].append((waits, fn, sem, 1 if inc else 0))
        self._commit(ev, reads, writes)

    def dma(self, eng, out, in_, reads=(), writes=(), key=None, slow=False):
        key = key if key is not None else (writes[0] if writes else reads[0])
        waits = self._deps(eng, reads, writes)
        if key not in self.dsem:
            self.dsem[key] = self.newsem()
            self.dcnt[key] = 0
        self.dcnt[key] += 16
        sem = self.dsem[key]
        ev = (sem, self.dcnt[key], 'dma')
        if slow:
            fn = lambda e: e.dma_start(out=out, in_=in_, allow_slow_non_contiguous=True)
        else:
            fn = lambda e: e.dma_start(out=out, in_=in_)
        self.q[eng].append((waits, fn, sem, 16))
        self._commit(ev, reads, writes)

    def barrier(self, skip_pool=False):
        evs = []
        for eng, (sem, c) in self.cur.items():
            if skip_pool and eng == 'pool':
                continue
            if c > 0:
                evs.append((sem, c))
        for key, sem in self.dsem.items():
            if skip_pool and isinstance(key, str) and key.startswith(('win_s', 'wout_s', 'w1_s', 'w2_s', 'wglu_s')):
                continue
            evs.append((sem, self.dcnt[key]))
        for eng in ENGS:
            if skip_pool and eng == 'pool':
                continue
            w = self.waited[eng]
            waits = []
            for (sem, val) in evs:
                if w.get(sem, 0) >= val:
                    continue
                w[sem] = val
                waits.append((sem, val))
            if waits:
                self.q[eng].append((waits, None, None, 0))

    def emit(self):
        self.barrier()
        so = self.semobj
        with self.nc.Block() as block:
            def run(name):
                def f(e):
                    for (waits, fn, sem, inc) in self.q[name]:
                        for (s, v) in waits:
                            e.wait_ge(so[s], v)
                        if fn is not None:
                            ins = fn(e)
                            if inc:
                                ins.then_inc(so[sem], inc)
                return f
            block.tensor(run('pe'))
            block.scalar(run('act'))
            block.vector(run('dve'))
            block.gpsimd(run('pool'))
            block.sync(run('sp'))


def MM(out, lhsT, rhs, start=True, stop=True, tp=None):
    if tp is None:
        return lambda e: e.matmul(out, lhsT, rhs, start=start, stop=stop)
    return lambda e: e.matmul(out, lhsT, rhs, start=start, stop=stop, tile_position=tp)


def TR(out, in_, ident):
    return lambda e: e.transpose(out, in_, ident)


def ACT(out, in_, func, bias=None, scale=None):
    kw = {}
    if bias is not None:
        kw['bias'] = bias
    if scale is not None:
        kw['scale'] = scale
    return lambda e: e.activation(out, in_, func, **kw)


def TT(out, a, b, op):
    return lambda e: e.tensor_tensor(out, a, b, op)


def TS2(out, a, s1, s2, op0, op1):
    return lambda e: e.tensor_scalar(out, a, s1, s2, op0, op1)


def TS1(out, a, s, op):
    return lambda e: e.tensor_single_scalar(out, a, s, op)


def STT(out, a, s, b, op0, op1):
    return lambda e: e.scalar_tensor_tensor(out, a, s, b, op0, op1)


def CP(out, a):
    return lambda e: e.tensor_copy(out, a)


def MS(out, v):
    return lambda e: e.memset(out, v)


def RCP(out, a):
    return lambda e: e.reciprocal(out, a)


class _Stop(Exception):
    pass


STAGE = os.environ.get('MK_STAGE', '')
ATT_LEVEL = int(os.environ.get('MK_ATT', '3'))
ATT_NQB = int(os.environ.get('MK_NQB', '16'))


def ckpt(name):
    if STAGE == name:
        raise _Stop()


class Arena:
    def __init__(self, ap, n):
        self.ap, self.n, self.top = ap, n, 0

    def f32(self, n):
        off = self.top
        self.top += n
        assert self.top <= self.n, ("arena overflow", self.top)
        return self.ap[:, off:off + n]

    def bf16(self, n):
        nf = (n + 1) // 2
        v = self.f32(nf).bitcast(BF16)
        return v[:, 0:n]

    def i32(self, n):
        return self.f32(n).bitcast(I32)


def v3(ap, a):
    return ap.rearrange("p (a b) -> p a b", a=a)


def v4(ap, a, b):
    return ap.rearrange("p (a b c) -> p a b c", a=a, b=b)


def v5(ap, a, b, c):
    return ap.rearrange("p (a b c d) -> p a b c d", a=a, b=b, c=c)


def tk(name, c0, c1):
    return [("%s_%d" % (name, i)) for i in range(c0 // 128, (c1 - 1) // 128 + 1)]


def build_program(layers, nseq=NSEQ, out_hc=False, dbg=None):
    nc = bass.Bass("TRN2", target_bir_lowering=False)
    es = ExitStack()
    dbg = dbg or []
    NL = len(layers)

    def din(name, shape, dt=F32):
        return nc.dram_tensor(name, list(shape), dt, kind="ExternalInput")

    x_t = din("x", [nseq, L, D])
    ctx_t = din("ctx", [nseq, LC, D])
    cc_t = din("cc", [nseq + 1, D])
    wada_t = din("w_ada", [DEPTH, D, 6 * D])
    bada_t = din("b_ada", [DEPTH, 6 * D])
    ng_t = din("norm_g", [DEPTH, 4, D])
    win_t = din("w_in", [DEPTH, D, 1792])
    convw_t = din("conv_w", [DEPTH, 3, 256])
    sink_t = din("attn_sink", [DEPTH, 8])
    lre_t = din("ssm_lam_re", [DEPTH, 2, 16, 64])
    lim_t = din("ssm_lam_im", [DEPTH, 2, 16, 64])
    ldt_t = din("ssm_log_dt", [DEPTH, 2, 16])
    bre_t = din("ssm_b_re", [DEPTH, 2, 16, 64, 16])
    bim_t = din("ssm_b_im", [DEPTH, 2, 16, 64, 16])
    cre_t = din("ssm_c_re", [DEPTH, 2, 16, 16, 64])
    cim_t = din("ssm_c_im", [DEPTH, 2, 16, 16, 64])
    sd_t = din("ssm_d", [DEPTH, 256])
    wglu_t = din("w_glu", [DEPTH, 256, 256])
    bglu_t = din("b_glu", [DEPTH, 256])
    wout_t = din("w_out", [DEPTH, D, D])
    w1_t = din("w_mlp_in", [DEPTH, D, 4 * D])
    w2_t = din("w_mlp_out", [DEPTH, 4 * D, D])
    ropec_t = din("ropec", [128, L])
    ropes_t = din("ropes", [128, L])
    maskp_t = din("maskp", [128, 128])
    maskn_t = din("maskn", [128, 128])
    ident_t = din("ident", [128, 128])
    out_t = nc.dram_tensor("out", [nseq, L, D], F32, kind="ExternalOutput")
    hc_t = nc.dram_tensor("hc_out", [nseq, LC, D], F32, kind="ExternalOutput") if out_hc else None
    win_s = nc.dram_tensor("win_s", [DEPTH, N_WIN, 128, KT, 128], BF16)
    wout_s = nc.dram_tensor("wout_s", [DEPTH, KT, 128, KT, 128], BF16)
    w1_s = nc.dram_tensor("w1_s", [DEPTH, 32, 128, KT, 128], BF16)
    w2_s = nc.dram_tensor("w2_s", [DEPTH, KT, 128, 32, 128], BF16)
    wglu_s = nc.dram_tensor("wglu_s", [DEPTH, 128, 2, 256], BF16)
    s5m_s = nc.dram_tensor("s5m_s", [DEPTH, 2, 128, 10240], BF16)
    s5k_s = nc.dram_tensor("s5k_s", [DEPTH, 2, 128, 216], F32)
    dbg_t = {}

    def dap(t, offset, ap):
        return bass.AP(tensor=t, offset=offset, ap=[list(a) for a in ap])

    with es:
        S = Sched(nc, es)
        arena_t = es.enter_context(nc.sbuf_tensor("arena", [128, ARENA_F32], F32))
        AR = Arena(arena_t[:, :], ARENA_F32)
        PS = [es.enter_context(nc.psum_tensor("ps%d" % i, [128, 512], F32)) for i in range(8)]
        psn = [0]

        def psum():
            i = psn[0] % 8
            psn[0] += 1
            return PS[i], "ps%d" % i

        def dump(name, ap, shape, key):
            if name not in dbg:
                return
            t = nc.dram_tensor("dbg_" + name, list(shape), ap.dtype, kind="ExternalOutput")
            dbg_t[name] = t
            S.dma('sp', t.ap(), ap, reads=key, writes=["dbg_" + name])

        identF = AR.f32(128)
        ones_bf = AR.bf16(128)
        maskp = AR.bf16(128)
        maskn = AR.bf16(128)
        vecs = AR.f32(256)
        mods = v5(AR.f32(DEPTH * 6 * KT * 5), DEPTH, 6, KT)
        sinkexp = AR.f32(32)
        dv = v4(AR.f32(2 * 6 * KT), 2, 6)
        smark = AR.top
        h = v3(AR.f32(KT * T), KT)
        A = v3(AR.bf16(KT * T), KT)
        gmark = AR.top
        AR.top = smark

        def hk(kt, c0, c1):
            return tk("h%d" % kt, c0, c1)

        def ak(kt, c0, c1):
            return tk("A%d" % kt, c0, c1)

        try:
            S.dma('sp', identF, ident_t.ap(), writes=['identF'])
            S.dma('pool', maskp, maskp_t.ap(), writes=['maskp'])
            S.dma('pool', maskn, maskn_t.ap(), writes=['maskn'])
            S.op('pool', MS(ones_bf, 1.0), writes=['ones_bf'])
            for sl_ in range(4):
                hh_ = (0, 2, 1, 3)[sl_]
                S.dma('sp', sinkexp[:, sl_:32:4], dap(sink_t, hh_, [[0, 128], [4, 8]]), writes=['sinkexp'], slow=True)
            S.op('act', ACT(sinkexp, sinkexp, AF.Exp), reads=['sinkexp'], writes=['sinkexp'])
            ckpt('c1')

            def wcast(dst_ap, src_ap, key):
                S.dma('pool', dst_ap, src_ap, writes=[key])

            def wcast_layer(l):
                wi = win_t.ap()
                ws = win_s.ap()

                def colsrc(c0):
                    return wi[l, :, c0:c0 + 128].rearrange("(kt p) c -> p kt c", p=128)

                def sw_cast(dst_tile, dcol0, c0, nblk):
                    for b_ in range(nblk):
                        for half in range(2):
                            sc = c0 + 32 * b_ + 16 * (1 - half)
                            dc = dcol0 + 32 * b_ + 16 * half
                            src = wi[l, :, sc:sc + 16].rearrange("(kt p) c -> p kt c", p=128)
                            wcast(dst_tile[:, :, dc:dc + 16], src, key)
                key = 'win_s%d' % l
                for i in range(4):
                    wcast(ws[l, O_Q + i], colsrc(128 * i), key)
                    sw_cast(ws[l, O_QS + i], 0, 128 * i, 4)
                for g in range(2):
                    c0 = 512 + 64 * g
                    for dup in range(2):
                        src = wi[l, :, c0:c0 + 64].rearrange("(kt p) c -> p kt c", p=128)
                        wcast(ws[l, O_K + g][:, :, dup * 64:(dup + 1) * 64], src, key)
                        sw_cast(ws[l, O_KS + g], dup * 64, c0, 2)
                for j, (o, c0) in enumerate([(O_CB, 768), (O_CB + 1, 896), (O_CC, 1024), (O_CC + 1, 1152),
                                             (O_CX, 1280), (O_CX + 1, 1408), (O_U, 1536), (O_U + 1, 1664), (O_V, 640)]):
                    wcast(ws[l, o], colsrc(c0), key)
                for m in range(KT):
                    wcast(wout_s.ap()[l, m], wout_t.ap()[l, :, m * 128:(m + 1) * 128].rearrange("(kt p) c -> p kt c", p=128),
                          'wout_s%d' % l)
                for j in range(32):
                    wcast(w1_s.ap()[l, j], w1_t.ap()[l, :, j * 128:(j + 1) * 128].rearrange("(kt p) c -> p kt c", p=128),
                          'w1_s%d' % l)
                for m in range(KT):
                    for jh in range(2):
                        wcast(w2_s.ap()[l, m][:, jh * 16:(jh + 1) * 16, :],
                              w2_t.ap()[l, jh * 2048:(jh + 1) * 2048, m * 128:(m + 1) * 128].rearrange("(j p) c -> p j c", p=128),
                              'w2_s%d' % l)
                wcast(wglu_s.ap()[l], wglu_t.ap()[l].rearrange("(kt p) c -> p kt c", p=128), 'wglu_s%d' % l)


            for l_ in layers:
                wcast_layer(l_)
            pm = AR.top
            st1 = AR.f32(128)
            st2 = AR.f32(128)
            S.op('dve', MS(st2, 0.0), writes=['st2'])
            S.dma('sp', st1, ng_t.ap().rearrange("l k (kt p) -> (l k kt) p", p=128), writes=['st1'])
            S.dma('sp', st2[0:24, :], convw_t.ap().rearrange("l k (hf p) -> (l k hf) p", p=128), writes=['st2'])
            S.dma('sp', st2[32:40, :], sd_t.ap().rearrange("l (hf p) -> (l hf) p", p=128), writes=['st2'])
            S.dma('sp', st2[64:72, :], bglu_t.ap().rearrange("l (hf p) -> (l hf) p", p=128), writes=['st2'])
            p0, k0 = psum()
            S.op('pe', TR(p0[:, 0:128], st1, identF), reads=['st1', 'identF'], writes=[k0])
            S.op('pe', TR(p0[:, 128:256], st2, identF), reads=['st2', 'identF'], writes=[k0])
            S.op('dve', CP(vecs, p0[:, 0:256]), reads=[k0], writes=['vecs'])
            ckpt('c2')

            def NG(l, k):
                return vecs[:, (l * 4 + k) * 8:(l * 4 + k) * 8 + 8]

            def CW(l, k, hf):
                c = 128 + (l * 3 + k) * 2 + hf
                return vecs[:, c:c + 1]

            def SD(l, hf):
                c = 160 + l * 2 + hf
                return vecs[:, c:c + 1]

            def BG(l, hf):
                c = 192 + l * 2 + hf
                return vecs[:, c:c + 1]

            R = nseq + 1
            ccs = AR.f32(D)
            cactT = v3(AR.f32(KT * R), KT)
            S.op('dve', MS(ccs, 0.0), writes=['ccs'])
            S.dma('sp', ccs[0:R, :], cc_t.ap(), writes=['ccs'])
            S.op('act', ACT(ccs, ccs, AF.Silu), reads=['ccs'], writes=['ccs'])
            for g4 in range(2):
                p0, k0 = psum()
                for j in range(4):
                    kt = g4 * 4 + j
                    S.op('pe', TR(p0[:, j * 128:(j + 1) * 128], ccs[:, kt * 128:(kt + 1) * 128], identF),
                         reads=['ccs', 'identF'], writes=[k0])
                S.op('dve', CP(cactT[:, g4 * 4:g4 * 4 + 4, :], v3(p0[:, :], 4)[:, :, 0:R]), reads=[k0], writes=['cactT'])
            ckpt('c3')
            badaT = AR.f32(DEPTH * 48)
            stb = AR.f32(128)
            S.op('dve', MS(stb, 0.0), writes=['stb'])
            for hb in range(2):
                S.dma('sp', stb[0:96, :], dap(bada_t, hb * 96 * 128, [[128, 96], [1, 128]]), writes=['stb'])
                p0, k0 = psum()
                S.op('pe', TR(p0[:, 0:128], stb, identF), reads=['stb', 'identF'], writes=[k0])
                S.op('dve', CP(badaT[:, hb * 96:(hb + 1) * 96], p0[:, 0:96]), reads=[k0], writes=['badaT'])
            wr = [v3(AR.f32(KT * 512), KT) for _ in range(2)]
            ckpt('c4')
            nld = 0
            for l in layers:
                pmod, kmod = psum()
                for cb in range(12):
                    slot = nld % 2
                    nld += 1
                    S.dma('sp', wr[slot], wada_t.ap()[l, :, cb * 512:(cb + 1) * 512].rearrange("(kt p) c -> p kt c", p=128),
                          writes=['wr%d' % slot])
                    for ct in range(4):
                        ctg = cb * 4 + ct
                        for kt in range(KT):
                            S.op('pe', MM(pmod[:, ctg * R:(ctg + 1) * R], wr[slot][:, kt, ct * 128:(ct + 1) * 128],
                                          cactT[:, kt, :], start=(kt == 0), stop=(kt == KT - 1)),
                                 reads=['wr%d' % slot, 'cactT'], writes=[kmod], inc=(kt == KT - 1))
                for m in range(6):
                    S.op('dve', TT(mods[:, l, m, :, 0:R], v3(pmod[:, m * 8 * R:(m + 1) * 8 * R], 8),
                                   badaT[:, l * 48 + m * 8:l * 48 + m * 8 + 8].unsqueeze(2).broadcast_to([128, 8, R]), ALU.add),
                         reads=[kmod, 'badaT'], writes=['mods'])
            S.barrier(skip_pool=True)
            AR.top = pm
            ckpt('mods')

            ckpt('wcast')
            def s5_precompute(l):
                m0 = AR.top
                f = AR.f32
                LRe, LIm, DT = f(16), f(16), f(16)
                BR, BI = v3(f(256), 16), v3(f(256), 16)
                CR, CI = v3(f(256), 16), v3(f(256), 16)
                for two in range(2):
                    ps_ = slice(64 * two, 64 * two + 64)
                    S.dma('sp', v3(LRe[ps_, :], 2), dap(lre_t, l * 2048 + two * 64, [[1, 64], [1024, 2], [128, 8]]),
                          writes=['LRe'], slow=True)
                    S.dma('sp', v3(LIm[ps_, :], 2), dap(lim_t, l * 2048 + two * 64, [[1, 64], [1024, 2], [128, 8]]),
                          writes=['LIm'], slow=True)
                    S.dma('sp', v3(DT[ps_, :], 2), dap(ldt_t, l * 32 + two, [[0, 64], [16, 2], [2, 8]]),
                          writes=['DT'], slow=True)
                    S.dma('sp', v4(BR[ps_, :, :].rearrange("p a b -> p (a b)"), 2, 8),
                          dap(bre_t, l * 32768 + two * 1024, [[16, 64], [16384, 2], [2048, 8], [1, 16]]), writes=['BR'])
                    S.dma('sp', v4(BI[ps_, :, :].rearrange("p a b -> p (a b)"), 2, 8),
                          dap(bim_t, l * 32768 + two * 1024, [[16, 64], [16384, 2], [2048, 8], [1, 16]]), writes=['BI'])
                CT = f(512)
                cst = f(128)
                S.op('dve', MS(cst, 0.0), writes=['cst'])
                for (src_t, dstC, nm) in ((cre_t, CR, 'CR'), (cim_t, CI, 'CI')):
                    for tq in range(4):
                        S.dma('sp', cst[:, 0:64], dap(src_t, l * 32768 + tq * 8192, [[64, 128], [1, 64]]), writes=['cst'])
                        p0, k0 = psum()
                        S.op('pe', TR(p0[:, 0:128], cst, identF), reads=['cst', 'identF'], writes=[k0])
                        S.op('dve', CP(CT[0:64, tq * 128:(tq + 1) * 128], p0[0:64, 0:128]), reads=[k0], writes=['CT'])
                    ctv = CT[0:64, :].rearrange("p (d gp two i) -> p d gp two i", d=2, gp=8, two=2)
                    for d_ in range(2):
                        S.op('dve', CP(dstC[0:64, d_ * 8:(d_ + 1) * 8, :], ctv[:, d_, :, 0, :]), reads=['CT'], writes=[nm])
                        S.op('dve', CP(dstC[64:128, d_ * 8:(d_ + 1) * 8, :], ctv[:, d_, :, 1, :]), reads=['CT'], writes=[nm])
                E = 'dve'

                def o1(fn, rd, wr_):
                    S.op(E, fn, reads=rd, writes=wr_)
                ar, ai, mag, sn, cs = f(16), f(16), f(16), f(16), f(16)
                t1, t2 = f(16), f(16)
                ki = AR.i32(16)
                S.op('act', ACT(DT, DT, AF.Exp), reads=['DT'], writes=['DT'])
                o1(TT(ar, LRe, DT, ALU.mult), ['LRe', 'DT'], ['ar'])
                o1(TT(ai, LIm, DT, ALU.mult), ['LIm', 'DT'], ['ai'])
                S.op('act', ACT(mag, ar, AF.Exp), reads=['ar'], writes=['mag'])
                for (dst, shift, nm) in ((sn, 0.0, 'sn'), (cs, PI / 2, 'cs')):
                    o1(TS1(t1, ai, shift, ALU.add), ['ai'], ['t1'])
                    o1(TS1(t2, t1, 1.0 / (2 * PI), ALU.mult), ['t1'], ['t2'])
                    o1(CP(ki, t2), ['t2'], ['ki'])
                    o1(CP(t2, ki), ['ki'], ['t2'])
                    o1(STT(t1, t2, -2 * PI, t1, ALU.mult, ALU.add), ['t1', 't2'], ['t1'])
                    o1(TS1(t2, t1, PI, ALU.is_gt), ['t1'], ['t2'])
                    o1(STT(t1, t2, -2 * PI, t1, ALU.mult, ALU.add), ['t1', 't2'], ['t1'])
                    o1(TS1(t2, t1, -PI, ALU.is_lt), ['t1'], ['t2'])
                    o1(STT(t1, t2, 2 * PI, t1, ALU.mult, ALU.add), ['t1', 't2'], ['t1'])
                    S.op('act', ACT(dst, t1, AF.Sin), reads=['t1'], writes=[nm])
                PWr, PWi = v3(f(9 * 16), 9), v3(f(9 * 16), 9)
                o1(MS(PWr[:, 0, :], 1.0), [], ['PW'])
                o1(MS(PWi[:, 0, :], 0.0), [], ['PW'])
                o1(TT(PWr[:, 1, :], mag, cs, ALU.mult), ['mag', 'cs'], ['PW'])
                o1(TT(PWi[:, 1, :], mag, sn, ALU.mult), ['mag', 'sn'], ['PW'])
                xr, den, cr, ci = f(16), f(16), f(16), f(16)
                o1(TS1(xr, PWr[:, 1, :], -1.0, ALU.add), ['PW'], ['xr'])
                o1(TT(den, LRe, LRe, ALU.mult), ['LRe'], ['den'])
                o1(TT(t1, LIm, LIm, ALU.mult), ['LIm'], ['t1'])
                o1(TT(den, den, t1, ALU.add), ['den', 't1'], ['den'])
                o1(RCP(den, den), ['den'], ['den'])
                o1(TT(cr, xr, LRe, ALU.mult), ['xr', 'LRe'], ['cr'])
                o1(TT(t1, PWi[:, 1, :], LIm, ALU.mult), ['PW', 'LIm'], ['t1'])
                o1(TT(cr, cr, t1, ALU.add), ['cr', 't1'], ['cr'])
                o1(TT(cr, cr, den, ALU.mult), ['cr', 'den'], ['cr'])
                o1(TT(ci, PWi[:, 1, :], LRe, ALU.mult), ['PW', 'LRe'], ['ci'])
                o1(TT(t1, xr, LIm, ALU.mult), ['xr', 'LIm'], ['t1'])
                o1(TT(ci, ci, t1, ALU.subtract), ['ci', 't1'], ['ci'])
                o1(TT(ci, ci, den, ALU.mult), ['ci', 'den'], ['ci'])
                for n in range(1, 8):
                    o1(TT(PWr[:, n + 1, :], PWr[:, n, :], PWr[:, 1, :], ALU.mult), ['PW'], ['PW'])
                    o1(TT(t1, PWi[:, n, :], PWi[:, 1, :], ALU.mult), ['PW'], ['t1'])
                    o1(TT(PWr[:, n + 1, :], PWr[:, n + 1, :], t1, ALU.subtract), ['PW', 't1'], ['PW'])
                    o1(TT(PWi[:, n + 1, :], PWr[:, n, :], PWi[:, 1, :], ALU.mult), ['PW'], ['PW'])
                    o1(TT(t1, PWi[:, n, :], PWr[:, 1, :], ALU.mult), ['PW'], ['t1'])
                    o1(TT(PWi[:, n + 1, :], PWi[:, n + 1, :], t1, ALU.add), ['PW', 't1'], ['PW'])
                Qr, Qi, Qn = v3(f(9 * 16), 9), v3(f(9 * 16), 9), v3(f(9 * 16), 9)
                o1(CP(Qr[:, 0, :], PWr[:, 8, :]), ['PW'], ['Q'])
                o1(CP(Qi[:, 0, :], PWi[:, 8, :]), ['PW'], ['Q'])
                for k in range(8):
                    o1(TT(Qr[:, k + 1, :], Qr[:, k, :], Qr[:, k, :], ALU.mult), ['Q'], ['Q'])
                    o1(TT(t1, Qi[:, k, :], Qi[:, k, :], ALU.mult), ['Q'], ['t1'])
                    o1(TT(Qr[:, k + 1, :], Qr[:, k + 1, :], t1, ALU.subtract), ['Q', 't1'], ['Q'])
                    o1(TT(t1, Qr[:, k, :], Qi[:, k, :], ALU.mult), ['Q'], ['t1'])
                    o1(TS1(Qi[:, k + 1, :], t1, 2.0, ALU.mult), ['t1'], ['Q'])
                o1(TS1(Qn, Qi, -1.0, ALU.mult), ['Q'], ['Q'])
                KSc = v5(f(2 * 2 * 4 * 9 * 3), 2, 2, 4)
                for hf in range(2):
                    for j, Qx in enumerate((Qr, Qi, Qn)):
                        src = Qx.rearrange("p k (d hf gpl) -> p d hf gpl k", d=2, hf=2)[:, :, hf, :, :]
                        dst = KSc[:, hf, :, :, :].rearrange("p d gpl (k j) -> p d gpl k j", j=3)[:, :, :, :, j]
                        o1(CP(dst, src), ['Q'], ['KSc'])
                    S.dma('sp', s5k_s.ap()[l, hf], KSc[:, hf, :, :, :].rearrange("p d gpl x -> p (d gpl x)"),
                          reads=['KSc'], writes=['s5k_s%d' % l])
                Bbr, Bbi, tB = v3(f(256), 16), v3(f(256), 16), v3(f(256), 16)

                def bc16(v):
                    return v.unsqueeze(2).broadcast_to([128, 16, 16])
                o1(TT(Bbr, BR, bc16(cr), ALU.mult), ['BR', 'cr'], ['Bbr'])
                o1(TT(tB, BI, bc16(ci), ALU.mult), ['BI', 'ci'], ['tB'])
                o1(TT(Bbr, Bbr, tB, ALU.subtract), ['Bbr', 'tB'], ['Bbr'])
                o1(TT(Bbi, BI, bc16(cr), ALU.mult), ['BI', 'cr'], ['Bbi'])
                o1(TT(tB, BR, bc16(ci), ALU.mult), ['BR', 'ci'], ['tB'])
                o1(TT(Bbi, Bbi, tB, ALU.add), ['Bbi', 'tB'], ['Bbi'])
                EBr, EBi = v3(f(2048), 16), v3(f(2048), 16)
                EWr, EWi = v3(f(2048), 16), v3(f(2048), 16)
                for (Ex, nm) in ((EBr, 'EBr'), (EBi, 'EBi'), (EWr, 'EWr'), (EWi, 'EWi')):
                    S.op('dve', MS(Ex, 0.0), writes=[nm])

                def expand(Ex, val, nm, vnm):
                    ev = Ex.rearrange("p (d hf gpl) c -> p d hf gpl c", d=2, hf=2)
                    vv = val.rearrange("p (d hf gpl) j -> p d hf gpl j", d=2, hf=2)
                    for two in range(2):
                        ps_ = slice(64 * two, 64 * two + 64)
                        for gpl in range(4):
                            for d_ in range(2):
                                o1(CP(ev[ps_, d_, :, gpl, 32 * gpl + 16 * two:32 * gpl + 16 * two + 16], vv[ps_, d_, :, gpl, :]),
                                   [vnm], [nm])
                expand(EBr, Bbr, 'EBr', 'Bbr')
                expand(EBi, Bbi, 'EBi', 'Bbi')
                WUP = [v4(AR.bf16(4096), 2, 2) .rearrange("p d part (s c) -> p d part s c", s=8) for _ in range(2)]
                WDN = [AR.bf16(4096) for _ in range(2)]
                KM = [v3(AR.bf16(2048), 2).rearrange("p d (t c) -> p d t c", t=8) for _ in range(2)]
                for hf in range(2):
                    S.op('dve', MS(WDN[hf], 0.0), writes=['WDN%d' % hf])
                WDNv = [w.rearrange("p (d gpl part n c) -> p d gpl part n c", d=2, gpl=4, part=2, n=8) for w in WDN]
                Wre, Wim, BPr, BPi, tW = (v3(f(256), 16) for _ in range(5))
                WUT = v3(f(8 * 128), 8)
                for (nm,) in (('WUT',),):
                    S.op('dve', MS(WUT, 0.0), writes=[nm])
                for n in range(9):
                    prn, pin = bc16(PWr[:, n, :]), bc16(PWi[:, n, :])
                    o1(TT(Wre, CR, prn, ALU.mult), ['CR', 'PW'], ['Wre'])
                    o1(TT(tW, CI, pin, ALU.mult), ['CI', 'PW'], ['tW'])
                    o1(TT(Wre, Wre, tW, ALU.subtract), ['Wre', 'tW'], ['Wre'])
                    o1(TT(Wim, CR, pin, ALU.mult), ['CR', 'PW'], ['Wim'])
                    o1(TT(tW, CI, prn, ALU.mult), ['CI', 'PW'], ['tW'])
                    o1(TT(Wim, Wim, tW, ALU.add), ['Wim', 'tW'], ['Wim'])
                    o1(TS1(Wim, Wim, -1.0, ALU.mult), ['Wim'], ['Wim'])
                    if n >= 1:
                        for part, Wx, wnm in ((0, Wre, 'Wre'), (1, Wim, 'Wim')):
                            wv = Wx.rearrange("p (d hf gpl) i -> p d hf gpl i", d=2, hf=2)
                            for hf in range(2):
                                for two in range(2):
                                    ps_ = slice(64 * two, 64 * two + 64)
                                    o1(CP(WDNv[hf][ps_, :, :, part, n - 1, 16 * two:16 * two + 16], wv[ps_, :, hf, :, :]),
                                       [wnm], ['WDN%d' % hf])
                    if n <= 7:
                        expand(EWr, Wre, 'EWr', 'Wre')
                        expand(EWi, Wim, 'EWi', 'Wim')
                        o1(TT(BPr, Bbr, prn, ALU.mult), ['Bbr', 'PW'], ['BPr'])
                        o1(TT(tW, Bbi, pin, ALU.mult), ['Bbi', 'PW'], ['tW'])
                        o1(TT(BPr, BPr, tW, ALU.subtract), ['BPr', 'tW'], ['BPr'])
                        o1(TT(BPi, Bbr, pin, ALU.mult), ['Bbr', 'PW'], ['BPi'])
                        o1(TT(tW, Bbi, prn, ALU.mult), ['Bbi', 'PW'], ['tW'])
                        o1(TT(BPi, BPi, tW, ALU.add), ['BPi', 'tW'], ['BPi'])
                        wut = WUT.rearrange("p (d hf part) (gpl c) -> p d hf part gpl c", d=2, hf=2, gpl=4)
                        for part, Bx, bnm in ((0, BPr, 'BPr'), (1, BPi, 'BPi')):
                            bv = Bx.rearrange("p (d hf gpl) j -> p d hf gpl j", d=2, hf=2)
                            for two in range(2):
                                ps_ = slice(64 * two, 64 * two + 64)
                                for d_ in range(2):
                                    o1(CP(wut[ps_, d_, :, part, :, 16 * two:16 * two + 16], bv[ps_, d_, :, :, :]),
                                       [bnm], ['WUT'])
                        for d_ in range(2):
                            s_idx = (7 - n) if d_ == 0 else n
                            for hf in range(2):
                                p0, k0 = psum()
                                for part in range(2):
                                    idx = (d_ * 2 + hf) * 2 + part
                                    S.op('pe', TR(p0[:, part * 128:(part + 1) * 128], WUT[:, idx, :], identF),
                                         reads=['WUT', 'identF'], writes=[k0])
                                S.op('act', ACT(WUP[hf][:, d_, :, s_idx, :], v3(p0[:, 0:256], 2), AF.Copy),
                                     reads=[k0], writes=['WUP%d' % hf])
                        for d_ in range(2):
                            for hf in range(2):
                                p0, k0 = psum()
                                cnt = 0
                                for gpl in range(4):
                                    cidx = d_ * 8 + hf * 4 + gpl
                                    for (Eb, Ew, bn, wn) in ((EBr, EWr, 'EBr', 'EWr'), (EBi, EWi, 'EBi', 'EWi')):
                                        S.op('pe', MM(p0[:, 0:128], Eb[:, cidx, :], Ew[:, cidx, :], start=(cnt == 0), stop=(cnt == 7)),
                                             reads=[bn, wn], writes=[k0], inc=(cnt == 7))
                                        cnt += 1
                                S.op('act', ACT(KM[hf][:, d_, n, :], p0[:, 0:128], AF.Copy), reads=[k0], writes=['KM%d' % hf])
                for hf in range(2):
                    dst = s5m_s.ap()[l, hf]
                    S.dma('sp', dst[:, 0:4096], WUP[hf].rearrange("p d part s c -> p (d part s c)"),
                          reads=['WUP%d' % hf], writes=['s5m_s%d' % l])
                    S.dma('sp', dst[:, 4096:8192], WDN[hf], reads=['WDN%d' % hf], writes=['s5m_s%d' % l])
                    S.dma('sp', dst[:, 8192:10240], KM[hf].rearrange("p d t c -> p (d t c)"),
                          reads=['KM%d' % hf], writes=['s5m_s%d' % l])
                S.barrier(skip_pool=True)
                AR.top = m0

            for l in layers:
                s5_precompute(l)
            assert AR.top == smark
            AR.top = gmark
            ckpt('s5pre')

            ring = {}

            def load_w(name, slots, shape_ap_fn, src_ap, skey):
                st = ring[name]
                i = st['n'] % len(st['slots'])
                st['n'] += 1
                ap = st['slots'][i]
                key = '%s_%d' % (name, i)
                S.dma('sp', ap, src_ap, reads=[skey], writes=[key])
                return ap, key

            def mk_ring(name, nslots, nelem_bf16, shaper):
                ring[name] = {'n': 0, 'slots': [shaper(AR.bf16(nelem_bf16)) for _ in range(nslots)]}

            def load_seq(s):
                m0 = AR.top
                stg = [AR.f32(D) for _ in range(2)]
                for tt in range(T // 128):
                    sl = tt % 2
                    if tt < 2:
                        src = ctx_t.ap()[s, tt * 128:(tt + 1) * 128, :]
                    else:
                        src = x_t.ap()[s, (tt - 2) * 128:(tt - 1) * 128, :]
                    S.dma('sp', stg[sl], src, writes=['stg%d' % sl])
                    for g4 in range(2):
                        p0, k0 = psum()
                        for j in range(4):
                            kt = g4 * 4 + j
                            S.op('pe', TR(p0[:, j * 128:(j + 1) * 128], stg[sl][:, kt * 128:(kt + 1) * 128], identF),
                                 reads=['stg%d' % sl, 'identF'], writes=[k0])
                        wk = []
                        for j in range(4):
                            wk += hk(g4 * 4 + j, tt * 128, tt * 128 + 128)
                        eng = 'act' if (g4 == 0) else 'dve'
                        if eng == 'act':
                            S.op('act', ACT(h[:, g4 * 4:g4 * 4 + 4, tt * 128:(tt + 1) * 128], v3(p0[:, :], 4), AF.Copy),
                                 reads=[k0], writes=wk)
                        else:
                            S.op('dve', CP(h[:, g4 * 4:g4 * 4 + 4, tt * 128:(tt + 1) * 128], v3(p0[:, :], 4)), reads=[k0], writes=wk)
                S.barrier()
                AR.top = m0

            def store_seq(s):
                m0 = AR.top
                stg = [AR.f32(D) for _ in range(2)]
                tts = list(range(2, T // 128)) + (list(range(2)) if out_hc else [])
                for n_, tt in enumerate(tts):
                    sl = n_ % 2
                    for g4 in range(2):
                        p0, k0 = psum()
                        rk = []
                        for j in range(4):
                            kt = g4 * 4 + j
                            S.op('pe', TR(p0[:, j * 128:(j + 1) * 128], h[:, kt, tt * 128:(tt + 1) * 128], identF),
                                 reads=hk(kt, tt * 128, tt * 128 + 128) + ['identF'], writes=[k0])
                        if g4 == 0:
                            S.op('act', ACT(stg[sl][:, 0:512], p0[:, :], AF.Copy), reads=[k0], writes=['ostg%d' % sl])
                        else:
                            S.op('dve', CP(stg[sl][:, 512:1024], p0[:, :]), reads=[k0], writes=['ostg%d' % sl])
                    if tt >= 2:
                        dst = out_t.ap()[s, (tt - 2) * 128:(tt - 1) * 128, :]
                    else:
                        dst = hc_t.ap()[s, tt * 128:(tt + 1) * 128, :]
                    S.dma('sp', dst, stg[sl], reads=['ostg%d' % sl], writes=['outd'], key='ostg%d' % sl)
                S.barrier()
                AR.top = m0

            def rms_stats(src_fn, src_keys_fn, n, tmp_sq, eng_sq='pool'):
                pss, kss = psum()
                for kt in range(KT):
                    sq, sqk = tmp_sq[kt % 2]
                    S.op(eng_sq, TT(sq[:, 0:n], src_fn(kt), src_fn(kt), ALU.mult), reads=src_keys_fn(kt), writes=[sqk])
                    S.op('pe', MM(pss[:, 0:n], ones_bf, sq[:, 0:n], start=(kt == 0), stop=(kt == KT - 1)),
                         reads=[sqk, 'ones_bf'], writes=[kss])
                return pss, kss

            def rstd_from(pss, kss, n, tmp, tmpk, rstd, rstdk):
                S.op('act', ACT(tmp[:, 0:n], pss[:, 0:n], AF.Sqrt, bias=EPS, scale=1.0 / D), reads=[kss], writes=[tmpk])
                S.op('dve', RCP(rstd[:, 0:n], tmp[:, 0:n]), reads=[tmpk], writes=[rstdk])

            def mk_scratch():
                sc = {'sq': [(AR.bf16(512), 'scq%d' % i) for i in range(2)],
                      'f': [(AR.f32(512), 'scf%d' % i) for i in range(5)]}
                return sc

            def norm_block(bi, i_scale, i_shift, sc, eng_sq='pool'):
                sqs = sc['sq']
                tmp, tmpk = sc['f'][2]
                rstds = [sc['f'][3], sc['f'][4]]
                tts = [sc['f'][0], sc['f'][1]]
                (c0, c1, w) = BLOCKS[bi]
                n = c1 - c0
                pss, kss = rms_stats(lambda kt: h[:, kt, c0:c1], lambda kt: hk(kt, c0, c1), n, sqs, eng_sq=eng_sq)
                rstd, rk = rstds[bi % 2]
                rstd_from(pss, kss, n, tmp, tmpk, rstd, rk)
                for kt in range(KT):
                    t_, tkk = tts[kt % 2]
                    S.op('dve', TT(t_[:, 0:n], h[:, kt, c0:c1], rstd[:, 0:n], ALU.mult), reads=hk(kt, c0, c1) + [rk], writes=[tkk])
                    S.op('act', ACT(A[:, kt, c0:c1], t_[:, 0:n], AF.Identity, bias=dv[:, w, i_shift, kt:kt + 1],
                                    scale=dv[:, w, i_scale, kt:kt + 1]), reads=[tkk, 'dv'], writes=ak(kt, c0, c1))

            def norm_to_A(i_scale, i_shift, sc):
                for bi in range(len(BLOCKS)):
                    norm_block(bi, i_scale, i_shift, sc)

            def calc_dv(l, s, idxs):
                for w, r in ((0, s), (1, nseq)):
                    M = lambda m: mods[:, l, m, :, r]
                    if 0 in idxs:
                        S.op('dve', STT(dv[:, w, 0, :], M(1), 1.0, NG(l, 0), ALU.add, ALU.mult), reads=['mods', 'vecs', 'dv'], writes=['dv'])
                        S.op('dve', CP(dv[:, w, 1, :], M(0)), reads=['mods', 'dv'], writes=['dv'])
                    if 2 in idxs:
                        S.op('dve', TT(dv[:, w, 2, :], M(2), NG(l, 1), ALU.mult), reads=['mods', 'vecs', 'dv'], writes=['dv'])
                        S.op('dve', STT(dv[:, w, 3, :], M(4), 1.0, NG(l, 2), ALU.add, ALU.mult), reads=['mods', 'vecs', 'dv'], writes=['dv'])
                        S.op('dve', CP(dv[:, w, 4, :], M(3)), reads=['mods', 'dv'], writes=['dv'])
                        S.op('dve', TT(dv[:, w, 5, :], M(5), NG(l, 3), ALU.mult), reads=['mods', 'vecs', 'dv'], writes=['dv'])

            def proj_mm(wt, wkey, c0, c1):
                pp, pk = psum()
                n = c1 - c0
                for kt in range(KT):
                    S.op('pe', MM(pp[:, 0:n], wt[:, kt, :], A[:, kt, c0:c1], start=(kt == 0), stop=(kt == KT - 1)),
                         reads=[wkey] + ak(kt, c0, c1), writes=[pk], inc=(kt == KT - 1))
                return pp, pk

            def win_tile(l, o):
                return load_w('win', None, None, win_s.ap()[l, o], 'win_s%d' % l)

            def layer(l, s, pre_normed, l_next):
                lm0 = AR.top
                calc_dv(l, s, (2,) if pre_normed else (0, 2))
                sso = v3(AR.bf16(2 * T), 2)
                pm0 = AR.top
                sc = mk_scratch()
                if not pre_normed:
                    norm_to_A(0, 1, sc)
                dump('a', A[:, :, :], [128, KT, T], sum([ak(kt, 0, T) for kt in range(KT)], []))
                ckpt('norm')
                mk_ring('win', 2, KT * 128, lambda a: v3(a, KT))
                u = v3(AR.bf16(2 * T), 2)
                for hf in range(2):
                    wt, wk = win_tile(l, O_U + hf)
                    for (c0, c1, w) in BLOCKS:
                        pp, pk = proj_mm(wt, wk, c0, c1)
                        S.op('act', ACT(u[:, hf, c0:c1], pp[:, 0:c1 - c0], AF.Copy), reads=[pk], writes=tk('u%d' % hf, c0, c1))
                dump('u', u[:, :, :], [128, 2, T], tk('u0', 0, T) + tk('u1', 0, T))
                ckpt('uproj')
                mats = AR.bf16(10240)
                ksc = AR.f32(216)
                P0 = [v3(AR.f32(2 * 290), 2) for _ in range(4)]
                P1 = [v3(AR.f32(2 * 290), 2) for _ in range(4)]
                Xb = [[v3(AR.bf16(2 * 290), 2) for _ in range(4)] for _ in range(2)]
                KT1 = [v3(AR.f32(2 * 290), 2) for _ in range(1)]
                KT2 = [v3(AR.f32(2 * 290), 2) for _ in range(1)]
                yv = [sc['f'][0], sc['f'][1]]
                (g1, g1k), (g2, g2k), (g3, g3k) = sc['f'][2], sc['f'][3], sc['f'][4]
                WUPv = mats[:, 0:4096].rearrange("p (d part s c) -> p d part s c", d=2, part=2, s=8)
                WDNv = mats[:, 4096:8192].rearrange("p (d gpl part n c) -> p d gpl part n c", d=2, gpl=4, part=2, n=8)
                KMv = mats[:, 8192:10240].rearrange("p (d t c) -> p d t c", d=2, t=8)
                kscv = ksc.rearrange("p (d gpl k j) -> p d gpl k j", d=2, gpl=4, k=9)
                NCH = T // 8
                for hf in range(2):
                    S.dma('sp', mats, s5m_s.ap()[l, hf], reads=['s5m_s%d' % l], writes=['mats'])
                    S.dma('sp', ksc, s5k_s.ap()[l, hf], reads=['s5k_s%d' % l], writes=['ksc'])
                    for d_ in range(2):
                        chains = []
                        for gpl in range(4):
                            chain = []
                            xk = 'X%d' % gpl
                            xbk = 'Xb%d_%d' % (d_, gpl)
                            ke = 'dve'
                            S.op(ke, MS(Xb[d_][gpl][:, :, 0:1] if d_ == 0 else Xb[d_][gpl][:, :, 32:33], 0.0), writes=[xbk])
                            for part in range(2):
                                pp, pk = psum()
                                for s_ in range(8):
                                    S.op('pe', MM(pp[:, 0:NCH], WUPv[32 * gpl:32 * gpl + 32, d_, part, s_, :],
                                                  u[32 * gpl:32 * gpl + 32, hf, s_:T:8], start=(s_ == 0), stop=(s_ == 7),
                                                  tp=(32 * gpl, 0)),
                                         reads=['mats'] + tk('u%d' % hf, 0, T), writes=[pk], inc=(s_ == 7))
                                if d_ == 0:
                                    S.op('act', ACT(P0[gpl][:, part, 1:289], pp[:, 0:288], AF.Copy), reads=[pk], writes=[xk])
                                else:
                                    S.op('act', ACT(P0[gpl][:, part, 0:32], pp[:, 0:32], AF.Copy), reads=[pk], writes=[xk])
                                    S.op('act', ACT(P0[gpl][:, part, 33:289], pp[:, 32:288], AF.Copy), reads=[pk], writes=[xk])

                            def ks_run(lo, W, nlev, right, final_bf, src, dst):
                                for k in range(nlev):
                                    sh = 1 << k
                                    n = W - sh
                                    last = (k == nlev - 1)
                                    a_, b_, nb_ = kscv[:, d_, gpl, k, 0:1], kscv[:, d_, gpl, k, 1:2], kscv[:, d_, gpl, k, 2:3]
                                    Dd = Xb[d_][gpl] if (last and final_bf) else dst
                                    wkeys = [xk] + ([xbk] if (last and final_bf) else [])
                                    if right:
                                        so_, do_ = slice(lo, lo + n), slice(lo + sh, lo + W)
                                        ho_ = slice(lo, lo + sh)
                                    else:
                                        so_, do_ = slice(lo + sh, lo + W), slice(lo, lo + n)
                                        ho_ = slice(lo + n, lo + W)
                                    if gpl < 3:
                                        chain.append(('dve', STT(dst[:, :, do_], src[:, :, so_], a_, src[:, :, do_], ALU.mult, ALU.add), [xk, 'ksc'], wkeys))
                                        chain.append(('dve', STT(Dd[:, 0, do_], src[:, 1, so_], nb_, dst[:, 0, do_], ALU.mult, ALU.add), [xk, 'ksc'], wkeys))
                                        chain.append(('dve', STT(Dd[:, 1, do_], src[:, 0, so_], b_, dst[:, 1, do_], ALU.mult, ALU.add), [xk, 'ksc'], wkeys))
                                        chain.append(('dve', CP(Dd[:, :, ho_], src[:, :, ho_]), [xk], wkeys))
                                    else:
                                        t1_, t1k = KT1[0], 'kt1_%d' % gpl
                                        t2_, t2k = KT2[0], 'kt2_%d' % gpl
                                        chain.append(('act', ACT(t1_[:, :, 0:n], src[:, :, so_], AF.Identity, scale=a_), [xk, 'ksc'], [t1k]))
                                        chain.append(('act', ACT(t2_[:, 0, 0:n], src[:, 1, so_], AF.Identity, scale=nb_), [xk, 'ksc'], [t2k]))
                                        chain.append(('act', ACT(t2_[:, 1, 0:n], src[:, 0, so_], AF.Identity, scale=b_), [xk, 'ksc'], [t2k]))
                                        chain.append(('pool', TT(dst[:, :, do_], t1_[:, :, 0:n], src[:, :, do_], ALU.add), [xk, t1k], wkeys))
                                        chain.append(('pool', TT(Dd[:, :, do_], t2_[:, :, 0:n], dst[:, :, do_], ALU.add), [xk, t2k], wkeys))
                                        chain.append(('pool', CP(Dd[:, :, ho_], src[:, :, ho_]), [xk], wkeys))
                                    src, dst = dst, src
                                return src
                            if d_ == 0:
                                ks_run(1, 288, 9, True, True, P0[gpl], P1[gpl])
                            else:
                                res = ks_run(0, 32, 5, False, False, P0[gpl], P1[gpl])
                                ce_ = 'dve' if gpl < 3 else 'pool'
                                chain.append((ce_, CP(Xb[d_][gpl][:, :, 0:32], res[:, :, 0:32]), [xk], [xk, xbk]))
                                chain.append((ce_, CP(P0[gpl][:, :, 289:290], res[:, :, 0:1]), [xk], [xk]))
                                ks_run(33, 257, 9, False, True, P0[gpl], P1[gpl])
                            chains.append(chain)
                        for i_ in range(max(len(c) for c in chains)):
                            for c in chains:
                                if i_ < len(c):
                                    S.op(c[i_][0], c[i_][1], reads=c[i_][2], writes=c[i_][3])
                    for bi, (c0, c1, w) in enumerate(BLOCKS):
                        n = c1 - c0
                        nch = n // 8
                        ch0 = c0 // 8
                        py, pyk = psum()
                        for s_ in range(8):
                            ops = []
                            for d_ in range(2):
                                srange = range(0, s_ + 1) if d_ == 0 else range(s_, 8)
                                for sp_ in srange:
                                    ops.append((py[:, s_:n:8], KMv[:, d_, abs(s_ - sp_), :], u[:, hf, c0 + sp_:c1:8], None,
                                                ['mats'] + tk('u%d' % hf, c0, c1), False))
                                nidx = s_ if d_ == 0 else 7 - s_
                                e0 = ch0 if d_ == 0 else (ch0 + 1 if w else ch0 + 2)
                                for gpl in range(4):
                                    for part in range(2):
                                        ops.append((py[32 * gpl:32 * gpl + 32, s_:n:8], WDNv[:, d_, gpl, part, nidx, :],
                                                    Xb[d_][gpl][:, part, e0:e0 + nch], (0, 32 * gpl),
                                                    ['mats', 'Xb%d_%d' % (d_, gpl)], (d_ == 1 and part == 1)))
                            for i_, (o_, l_, r_, tp_, rd_, st_) in enumerate(ops):
                                S.op('pe', MM(o_, l_, r_, start=(i_ == 0), stop=st_, tp=tp_),
                                     reads=rd_, writes=[pyk], inc=(i_ == len(ops) - 1))
                        y_, yk = yv[bi % 2]
                        S.op('dve', STT(y_[:, 0:n], u[:, hf, c0:c1], SD(l, hf), py[:, 0:n], ALU.mult, ALU.add),
                             reads=[pyk, 'vecs'] + tk('u%d' % hf, c0, c1), writes=[yk])
                        S.op('pool', TT(g1[:, 0:n], y_[:, 0:n], y_[:, 0:n], ALU.mult), reads=[yk], writes=[g1k])
                        S.op('pool', TS2(g1[:, 0:n], g1[:, 0:n], 0.044715, 1.0, ALU.mult, ALU.add), reads=[g1k], writes=[g1k])
                        S.op('pool', TT(g2[:, 0:n], g1[:, 0:n], y_[:, 0:n], ALU.mult), reads=[g1k, yk], writes=[g2k])
                        S.op('act', ACT(g3[:, 0:n], g2[:, 0:n], AF.Sigmoid, scale=1.5957691216057308), reads=[g2k], writes=[g3k])
                        S.op('dve', TT(sso[:, hf, c0:c1], y_[:, 0:n], g3[:, 0:n], ALU.mult), reads=[yk, g3k],
                             writes=tk('sso%d' % hf, c0, c1))
                dump('g', sso[:, :, :], [128, 2, T], tk('sso0', 0, T) + tk('sso1', 0, T))
                wg = v3(AR.bf16(512), 2)
                S.dma('sp', wg, wglu_s.ap()[l], reads=['wglu_s%d' % l], writes=['wg'])
                for bi, (c0, c1, w) in enumerate(BLOCKS):
                    n = c1 - c0
                    zs = []
                    for ho in range(2):
                        pz, pzk = psum()
                        for hf in range(2):
                            S.op('pe', MM(pz[:, 0:n], wg[:, hf, ho * 128:(ho + 1) * 128], sso[:, hf, c0:c1], start=(hf == 0), stop=(hf == 1)),
                                 reads=['wg'] + tk('sso%d' % hf, c0, c1), writes=[pzk], inc=(hf == 1))
                        zs.append((pz, pzk))
                    for ho in range(2):
                        pz, pzk = zs[ho]
                        gt, gk = (g1, g1k) if ho == 0 else (g2, g2k)
                        S.op('act', ACT(gt[:, 0:n], pz[:, 0:n], AF.Sigmoid, bias=BG(l, ho)), reads=[pzk, 'vecs'], writes=[gk])
                        S.op('pool', TT(sso[:, ho, c0:c1], sso[:, ho, c0:c1], gt[:, 0:n], ALU.mult),
                             reads=[gk] + tk('sso%d' % ho, c0, c1), writes=tk('sso%d' % ho, c0, c1))
                dump('ssm', sso[:, :, :], [128, 2, T], tk('sso0', 0, T) + tk('sso1', 0, T))
                ckpt('s5')
                S.barrier()
                AR.top = pm0
                cvo = v3(AR.bf16(2 * T), 2)
                pm1 = AR.top
                mk_ring('win', 3, KT * 128, lambda a: v3(a, KT))
                CW_ = T + 4
                ccx = v3(AR.bf16(2 * CW_), 2)
                cb = v3(AR.bf16(2 * T), 2)
                cct = [AR.bf16(512) for _ in range(2)]
                o1t = [AR.f32(512) for _ in range(2)]
                for col in (0, 257, 258, CW_ - 1):
                    S.op('pool', MS(ccx[:, :, col:col + 1], 0.0), writes=['ccxpad'])

                def coff(c0):
                    return c0 + 1 if c0 < LC else c0 + 3
                for hf in range(2):
                    wcb, kcb = win_tile(l, O_CB + hf)
                    wcc, kcc = win_tile(l, O_CC + hf)
                    wcx, kcx = win_tile(l, O_CX + hf)
                    for bi, (c0, c1, w) in enumerate(BLOCKS):
                        n = c1 - c0
                        pp, pk = proj_mm(wcb, kcb, c0, c1)
                        S.op('act', ACT(cb[:, hf, c0:c1], pp[:, 0:n], AF.Copy), reads=[pk], writes=tk('cb%d' % hf, c0, c1))
                        pp, pk = proj_mm(wcc, kcc, c0, c1)
                        ct_, ctk = cct[bi % 2], 'cct%d' % (bi % 2)
                        S.op('act', ACT(ct_[:, 0:n], pp[:, 0:n], AF.Copy), reads=[pk], writes=[ctk])
                        pp, pk = proj_mm(wcx, kcx, c0, c1)
                        S.op('dve', TT(ccx[:, hf, coff(c0):coff(c0) + n], pp[:, 0:n], ct_[:, 0:n], ALU.mult), reads=[pk, ctk],
                             writes=['ccx%d' % hf])
                for hf in range(2):
                    for bi, (c0, c1, w) in enumerate(BLOCKS):
                        n = c1 - c0
                        b0 = coff(c0)
                        ot, otk = o1t[bi % 2], 'o1t%d' % (bi % 2)
                        S.op('pool', TS1(ot[:, 0:n], ccx[:, hf, b0 - 1:b0 - 1 + n], CW(l, 0, hf), ALU.mult),
                             reads=['ccx%d' % hf, 'ccxpad', 'vecs'], writes=[otk])
                        S.op('dve', STT(ot[:, 0:n], ccx[:, hf, b0:b0 + n], CW(l, 1, hf), ot[:, 0:n], ALU.mult, ALU.add),
                             reads=['ccx%d' % hf, 'vecs', otk], writes=[otk])
                        S.op('dve', STT(ot[:, 0:n], ccx[:, hf, b0 + 1:b0 + 1 + n], CW(l, 2, hf), ot[:, 0:n], ALU.mult, ALU.add),
                             reads=['ccx%d' % hf, 'ccxpad', 'vecs', otk], writes=[otk])
                        S.op('dve', TT(cvo[:, hf, c0:c1], ot[:, 0:n], cb[:, hf, c0:c1], ALU.mult),
                             reads=[otk] + tk('cb%d' % hf, c0, c1), writes=tk('cvo%d' % hf, c0, c1))
                dump('conv', cvo[:, :, :], [128, 2, T], tk('cvo0', 0, T) + tk('cvo1', 0, T))
                ckpt('conv')
                S.barrier()
                AR.top = pm1
                q = v3(AR.bf16(4 * T), 4)
                kd = v3(AR.bf16(2 * T), 2)
                Vt = v4(AR.bf16(18 * 2 * 128), 18, 2)
                pm2 = AR.top
                mk_ring('win', 4, KT * 128, lambda a: v3(a, KT))
                ropec = AR.f32(L)
                ropes = AR.f32(L)
                S.dma('sp', ropec, ropec_t.ap(), writes=['ropec'])
                S.dma('sp', ropes, ropes_t.ap(), writes=['ropes'])
                rt = [AR.f32(512) for _ in range(2)]
                for (dst, dnm, o_pl, o_sw, cnt) in ((q, 'q', O_Q, O_QS, 4), (kd, 'kd', O_K, O_KS, 2)):
                    for i in range(cnt):
                        wp, kp = win_tile(l, o_pl + i)
                        wsw, ksw = win_tile(l, o_sw + i)
                        for (c0, c1, w) in BLOCKS:
                            n = c1 - c0
                            pp, pk = proj_mm(wp, kp, c0, c1)
                            wkeys = tk('%s%d' % (dnm, i), c0, c1)
                            if w:
                                S.op('act', ACT(dst[:, i, c0:c1], pp[:, 0:n], AF.Copy), reads=[pk], writes=wkeys)
                            else:
                                p2, pk2 = proj_mm(wsw, ksw, c0, c1)
                                lc0 = c0 - LC
                                S.op('dve', TT(rt[0][:, 0:n], pp[:, 0:n], ropec[:, lc0:lc0 + n], ALU.mult), reads=[pk, 'ropec'], writes=['rt0'])
                                S.op('dve', TT(rt[1][:, 0:n], p2[:, 0:n], ropes[:, lc0:lc0 + n], ALU.mult), reads=[pk2, 'ropes'], writes=['rt1'])
                                S.op('pool', TT(dst[:, i, c0:c1], rt[0][:, 0:n], rt[1][:, 0:n], ALU.add), reads=['rt0', 'rt1'], writes=wkeys)
                S.op('pool', MS(Vt[:, :, :, 64:128], 1.0), writes=['Vones'])
                wv, kv = win_tile(l, O_V)
                for t4 in range(0, 18, 4):
                    nt = min(4, 18 - t4)
                    pp, pk = psum()
                    for j in range(nt):
                        tt = t4 + j
                        for kt in range(KT):
                            S.op('pe', MM(pp[:, j * 128:(j + 1) * 128], A[:, kt, tt * 128:(tt + 1) * 128], wv[:, kt, :],
                                          start=(kt == 0), stop=(kt == KT - 1)),
                                 reads=[kv] + ak(kt, tt * 128, tt * 128 + 128), writes=[pk], inc=(kt == KT - 1))
                    S.op('act', ACT(Vt[:, t4:t4 + nt, :, 0:64], pp[:, 0:nt * 128].rearrange("p (t g c) -> p t g c", t=nt, g=2), AF.Copy),
                         reads=[pk], writes=['V%d' % (t4 + j) for j in range(nt)])
                dump('q', q[:, :, :], [128, 4, T], sum([tk('q%d' % i, 0, T) for i in range(4)], []))
                dump('kd', kd[:, :, :], [128, 2, T], tk('kd0', 0, T) + tk('kd1', 0, T))
                ckpt('qkv')
                S.barrier()
                AR.top = pm2
                Pt = [v3(AR.bf16(5 * 512), 5) for _ in range(2)]
                rc = AR.f32(512)
                nblk = [0]

                def attend(qc0, key_tiles, l_):
                    for g in range(2):
                        P_ = Pt[nblk[0] % 2]
                        pkey = 'Pt%d' % (nblk[0] % 2)
                        nblk[0] += 1
                        nk = len(key_tiles)
                        for half in range(2):
                            rows = slice(64 * half, 64 * half + 64)
                            for kp in range(0, nk, 2):
                                nkk = min(2, nk - kp)
                                ps_, psk = psum()
                                for j in range(nkk):
                                    kc0 = key_tiles[kp + j][0] * 128
                                    for a_ in range(2):
                                        hh = 2 * a_ + half
                                        hd = 4 * g + hh
                                        qi = hd // 2
                                        S.op('pe', MM(ps_[:, (2 * j + a_) * 128:(2 * j + a_ + 1) * 128], kd[rows, g, kc0:kc0 + 128],
                                                      q[rows, qi, qc0:qc0 + 128], start=True, stop=True, tp=(64 * half, 0)),
                                             reads=tk('kd%d' % g, kc0, kc0 + 128) + tk('q%d' % qi, qc0, qc0 + 128), writes=[psk],
                                             inc=(j == nkk - 1 and a_ == 1))
                                S.op('act', ACT(P_[:, kp:kp + nkk, half * 256:(half + 1) * 256], v3(ps_[:, 0:nkk * 256], nkk), AF.Exp, scale=0.125),
                                     reads=[psk], writes=[pkey + '_%d' % (kp + j) for j in range(nkk)])
                        for ki_, (ktile, msk) in enumerate(key_tiles):
                            if msk is not None:
                                mk_, mkk = (maskp, 'maskp') if msk == 'p' else (maskn, 'maskn')
                                S.op('pool', TT(v3(P_[:, ki_, :], 4), v3(P_[:, ki_, :], 4),
                                                mk_.unsqueeze(1).broadcast_to([128, 4, 128]), ALU.mult),
                                     reads=[pkey + '_%d' % ki_, mkk], writes=[pkey + '_%d' % ki_])
                        if ATT_LEVEL < 1:
                            continue
                        po, pok = psum()
                        for ki_, (ktile, msk) in enumerate(key_tiles):
                            S.op('pe', MM(po[:, :], Vt[:, ktile, g, :], P_[:, ki_, :], start=(ki_ == 0), stop=(ki_ == nk - 1)),
                                 reads=['V%d' % ktile, 'Vones', pkey + '_%d' % ki_], writes=[pok], inc=(ki_ == nk - 1))
                        if ATT_LEVEL < 2:
                            continue
                        S.op('dve', TT(v3(rc[0:64, :], 4), v3(po[64:128, :], 4),
                                       sinkexp[64:128, l_ * 8 + 4 * g:l_ * 8 + 4 * g + 4].unsqueeze(2).broadcast_to([64, 4, 128]), ALU.add),
                             reads=[pok, 'sinkexp'], writes=['rc'])
                        S.op('dve', RCP(rc[0:64, :], rc[0:64, :]), reads=['rc'], writes=['rc'])
                        if ATT_LEVEL < 3:
                            continue
                        for half in range(2):
                            S.op('dve', TT(A[64 * half:64 * half + 64, 2 * g:2 * g + 2, qc0:qc0 + 128],
                                           v3(po[0:64, half * 256:(half + 1) * 256], 2), v3(rc[0:64, half * 256:(half + 1) * 256], 2), ALU.mult),
                                 reads=[pok, 'rc'], writes=ak(2 * g, qc0, qc0 + 128) + ak(2 * g + 1, qc0, qc0 + 128))
                for qb in range(ATT_NQB):
                    kts = [(0, None), (1, None)]
                    if qb > 0:
                        kts.append((2 + qb - 1, 'p'))
                    kts.append((2 + qb, None))
                    if qb < 15:
                        kts.append((2 + qb + 1, 'n'))
                    attend(LC + qb * 128, kts, l)
                if l < DEPTH - 1:
                    for qb in range(2):
                        attend(qb * 128, [(0, None), (1, None)], l)
                dump('attn', A[:, 0:4, :], [128, 4, T], sum([ak(kt, 0, T) for kt in range(4)], []))
                ckpt('attn')
                S.barrier()
                AR.top = pm1
                wo = [v3(AR.bf16(KT * 128), KT) for _ in range(KT)]
                for m in range(KT):
                    S.dma('sp', wo[m], wout_s.ap()[l, m], reads=['wout_s%d' % l], writes=['wo%d' % m])
                mblk = v3(AR.f32(KT * 512), KT)
                sqs = [(AR.bf16(512), 'osq%d' % i) for i in range(2)]
                tmp = AR.f32(512)
                rstd = AR.f32(512)
                tts = [AR.f32(512) for _ in range(2)]

                def mixk(kt, c0, c1):
                    if kt < 4:
                        return A[:, kt, c0:c1], ak(kt, c0, c1)
                    if kt < 6:
                        return cvo[:, kt - 4, c0:c1], tk('cvo%d' % (kt - 4), c0, c1)
                    return sso[:, kt - 6, c0:c1], tk('sso%d' % (kt - 6), c0, c1)
                for bi, (c0, c1, w) in enumerate(BLOCKS):
                    n = c1 - c0
                    for m in range(KT):
                        pp, pk = psum()
                        for kt in range(KT):
                            rap, rkeys = mixk(kt, c0, c1)
                            S.op('pe', MM(pp[:, 0:n], wo[m][:, kt, :], rap, start=(kt == 0), stop=(kt == KT - 1)),
                                 reads=['wo%d' % m] + rkeys, writes=[pk], inc=(kt == KT - 1))
                        S.op('act', ACT(mblk[:, m, 0:n], pp[:, 0:n], AF.Copy), reads=[pk], writes=['mblk%d' % m])
                    pss, kss = rms_stats(lambda kt: mblk[:, kt, 0:n], lambda kt: ['mblk%d' % kt], n, sqs)
                    rstd_from(pss, kss, n, tmp, 'otmp', rstd, 'orstd')
                    for m in range(KT):
                        t_, tkk = tts[m % 2], 'ott%d' % (m % 2)
                        S.op('pool' if m % 4 == 3 else 'dve', TT(t_[:, 0:n], mblk[:, m, 0:n], rstd[:, 0:n], ALU.mult), reads=['mblk%d' % m, 'orstd'], writes=[tkk])
                        S.op('dve', STT(h[:, m, c0:c1], t_[:, 0:n], dv[:, w, 2, m:m + 1], h[:, m, c0:c1], ALU.mult, ALU.add),
                             reads=[tkk, 'dv'] + hk(m, c0, c1), writes=hk(m, c0, c1))
                dump('h1', h[:, :, :], [128, KT, T], sum([hk(kt, 0, T) for kt in range(KT)], []))
                ckpt('oproj')
                S.barrier()
                AR.top = lm0
                sc = mk_scratch()
                norm_to_A(3, 4, sc)
                NB = 576
                HB = 288
                hid = v3(AR.bf16(32 * NB), 32)
                mk_ring('w1', 3, KT * 128, lambda a: v3(a, KT))
                mk_ring('w2', 2, 32 * 128, lambda a: v3(a, 32))
                fblk = v3(AR.f32(KT * NB), KT)
                sqs = sc['sq']
                tmp, tmpk = sc['f'][2]
                rstds = [sc['f'][3], sc['f'][4]]
                tts = [sc['f'][0], sc['f'][1]]
                if l_next is not None:
                    calc_dv(l_next, s, (0,))
                nb_done = 0
                for b4 in range(T // NB):
                    bc0 = b4 * NB
                    for j in range(32):
                        wt, wk = load_w('w1', None, None, w1_s.ap()[l, j], 'w1_s%d' % l)
                        for hh in range(2):
                            c0 = bc0 + hh * HB
                            pp, pk = proj_mm(wt, wk, c0, c0 + HB)
                            hkey = 'hid%d_%d' % (j, hh)
                            S.op('act', ACT(hid[:, j, hh * HB:(hh + 1) * HB], pp[:, 0:HB], AF.Relu), reads=[pk], writes=[hkey])
                            S.op('pool', TT(hid[:, j, hh * HB:(hh + 1) * HB], hid[:, j, hh * HB:(hh + 1) * HB],
                                            hid[:, j, hh * HB:(hh + 1) * HB], ALU.mult), reads=[hkey], writes=[hkey])
                    for m in range(KT):
                        wt2, wk2 = load_w('w2', None, None, w2_s.ap()[l, m], 'w2_s%d' % l)
                        for hh in range(2):
                            pp, pk = psum()
                            for j in range(32):
                                S.op('pe', MM(pp[:, 0:HB], wt2[:, j, :], hid[:, j, hh * HB:(hh + 1) * HB], start=(j == 0), stop=(j == 31)),
                                     reads=[wk2, 'hid%d_%d' % (j, hh)], writes=[pk], inc=(j == 31))
                            S.op('act', ACT(fblk[:, m, hh * HB:(hh + 1) * HB], pp[:, 0:HB], AF.Copy), reads=[pk], writes=['fblk%d_%d' % (m, hh)])
                    for hh in range(2):
                        c0 = bc0 + hh * HB
                        pss, kss = rms_stats(lambda kt: fblk[:, kt, hh * HB:(hh + 1) * HB], lambda kt: ['fblk%d_%d' % (kt, hh)], HB, sqs)
                        rstd, rk = rstds[hh]
                        rstd_from(pss, kss, HB, tmp, tmpk, rstd, rk)
                        segs = []
                        if c0 < LC:
                            e = min(LC, c0 + HB)
                            segs.append((c0, e, 1))
                            if e < c0 + HB:
                                segs.append((e, c0 + HB, 0))
                        else:
                            segs.append((c0, c0 + HB, 0))
                        for m in range(KT):
                            t_, tkk = tts[m % 2]
                            S.op('pool' if m % 4 == 3 else 'dve', TT(t_[:, 0:HB], fblk[:, m, hh * HB:(hh + 1) * HB], rstd[:, 0:HB], ALU.mult),
                                 reads=['fblk%d_%d' % (m, hh), rk], writes=[tkk])
                            for (a0, a1, w) in segs:
                                S.op('dve', STT(h[:, m, a0:a1], t_[:, a0 - c0:a1 - c0], dv[:, w, 5, m:m + 1], h[:, m, a0:a1], ALU.mult, ALU.add),
                                     reads=[tkk, 'dv'] + hk(m, a0, a1), writes=hk(m, a0, a1))
                    if l_next is not None:
                        while nb_done < len(BLOCKS) and BLOCKS[nb_done][1] <= bc0 + NB:
                            norm_block(nb_done, 0, 1, sc, eng_sq='dve')
                            nb_done += 1
                S.barrier()
                AR.top = lm0

            for s in range(nseq):
                load_seq(s)
                ckpt('load')
                for li_, l in enumerate(layers):
                    layer(l, s, li_ > 0, layers[li_ + 1] if li_ + 1 < len(layers) else None)
                store_seq(s)

        except _Stop:
            pass
        S.emit()
    return nc, dbg_t


def _consts():
    f32 = np.float32
    n = np.arange(L)
    row = (n // 64).astype(f32)
    col = (n % 64).astype(f32)
    freqs = (np.float32(10000.0) ** (-np.arange(16, dtype=f32) / np.float32(16))).astype(f32)
    cosT = np.zeros((128, L), f32)
    sinT = np.zeros((128, L), f32)
    for p in range(128):
        dd = p % 64
        i = dd % 16
        pos = row if dd < 32 else col
        ang = (pos * freqs[i]).astype(f32)
        cosT[p] = np.cos(ang).astype(f32)
        sgn = -1.0 if (dd % 32) < 16 else 1.0
        sinT[p] = (sgn * np.sin(ang)).astype(f32)
    ii = np.arange(128)[:, None]
    jj = np.arange(128)[None, :]
    maskp = (ii >= jj).astype(f32)
    maskn = (ii <= jj).astype(f32)
    return dict(ropec=cosT, ropes=sinT, maskp=maskp, maskn=maskn, ident=np.eye(128, dtype=f32))


_WKEYS = ['w_ada', 'b_ada', 'norm_g', 'w_in', 'conv_w', 'attn_sink', 'ssm_lam_re', 'ssm_lam_im', 'ssm_log_dt',
          'ssm_b_re', 'ssm_b_im', 'ssm_c_re', 'ssm_c_im', 'ssm_d', 'w_glu', 'b_glu', 'w_out', 'w_mlp_in', 'w_mlp_out']

LAUNCH_PLAN = [[0, 1, 2, 3]]


def kernel(**inputs):
    f32 = np.float32
    inp = {k: np.ascontiguousarray(np.asarray(v, dtype=f32)) for k, v in inputs.items()}
    consts = _consts()
    hx = inp['x']
    hctx = inp['ctx']
    B = hx.shape[0]
    per = B // NCORES
    for li, layers in enumerate(LAUNCH_PLAN):
        last = (li == len(LAUNCH_PLAN) - 1)
        nc, _ = build_program(layers, nseq=per, out_hc=not last)
        in_maps = []
        for c in range(NCORES):
            sl = slice(c * per, (c + 1) * per)
            m = {'x': np.ascontiguousarray(hx[sl]), 'ctx': np.ascontiguousarray(hctx[sl]),
                 'cc': np.ascontiguousarray(np.concatenate([inp['c'][sl], inp['c_ctx'][None, :]], axis=0))}
            for k in _WKEYS:
                m[k] = inp[k]
            m.update(consts)
            in_maps.append(m)
        res = run_bass_kernel_spmd(nc, in_maps, core_ids=list(range(NCORES)))
        hx = np.concatenate([r['out'] for r in res.results], axis=0)
        if not last:
            hctx = np.concatenate([r['hc_out'] for r in res.results], axis=0)
    return hx.astype(f32)
```
